# Optimizing a Trainium2 kernel written in Bass

```python
import jax, jax.numpy as jnp
from jax import lax
import numpy as np


D_MODEL = 1024
BATCH = 8
SEQ = 4096
DEPTH = 4

GRID_W = 64
CTX_LEN = 256
MIXER_KINDS = ('rwkv7', 'rglru', 'natten')
RMS_EPS = 1e-6

RW_HEAD_DIM = 64
RW_HEADS = D_MODEL // RW_HEAD_DIM
RW_LORA = 64
RW_GN_EPS = 64e-5

LRU_WIDTH = 1408
LRU_BLOCKS = 16
LRU_BLOCK_DIM = LRU_WIDTH // LRU_BLOCKS
LRU_CONV = 4
LRU_C = 8.0

NA_HEAD_DIM = 64
NA_HEADS = D_MODEL // NA_HEAD_DIM
NA_WIDTH = NA_HEADS * NA_HEAD_DIM
WIN_H = 8
WIN_W = 16
ROPE_THETA = 10000.0

kernel_name = 'hybrid_rwkv7_rglru_natten_dit'


def rms_norm(x, g, eps=RMS_EPS):
    xf = x.astype(jnp.float32)
    xf = xf * lax.rsqrt(jnp.mean(xf * xf, axis=-1, keepdims=True) + eps)
    return xf.astype(x.dtype) * g


def centred_shift(h):
    prev = jnp.pad(h[:, :-1], ((0, 0), (1, 0), (0, 0)))
    nxt = jnp.pad(h[:, 1:], ((0, 0), (0, 1), (0, 0)))
    return 0.5 * (prev + nxt)


def rwkv7_project(h, p):
    B, T, D = h.shape
    heads = lambda t: t.reshape(B, T, RW_HEADS, RW_HEAD_DIM)
    xx = centred_shift(h) - h
    mu = p['mu']
    xr, xw, xk, xv, xa, xg = [h + xx * mu[j] for j in range(6)]
    w_in = p['w_in']
    r = heads(xr @ w_in[0])
    k = heads(xk @ w_in[1])
    v = heads(xv @ w_in[2])
    gate = jax.nn.silu(xg @ w_in[3])
    k_k = p['k_ka'][0].reshape(RW_HEADS, RW_HEAD_DIM)
    k_a = p['k_ka'][1].reshape(RW_HEADS, RW_HEAD_DIM)
    kkf = (k * k_k).astype(jnp.float32)
    kk = (kkf / jnp.maximum(jnp.sqrt(jnp.sum(kkf * kkf, -1, keepdims=True)), 1e-12)).astype(k.dtype)
    b0, down, up = p['lora_b0'], p['lora_down'], p['lora_up']
    decay, a, k_dir = [], [], []
    for d in range(2):
        w_log = -jax.nn.softplus(-(b0[d, 0] + jnp.tanh(xw @ down[d, 0]) @ up[d, 0])) - 0.5
        a_d = heads(jax.nn.sigmoid(b0[d, 1] + (xa @ down[d, 1]) @ up[d, 1]))
        decay.append(heads(jnp.exp(-jnp.exp(w_log))))
        a.append(a_d)
        k_dir.append(k * (1.0 + (a_d - 1.0) * k_a))
    return dict(r=r, v=v, kk=kk, decay=decay, a=a, k=k_dir, gate=gate)


def rwkv7_scan(proj, d, s0, reverse):
    def step(S, inp):
        r_t, w_t, k_t, kk_t, a_t, v_t = inp
        sa = jnp.einsum('bhvk,bhk->bhv', S, -kk_t)
        S = (S * w_t[:, :, None, :] + sa[..., None] * (kk_t * a_t)[:, :, None, :]
             + v_t[..., None] * k_t[:, :, None, :])
        return S, jnp.einsum('bhvk,bhk->bhv', S, r_t)
    xs = tuple(jnp.moveaxis(t, 1, 0) for t in
               (proj['r'], proj['decay'][d], proj['k'][d], proj['kk'], proj['a'][d], proj['v']))
    s_final, ys = lax.scan(step, s0, xs, reverse=reverse)
    return jnp.moveaxis(ys, 0, 1), s_final


def rwkv7_output(proj, y, p):
    B, T, H, Dh = y.shape
    yf = y.astype(jnp.float32)
    mean = jnp.mean(yf, -1, keepdims=True)
    var = jnp.mean(jnp.square(yf - mean), -1, keepdims=True)
    yn = ((yf - mean) * lax.rsqrt(var + RW_GN_EPS)).astype(y.dtype).reshape(B, T, H * Dh)
    yn = yn * p['gn'][0] + p['gn'][1]
    r, v, r_k = proj['r'], proj['v'], p['r_k']
    bonus = (jnp.sum(r * proj['k'][0] * r_k, -1, keepdims=True)
             + jnp.sum(r * proj['k'][1] * r_k, -1, keepdims=True)) * v
    return ((yn + bonus.reshape(B, T, H * Dh)) * proj['gate']) @ p['w_out']


def rwkv7_mixer(h_l, h_c, p, ctx_out):
    proj_c = rwkv7_project(h_c, p)
    proj_l = rwkv7_project(h_l, p)
    s0 = jnp.zeros((h_l.shape[0], RW_HEADS, RW_HEAD_DIM, RW_HEAD_DIM), h_l.dtype)
    ys_c, ys_l = [], []
    for d in range(2):
        rev = d == 1
        y_c, s_c = rwkv7_scan(proj_c, d, s0, rev)
        y_l, _ = rwkv7_scan(proj_l, d, s_c, rev)
        ys_c.append(y_c)
        ys_l.append(y_l)
    out_l = rwkv7_output(proj_l, ys_l[0] + ys_l[1], p)
    out_c = rwkv7_output(proj_c, ys_c[0] + ys_c[1], p) if ctx_out else None
    return out_l, out_c


def depthwise_conv_centred(x, w, b):
    T = x.shape[1]
    xp = jnp.pad(x, ((0, 0), (LRU_CONV // 2, LRU_CONV - 1 - LRU_CONV // 2), (0, 0)))
    out = b + xp[:, 0:T] * w[0]
    for j in range(1, LRU_CONV):
        out = out + xp[:, j:j + T] * w[j]
    return out


def block_diag(x, w, b):
    B, T, _ = x.shape
    xb = x.reshape(B, T, LRU_BLOCKS, LRU_BLOCK_DIM)
    return jnp.einsum('btnc,ncd->btnd', xb, w).reshape(B, T, LRU_WIDTH) + b


def rglru_coeffs(x, p, d):
    r = jax.nn.sigmoid(block_diag(x, p['gate_w'][d, 0], p['gate_b'][d, 0]))
    i = jax.nn.sigmoid(block_diag(x, p['gate_w'][d, 1], p['gate_b'][d, 1]))
    log_a = -LRU_C * r * jax.nn.softplus(-p['lam'][d])
    a = jnp.exp(log_a)
    b = jnp.sqrt(-jnp.expm1(2.0 * log_a)) * (i * x)
    return a, b


def linear_scan(a, b, h0, reverse):
    def combine(e1, e2):
        a1, b1 = e1
        a2, b2 = e2
        return a1 * a2, a2 * b1 + b2
    a_cum, b_cum = lax.associative_scan(combine, (a, b), axis=1, reverse=reverse)
    h = a_cum * h0[:, None, :] + b_cum
    final = h[:, 0] if reverse else h[:, -1]
    return h, final


def rglru_mixer(h_l, h_c, p, ctx_out):
    def pre(h):
        z = h @ p['w_in']
        xr, g = z[..., :LRU_WIDTH], z[..., LRU_WIDTH:]
        return depthwise_conv_centred(xr, p['conv_w'], p['conv_b']), jax.nn.silu(g)
    x_c, g_c = pre(h_c)
    x_l, g_l = pre(h_l)
    h0 = jnp.zeros((h_l.shape[0], LRU_WIDTH), h_l.dtype)
    hs_c, hs_l = [], []
    for d in range(2):
        rev = d == 1
        a, b = rglru_coeffs(x_c, p, d)
        hc, hc_final = linear_scan(a, b, h0, rev)
        a, b = rglru_coeffs(x_l, p, d)
        hl, _ = linear_scan(a, b, hc_final, rev)
        hs_c.append(hc)
        hs_l.append(hl)
    out_l = ((hs_l[0] + hs_l[1]) * g_l) @ p['w_out']
    out_c = ((hs_c[0] + hs_c[1]) * g_c) @ p['w_out'] if ctx_out else None
    return out_l, out_c


def axial_rope(x, row, col):
    half = x.shape[-1] // 2
    nfreq = half // 2
    inv = ROPE_THETA ** (-jnp.arange(nfreq, dtype=jnp.float32) / nfreq)
    def rot(xp, pos):
        ang = pos.astype(jnp.float32)[:, None] * inv
        cos = jnp.cos(ang)[None, :, None, :]
        sin = jnp.sin(ang)[None, :, None, :]
        x1, x2 = xp[..., :nfreq], xp[..., nfreq:]
        return jnp.concatenate([x1 * cos - x2 * sin, x1 * sin + x2 * cos], -1).astype(x.dtype)
    return jnp.concatenate([rot(x[..., :half], row), rot(x[..., half:], col)], -1)


def natten_mixer(h_l, h_c, p, ctx_out):
    B, T, _ = h_l.shape
    rows = T // GRID_W
    kh = min(WIN_H, rows)
    H, Dh = NA_HEADS, NA_HEAD_DIM
    scale = Dh ** -0.5

    def project(h):
        n = h.shape[1]
        q, k, v, g = jnp.split(h @ p['w_in'], 4, axis=-1)
        q = rms_norm(q.reshape(B, n, H, Dh), p['qk_g'][0])
        k = rms_norm(k.reshape(B, n, H, Dh), p['qk_g'][1])
        return q, k, v.reshape(B, n, H, Dh), jax.nn.silu(g)

    q_c, k_c, v_c, g_c = project(h_c)
    q_l, k_l, v_l, g_l = project(h_l)
    pos = jnp.arange(T)
    row, col = pos // GRID_W, pos % GRID_W
    q_rot = axial_rope(q_l, row, col)
    k_rot = axial_rope(k_l, row, col)
    to_rows = lambda t: jnp.moveaxis(t.reshape(B, rows, GRID_W, H, Dh), 1, 0)
    k_grid = k_rot.reshape(B, rows, GRID_W, H, Dh)
    v_grid = v_l.reshape(B, rows, GRID_W, H, Dh)

    cols = np.arange(GRID_W)
    c_start = np.clip(cols - WIN_W // 2, 0, GRID_W - WIN_W)
    col_ok = (cols[None, :] >= c_start[:, None]) & (cols[None, :] < c_start[:, None] + WIN_W)
    col_ok = jnp.asarray(col_ok)[:, None, :]
    dc_idx = np.clip(cols[None, :] - cols[:, None] + WIN_W - 1, 0, 2 * WIN_W - 2)
    n_band = kh * GRID_W

    def row_block(args):
        r, q_r, q_p = args
        start = jnp.clip(r - kh // 2, 0, rows - kh)
        k_band = lax.dynamic_slice_in_dim(k_grid, start, kh, axis=1)
        v_band = lax.dynamic_slice_in_dim(v_grid, start, kh, axis=1)
        s_band = jnp.einsum('bqhd,bikhd->bhqik', q_r, k_band).astype(jnp.float32) * scale
        dr_idx = start + jnp.arange(kh) - r + WIN_H - 1
        bias = p['rpb'][:, dr_idx[None, :, None], dc_idx[:, None, :]]
        s_band = jnp.where(col_ok, s_band + bias.astype(jnp.float32), -jnp.inf)
        s_band = s_band.reshape(B, H, GRID_W, n_band)
        s_ctx = jnp.einsum('bqhd,bchd->bhqc', q_p, k_c).astype(jnp.float32) * scale
        prob = jax.nn.softmax(jnp.concatenate([s_band, s_ctx], -1), axis=-1).astype(v_l.dtype)
        o = jnp.einsum('bhqj,bjhd->bqhd', prob[..., :n_band], v_band.reshape(B, n_band, H, Dh))
        return o + jnp.einsum('bhqc,bchd->bqhd', prob[..., n_band:], v_c)

    o = lax.map(row_block, (jnp.arange(rows), to_rows(q_rot), to_rows(q_l)))
    o = jnp.moveaxis(o, 0, 1).reshape(B, T, NA_WIDTH)
    out_l = (o * g_l) @ p['w_out']
    out_c = None
    if ctx_out:
        s = jnp.einsum('bqhd,bkhd->bhqk', q_c, k_c).astype(jnp.float32) * scale
        prob = jax.nn.softmax(s, axis=-1).astype(v_c.dtype)
        o_c = jnp.einsum('bhqk,bkhd->bqhd', prob, v_c).reshape(B, h_c.shape[1], NA_WIDTH)
        out_c = (o_c * g_c) @ p['w_out']
    return out_l, out_c


MIXERS = {'rwkv7': rwkv7_mixer, 'rglru': rglru_mixer, 'natten': natten_mixer}


def layer_forward(x, ctx, c_act, cctx_act, p, kind, ctx_out):
    mod_l = c_act @ p['ada_w'] + p['ada_b']
    mod_c = cctx_act @ p['ada_w'] + p['ada_b']
    shift_l, scale_l, gate_l = jnp.split(mod_l[:, None, :], 3, axis=-1)
    shift_c, scale_c, gate_c = jnp.split(mod_c, 3, axis=-1)
    h_l = rms_norm(x, p['norm_g']) * (1.0 + scale_l) + shift_l
    h_c = rms_norm(ctx, p['norm_g']) * (1.0 + scale_c) + shift_c
    y_l, y_c = MIXERS[kind](h_l, h_c, p, ctx_out)
    x = x + gate_l * y_l
    if ctx_out:
        ctx = ctx + gate_c * y_c
    return x, ctx


def _normal(key, shape, scale):
    return scale * jax.random.normal(key, shape, jnp.float32)


def _ada_params(ks, pre):
    return {
        pre + 'norm_g': 1.0 + _normal(ks[0], (D_MODEL,), 0.02),
        pre + 'ada_w': _normal(ks[1], (D_MODEL, 3 * D_MODEL), 0.5 * D_MODEL ** -0.5),
        pre + 'ada_b': _normal(ks[2], (3 * D_MODEL,), 0.01),
    }


def _rwkv7_params(key, pre):
    ks = jax.random.split(key, 16)
    d = _ada_params(ks, pre)
    d[pre + 'w_in'] = _normal(ks[3], (4, D_MODEL, D_MODEL), D_MODEL ** -0.5)
    d[pre + 'mu'] = jax.random.uniform(ks[4], (6, D_MODEL), jnp.float32)
    w0 = jax.random.uniform(ks[5], (2, D_MODEL), jnp.float32, minval=-6.0, maxval=0.0)
    a0 = _normal(ks[6], (2, D_MODEL), 0.1)
    d[pre + 'lora_b0'] = jnp.stack([w0, a0], axis=1)
    d[pre + 'lora_down'] = _normal(ks[7], (2, 2, D_MODEL, RW_LORA), D_MODEL ** -0.5)
    d[pre + 'lora_up'] = _normal(ks[8], (2, 2, RW_LORA, D_MODEL), 0.1 * RW_LORA ** -0.5)
    d[pre + 'k_ka'] = jnp.stack([0.85 + _normal(ks[9], (D_MODEL,), 0.02),
                                 1.0 + _normal(ks[10], (D_MODEL,), 0.02)])
    d[pre + 'r_k'] = _normal(ks[11], (RW_HEADS, RW_HEAD_DIM), 0.1)
    d[pre + 'gn'] = jnp.stack([1.0 + _normal(ks[12], (D_MODEL,), 0.02),
                               _normal(ks[13], (D_MODEL,), 0.01)])
    d[pre + 'w_out'] = _normal(ks[14], (D_MODEL, D_MODEL), D_MODEL ** -0.5)
    return d


def _rglru_params(key, pre):
    ks = jax.random.split(key, 12)
    d = _ada_params(ks, pre)
    d[pre + 'w_in'] = _normal(ks[3], (D_MODEL, 2 * LRU_WIDTH), D_MODEL ** -0.5)
    d[pre + 'conv_w'] = _normal(ks[4], (LRU_CONV, LRU_WIDTH), LRU_CONV ** -0.5)
    d[pre + 'conv_b'] = _normal(ks[5], (LRU_WIDTH,), 0.01)
    d[pre + 'gate_w'] = _normal(ks[6], (2, 2, LRU_BLOCKS, LRU_BLOCK_DIM, LRU_BLOCK_DIM), LRU_BLOCK_DIM ** -0.5)
    d[pre + 'gate_b'] = _normal(ks[7], (2, 2, LRU_WIDTH), 0.01)
    u = jax.random.uniform(ks[8], (2, LRU_WIDTH), jnp.float32, minval=0.9, maxval=0.999)
    a_base = u ** (1.0 / LRU_C)
    d[pre + 'lam'] = jnp.log(a_base) - jnp.log1p(-a_base)
    d[pre + 'w_out'] = _normal(ks[9], (LRU_WIDTH, D_MODEL), LRU_WIDTH ** -0.5)
    return d


def _natten_params(key, pre):
    ks = jax.random.split(key, 8)
    d = _ada_params(ks, pre)
    d[pre + 'w_in'] = _normal(ks[3], (D_MODEL, 4 * NA_WIDTH), D_MODEL ** -0.5)
    d[pre + 'qk_g'] = 1.0 + _normal(ks[4], (2, NA_HEAD_DIM), 0.02)
    d[pre + 'rpb'] = _normal(ks[5], (NA_HEADS, 2 * WIN_H - 1, 2 * WIN_W - 1), 0.1)
    d[pre + 'w_out'] = _normal(ks[6], (NA_WIDTH, D_MODEL), NA_WIDTH ** -0.5)
    return d


def setup_inputs(seed: int = 0) -> dict:
    key = jax.random.key(seed)
    ks = jax.random.split(key, 4 + DEPTH)
    inputs = {
        'x': jax.random.normal(ks[0], (BATCH, SEQ, D_MODEL), jnp.float32),
        'c': jax.random.normal(ks[1], (BATCH, D_MODEL), jnp.float32),
        'ctx': jax.random.normal(ks[2], (BATCH, CTX_LEN, D_MODEL), jnp.float32),
        'c_ctx': jax.random.normal(ks[3], (D_MODEL,), jnp.float32),
    }
    makers = (_rwkv7_params, _rglru_params, _natten_params)
    for i in range(DEPTH):
        inputs.update(makers[i % len(makers)](ks[4 + i], 'l%d_' % i))
    return inputs


def reference(x, c, ctx, c_ctx,
              l0_norm_g, l0_ada_w, l0_ada_b, l0_w_in, l0_mu, l0_lora_b0, l0_lora_down, l0_lora_up,
              l0_k_ka, l0_r_k, l0_gn, l0_w_out,
              l1_norm_g, l1_ada_w, l1_ada_b, l1_w_in, l1_conv_w, l1_conv_b, l1_gate_w, l1_gate_b,
              l1_lam, l1_w_out,
              l2_norm_g, l2_ada_w, l2_ada_b, l2_w_in, l2_qk_g, l2_rpb, l2_w_out,
              l3_norm_g, l3_ada_w, l3_ada_b, l3_w_in, l3_mu, l3_lora_b0, l3_lora_down, l3_lora_up,
              l3_k_ka, l3_r_k, l3_gn, l3_w_out):
    layer_params = (
        dict(norm_g=l0_norm_g, ada_w=l0_ada_w, ada_b=l0_ada_b, w_in=l0_w_in, mu=l0_mu,
             lora_b0=l0_lora_b0, lora_down=l0_lora_down, lora_up=l0_lora_up, k_ka=l0_k_ka,
             r_k=l0_r_k, gn=l0_gn, w_out=l0_w_out),
        dict(norm_g=l1_norm_g, ada_w=l1_ada_w, ada_b=l1_ada_b, w_in=l1_w_in, conv_w=l1_conv_w,
             conv_b=l1_conv_b, gate_w=l1_gate_w, gate_b=l1_gate_b, lam=l1_lam, w_out=l1_w_out),
        dict(norm_g=l2_norm_g, ada_w=l2_ada_w, ada_b=l2_ada_b, w_in=l2_w_in, qk_g=l2_qk_g,
             rpb=l2_rpb, w_out=l2_w_out),
        dict(norm_g=l3_norm_g, ada_w=l3_ada_w, ada_b=l3_ada_b, w_in=l3_w_in, mu=l3_mu,
             lora_b0=l3_lora_b0, lora_down=l3_lora_down, lora_up=l3_lora_up, k_ka=l3_k_ka,
             r_k=l3_r_k, gn=l3_gn, w_out=l3_w_out),
    )
    c_act = jax.nn.silu(c)
    cctx_act = jax.nn.silu(c_ctx)
    for i in range(DEPTH):
        x, ctx = layer_forward(x, ctx, c_act, cctx_act, layer_params[i],
                               MIXER_KINDS[i % len(MIXER_KINDS)], ctx_out=i < DEPTH - 1)
    return x
```

```python
import numpy as np
import ml_dtypes
import concourse.bass as bass
import concourse.mybir as mybir
from concourse.bass_utils import run_bass_kernel_spmd

F32 = mybir.dt.float32
BF16 = mybir.dt.bfloat16
AF = mybir.ActivationFunctionType
ALU = mybir.AluOpType
AX = mybir.AxisListType

D = 1024
SEQ = 4096
CTX = 256
T = SEQ + CTX
NCORES = 8
ARENA0 = 16512
ARENA_END = 229000
NDMASEM = 32


class Res:
    __slots__ = ("name", "w", "rs")

    def __init__(self, name=""):
        self.name = name
        self.w = None
        self.rs = {}


class Op:
    __slots__ = ("eng", "fn", "deps", "needed", "sem", "val", "is_dma", "seq")

    def __init__(self, eng, fn, is_dma=False):
        self.eng = eng
        self.fn = fn
        self.deps = []
        self.needed = False
        self.sem = None
        self.val = 0
        self.is_dma = is_dma
        self.seq = 0


class Tile:
    def __init__(self, t, name):
        self.t = t
        self.name = name
        self.res = Res(name)
        self._sub = {}

    def __getitem__(self, k):
        return self.t[k]

    def r(self, key=None):
        if key is None:
            return self.res
        if key not in self._sub:
            self._sub[key] = Res("%s/%s" % (self.name, key))
        return self._sub[key]


class Prog:
    ENGS = ("pe", "act", "dve", "pool", "sp")

    def __init__(self):
        self.nc = bass.Bass("TRN2", target_bir_lowering=False)
        self.ops = {e: [] for e in self.ENGS}
        self.nops = 0
        self.sb_off = ARENA0
        self.ndma = 0
        self.dma_ops = []
        self.uid = 0
        self.pending_dma = []
        self.carry = {}

    def sb(self, name, shape, dtype):
        esz = 2 if dtype == BF16 else 4
        n = 1
        for s in shape[1:]:
            n *= s
        nbytes = (n * esz + 63) // 64 * 64
        off = self.sb_off
        assert off + nbytes <= ARENA_END, ("SBUF overflow", name, off, nbytes)
        self.sb_off += nbytes
        self.uid += 1
        t = self.nc.alloc_sbuf_tensor_at("%s_%d" % (name, self.uid), list(shape), dtype, offset=off)
        return Tile(t, name)

    def sb_mark(self):
        return self.sb_off

    def sb_reset(self, mark):
        if mark < self.sb_off:
            self.barrier()
        self.sb_off = mark

    def dram(self, name, shape, dtype, kind="Internal"):
        t = self.nc.dram_tensor(name, list(shape), dtype, kind=kind)
        return Tile(t.ap(), name)

    def op(self, eng, fn, r=(), w=(), is_dma=False):
        o = Op(eng, fn, is_dma)
        self.nops += 1
        o.seq = self.nops
        deps = {}

        def add(d):
            if d is None or d is o:
                return
            if d.eng == "pe" and eng == "pe":
                return
            deps[id(d)] = d

        for x in r:
            add(x.w)
        for x in w:
            add(x.w)
            for lst in x.rs.values():
                for d in lst:
                    add(d)
        if is_dma:
            i = self.ndma
            self.ndma += 1
            if i >= NDMASEM:
                add(self.dma_ops[i - NDMASEM])
            self.dma_ops.append(o)
            self.pending_dma.append(o)
        best = {}
        out = []
        for d in deps.values():
            if d.is_dma:
                out.append(d)
            else:
                b = best.get(d.eng)
                if b is None or d.seq > b.seq:
                    best[d.eng] = d
        out.extend(best.values())
        if self.carry.get(eng):
            have = set(id(d) for d in out)
            for d in self.carry[eng]:
                if id(d) not in have and d is not o:
                    out.append(d)
            self.carry[eng] = []
        o.deps = out
        for d in out:
            d.needed = True
        for x in r:
            if is_dma:
                x.rs.setdefault("dma", []).append(o)
            else:
                x.rs[eng] = [o]
        for x in w:
            x.w = o
            x.rs = {}
        self.ops[eng].append(o)
        return o

    def barrier(self, final=False):
        D = []
        for e in self.ENGS:
            if e == "sp":
                continue
            if self.ops[e]:
                D.append(self.ops[e][-1])
        D.extend(self.pending_dma)
        self.pending_dma = []
        for d in D:
            d.needed = True
        if final:
            o = Op("sp", lambda eng: eng.nop(), False)
            self.nops += 1
            o.seq = self.nops
            o.deps = list(D)
            self.ops["sp"].append(o)
            return
        for e in self.ENGS:
            self.carry[e] = self.carry.get(e, []) + [d for d in D if not (d.eng == e and not d.is_dma and e == "pe")]

    def dma(self, out, in_, r=(), w=(), eng="sp"):
        return self.op(eng, lambda e: e.dma_start(out=out, in_=in_), r, w, is_dma=True)

    def mm(self, out, lhsT, rhs, start=True, stop=True, r=(), w=(), **kw):
        return self.op("pe", lambda e: e.matmul(out, lhsT, rhs, start=start, stop=stop, **kw), r, w)

    def tr(self, out, in_, ident, r=(), w=()):
        return self.op("pe", lambda e: e.transpose(out, in_, ident), r, w)

    def act(self, out, in_, func, bias=0.0, scale=1.0, r=(), w=(), eng="act"):
        return self.op(eng, lambda e: e.activation(out=out, in_=in_, func=func, bias=bias, scale=scale), r, w)

    def ts(self, eng, out, in0, s1, op0, s2=None, op1=None, r=(), w=()):
        if op1 is None:
            return self.op(eng, lambda e: e.tensor_scalar(out=out, in0=in0, scalar1=s1, scalar2=None, op0=op0), r, w)
        return self.op(eng, lambda e: e.tensor_scalar(out=out, in0=in0, scalar1=s1, scalar2=s2, op0=op0, op1=op1), r, w)

    def tt(self, eng, out, in0, in1, op, r=(), w=()):
        return self.op(eng, lambda e: e.tensor_tensor(out=out, in0=in0, in1=in1, op=op), r, w)

    def stt(self, out, in0, scalar, in1, op0, op1, r=(), w=()):
        return self.op("dve", lambda e: e.scalar_tensor_tensor(out=out, in0=in0, scalar=scalar, in1=in1, op0=op0, op1=op1), r, w)

    def copy(self, eng, out, in_, r=(), w=()):
        if eng == "act":
            return self.op(eng, lambda e: e.copy(out=out, in_=in_), r, w)
        return self.op(eng, lambda e: e.tensor_copy(out=out, in_=in_), r, w)

    def memset(self, eng, ap, val, w=()):
        return self.op(eng, lambda e: e.memset(ap, val), (), w)

    def emit(self):
        nc = self.nc
        from contextlib import ExitStack
        with ExitStack() as st:
            esem = {e: st.enter_context(nc.semaphore("s_" + e)) for e in self.ENGS if e != "sp"}
            dsem = [st.enter_context(nc.semaphore("d_%d" % i)) for i in range(NDMASEM)]
            for e in self.ENGS:
                cnt = 0
                for o in self.ops[e]:
                    if o.is_dma:
                        continue
                    if e == "sp":
                        continue
                    if o.needed:
                        cnt += 1
                        o.sem = esem[e]
                        o.val = cnt
            for i, o in enumerate(self.dma_ops):
                o.sem = dsem[i % NDMASEM]
                o.val = 16 * (i // NDMASEM + 1)
            spsem = st.enter_context(nc.semaphore("s_sp"))
            cnt = 0
            for o in self.ops["sp"]:
                if not o.is_dma and o.needed:
                    cnt += 1
                    o.sem = spsem
                    o.val = cnt
            block = st.enter_context(nc.Block())

            def run(ename):
                def body(eng):
                    waited = {}
                    for o in self.ops[ename]:
                        for d in o.deps:
                            key = id(d.sem)
                            if waited.get(key, 0) >= d.val:
                                continue
                            eng.wait_ge(d.sem, d.val)
                            waited[key] = d.val
                        ins = o.fn(eng)
                        if o.is_dma:
                            ins.then_inc(o.sem, 16)
                        elif o.needed:
                            ins.then_inc(o.sem, 1)
                return body

            block.tensor(run("pe"))
            block.scalar(run("act"))
            block.vector(run("dve"))
            block.gpsimd(run("pool"))
            block.sync(run("sp"))
        return nc


LRU_W = 1408
LRU_NB = 16
LRU_BD = 88
PC0 = 2
PL0 = 261
PTOT = 4358


def tok_blocks():
    out = [(0, 256, True)]
    for i in range(8):
        out.append((256 + 512 * i, 512, False))
    return out


def pad_col(t0):
    return PC0 + t0 if t0 < CTX else PL0 + (t0 - CTX)


class StopBuild(Exception):
    pass


class Model:
    def __init__(self, layers=(0, 1, 2, 3), debug=False, stop=None):
        self.p = Prog()
        self.layers = layers
        self.debug = debug
        self.stop = stop
        try:
            self.build()
        except StopBuild:
            self.p.barrier(final=True)
            self.p.emit()

    def chk(self, name):
        if self.stop == name:
            raise StopBuild()

    def build(self):
        p = self.p
        nc = p.nc
        self.xT_in = p.dram("xT", [D, T], F32, kind="ExternalInput")
        self.cc_in = p.dram("cc", [128, 8, 2], F32, kind="ExternalInput")
        self.outT = p.dram("outT", [D, SEQ], F32, kind="ExternalOutput")
        self.XS = p.dram("XS", [D, T], F32)
        self.inp = {}
        self.cmask_in = p.dram("cmask", [128, 1536], F32, kind="ExternalInput")
        pst = nc.alloc_psum_tensor("ps", [128, 8, 512], F32)
        self.ps = Tile(pst, "ps")
        self.ones_f = p.sb("ones_f", [128, 128], F32)
        p.memset("pool", self.ones_f[:, :], 1.0, w=[self.ones_f.r()])
        self.consts()
        self.cc = p.sb("cc", [128, 8, 2], F32)
        p.dma(self.cc[:, :, :], self.cc_in[:, :, :], w=[self.cc.r()])
        self.cact = p.sb("cact", [128, 8, 2], F32)
        p.act(self.cact[:, :, :], self.cc[:, :, :], AF.Silu, r=[self.cc.r()], w=[self.cact.r()])
        self.mod = p.sb("mod", [128, 24, 2], F32)
        self.g1 = p.sb("g1", [128, 8, 2], F32)
        self.base_mark = p.sb_mark()
        self.chk("setup")
        src = self.xT_in
        for li in range(4):
            if li not in self.layers:
                continue
            p.sb_reset(self.base_mark)
            last = li == max(self.layers)
            dst = self.XS
            if self.debug != 3:
                self.layer_prologue(li)
            if li % 3 == 1:
                self.layer_rglru(li, src, dst, last)
            elif li % 3 == 2:
                self.layer_natten(li, src, dst, last)
            else:
                self.layer_rwkv(li, src, dst, last)
            src = dst
            p.barrier()
        p.barrier(final=True)
        p.emit()

    def din(self, name, shape, dtype=F32):
        t = self.p.dram(name, shape, dtype, kind="ExternalInput")
        self.inp[name] = t
        return t

    def layer_prologue(self, li):
        p = self.p
        pre = "l%d_" % li
        ada_w = self.din(pre + "ada_w", [D, 3 * D])
        vec = self.din(pre + "adavec", [128, 32])
        mark = p.sb_mark()
        av = p.sb("adavec", [128, 32], F32)
        p.dma(av[:, :], vec[:, :], w=[av.r()])
        ps = self.ps
        aw_v = ada_w.t.rearrange("(k p) c -> p k c", p=128)
        wst = [p.sb("adaw%d" % i, [128, 8, 512], F32) for i in range(2)]
        for oc4 in range(6):
            wt = wst[oc4 % 2]
            p.dma(wt[:, :, :], aw_v[:, :, oc4 * 512:(oc4 + 1) * 512], w=[wt.r()])
            for j in range(4):
                oc = oc4 * 4 + j
                for k in range(8):
                    p.mm(ps[:, 0, oc * 2:oc * 2 + 2], wt[:, k, j * 128:(j + 1) * 128], self.cact[:, k, :],
                         start=(k == 0), stop=(k == 7), r=[wt.r(), self.cact.r()], w=[ps.r(0)])
        p.tt("dve", self.mod[:, :, :], ps[:, 0, 0:48].rearrange("p (c two) -> p c two", two=2),
             av[:, 0:24].unsqueeze(2).to_broadcast([128, 24, 2]), ALU.add,
             r=[ps.r(0), av.r()], w=[self.mod.r()])
        p.ts("dve", self.g1[:, :, :], self.mod[:, 8:16, :], 1.0, ALU.add, r=[self.mod.r()], w=[self.g1.r()])
        p.tt("dve", self.g1[:, :, :], self.g1[:, :, :], av[:, 24:32].unsqueeze(2).to_broadcast([128, 8, 2]), ALU.mult,
             r=[self.g1.r(), av.r()], w=[self.g1.r()])
        p.barrier()
        p.sb_reset(mark)
        self.chk("prologue")

    def stage_norm(self, src, HT, pad=None, HTd=None):
        p = self.p
        ps = self.ps
        mark = p.sb_mark()
        xb = [p.sb("nx%d" % i, [128, 8, 512], F32) for i in range(2)]
        sq = p.sb("nsq", [128, 8, 512], F32)
        rs = p.sb("nrs", [128, 512], F32)
        tmp = p.sb("ntmp", [128, 512], F32)
        src_v = src.t.rearrange("(k p) t -> p k t", p=128)
        if HTd is not None:
            hbk = [p.sb("nhb%d" % i, [128, 8, 512], BF16) for i in range(2)]
            htd_v = HTd.t.rearrange("(k p) t -> p k t", p=128)
        for bi, (t0, nb, isc) in enumerate(tok_blocks()):
            if HTd is not None:
                HT = hbk[bi % 2]
            x = xb[bi % 2]
            mc = 1 if isc else 0
            p.dma(x[:, :, 0:nb], src_v[:, :, t0:t0 + nb], r=[src.r(bi)], w=[x.r()])
            self.chk("n1")
            p.act(sq[:, :, 0:nb], x[:, :, 0:nb], AF.Square, r=[x.r()], w=[sq.r()])
            self.chk("n2")
            for k in range(8):
                p.mm(ps[:, 1, 0:nb], self.ones_f[:, :], sq[:, k, 0:nb], start=(k == 0), stop=(k == 7),
                     r=[sq.r(), self.ones_f.r()], w=[ps.r(1)])
            self.chk("n3")
            if self.debug == 1:
                p.act(rs[:, 0:nb], sq[:, 0, 0:nb], AF.Sqrt, bias=self.eps_t[:, 0:1], scale=1.0 / D,
                      r=[ps.r(1), self.eps_t.r()], w=[rs.r()])
            elif self.debug in (2, 3):
                p.copy("dve", rs[:, 0:nb], ps[:, 1, 0:nb], r=[ps.r(1), self.eps_t.r()], w=[rs.r()])
            else:
                p.act(rs[:, 0:nb], ps[:, 1, 0:nb], AF.Sqrt, bias=self.eps_t[:, 0:1], scale=1.0 / D,
                      r=[ps.r(1), self.eps_t.r()], w=[rs.r()])
            self.chk("n4")
            p.op("dve", lambda e, o=rs[:, 0:nb]: e.reciprocal(out=o, in_=o), r=[rs.r()], w=[rs.r()])
            self.chk("n5")
            c0 = t0 if pad is None else pad_col(t0)
            if HTd is not None:
                c0 = 0
            for k in range(8):
                p.tt("dve", tmp[:, 0:nb], x[:, k, 0:nb], rs[:, 0:nb], ALU.mult, r=[x.r(), rs.r()], w=[tmp.r()])
                p.ts("pool", HT[:, k, c0:c0 + nb], tmp[:, 0:nb], self.g1[:, k, mc:mc + 1], ALU.mult,
                     self.mod[:, k, mc:mc + 1], ALU.add, r=[tmp.r(), self.g1.r(), self.mod.r()],
                     w=[HT.r(bi) if HTd is None else HT.r()])
            if HTd is not None:
                p.dma(htd_v[:, :, t0:t0 + nb], HT[:, :, 0:nb], r=[HT.r()], w=[HTd.r(bi)])
            self.chk("n6")
        p.sb_reset(mark)

    def stage_out(self, li, src, dst, last, U_loader, nK, KP, w_out_name, w_rows):
        p = self.p
        ps = self.ps
        pre = "l%d_" % li
        w_out = self.din(pre + w_out_name, [w_rows, D])
        mark = p.sb_mark()
        wst = p.sb("wo_st", [128, D], F32)
        wo = p.sb("wo", [128, nK, D], BF16)
        for kc in range(nK):
            p.dma(wst[0:KP, :], w_out[kc * KP:(kc + 1) * KP, :], w=[wst.r()])
            p.copy("pool", wo[0:KP, kc, :], wst[0:KP, :], r=[wst.r()], w=[wo.r()])
        xb = [p.sb("ox%d" % i, [128, 8, 512], F32) for i in range(2)]
        src_v = src.t.rearrange("(k p) t -> p k t", p=128)
        dst_v = dst.t.rearrange("(k p) t -> p k t", p=128)
        out_v = self.outT.t.rearrange("(k p) t -> p k t", p=128)
        for bi, (t0, nb, isc) in enumerate(tok_blocks()):
            if last and isc:
                continue
            mc = 1 if isc else 0
            U, ures = U_loader(bi, t0, nb)
            x = xb[bi % 2]
            p.dma(x[:, :, 0:nb], src_v[:, :, t0:t0 + nb], r=[src.r(bi)], w=[x.r()])
            for fc in range(8):
                bank = 2 + (fc % 2)
                for kc in range(nK):
                    p.mm(ps[:, bank, 0:nb], wo[0:KP, kc, fc * 128:(fc + 1) * 128], U[0:KP, kc, 0:nb],
                         start=(kc == 0), stop=(kc == nK - 1), r=[wo.r()] + ures, w=[ps.r(bank)])
                p.stt(x[:, fc, 0:nb], ps[:, bank, 0:nb], self.mod[:, 16 + fc, mc:mc + 1], x[:, fc, 0:nb],
                      ALU.mult, ALU.add, r=[ps.r(bank), self.mod.r(), x.r()], w=[x.r()])
            if last:
                p.dma(out_v[:, :, t0 - CTX:t0 - CTX + nb], x[:, :, 0:nb], r=[x.r()], w=[self.outT.r(bi)])
            else:
                p.dma(dst_v[:, :, t0:t0 + nb], x[:, :, 0:nb], r=[x.r()], w=[dst.r(bi)])
        p.sb_reset(mark)

    def consts(self):
        p = self.p
        if hasattr(self, "eps_t"):
            return
        self.eps_t = p.sb("eps", [128, 4], F32)
        p.memset("pool", self.eps_t[:, 0:1], 1e-6, w=[self.eps_t.r()])
        p.memset("pool", self.eps_t[:, 1:2], 1.0, w=[self.eps_t.r()])
        p.memset("pool", self.eps_t[:, 2:3], 0.0, w=[self.eps_t.r()])
        p.memset("pool", self.eps_t[:, 3:4], 64e-5, w=[self.eps_t.r()])

    def layer_rglru(self, li, src, dst, last):
        p = self.p
        ps = self.ps
        pre = "l%d_" % li
        w_in = self.din(pre + "w_in", [D, 2 * LRU_W])
        gate_w = self.din(pre + "gate_w", [2, 2, LRU_NB, LRU_BD, LRU_BD])
        lvec = self.din(pre + "lruvec", [LRU_BD, LRU_NB, 11])
        UT = p.dram(pre + "UT", [LRU_NB, LRU_BD, T], BF16)
        NP = LRU_BD
        HT = p.sb("HT", [128, 8, T], BF16)
        self.stage_norm(src, HT)
        self.chk("norm")
        mark = p.sb_mark()
        lv = p.sb("lv", [NP, LRU_NB, 11], F32)
        p.dma(lv[:, :, :], lvec[:, :, :], w=[lv.r()])
        cd = p.sb("cd", [NP, LRU_NB, 2], F32)
        p.act(cd[:, :, :], lv[:, :, 9:11], AF.Exp, scale=-1.0, r=[lv.r()], w=[cd.r()])
        p.act(cd[:, :, :], cd[:, :, :], AF.Ln, bias=self.eps_t[0:NP, 1:2], r=[cd.r(), self.eps_t.r()], w=[cd.r()])
        p.ts("dve", cd[:, :, :], cd[:, :, :], -8.0, ALU.mult, r=[cd.r()], w=[cd.r()])
        XR = p.sb("XR", [NP, PTOT], F32)
        X = p.sb("X", [NP, PTOT], F32)
        XB = p.sb("XB", [NP, PTOT], BF16)
        G = p.sb("G", [NP, PTOT], BF16)
        A = p.sb("A", [NP, PTOT], F32)
        B = p.sb("B", [NP, PTOT], F32)
        HS0 = p.sb("HS0", [NP, PTOT], F32)
        wst = p.sb("wst", [128, 8, 2 * NP], F32)
        wb = p.sb("wb", [128, 8, 2 * NP], BF16)
        gst = p.sb("gst", [NP, 4, NP], F32)
        gb = p.sb("gb", [NP, 4, NP], BF16)
        p.memset("pool", XR[:, :], 0.0, w=[XR.r()])
        w_v = w_in.t.rearrange("(k p) c -> p k c", p=128)
        blocks = tok_blocks()
        L = PTOT - 3
        segs = [(PC0, CTX), (PL0, SEQ)]
        for n in range(LRU_NB):
            c0 = n * NP
            p.dma(wst[:, :, 0:NP], w_v[:, :, c0:c0 + NP], w=[wst.r()])
            p.dma(wst[:, :, NP:2 * NP], w_v[:, :, LRU_W + c0:LRU_W + c0 + NP], w=[wst.r()])
            p.copy("pool", wb[:, :, :], wst[:, :, :], r=[wst.r()], w=[wb.r()])
            p.dma(gst[:, :, :], gate_w.t[:, :, n].rearrange("d g c e -> c (d g) e"), w=[gst.r()])
            p.copy("pool", gb[:, :, :], gst[:, :, :], r=[gst.r()], w=[gb.r()])
            p.memset("pool", XR[:, 0:PC0], 0.0, w=[XR.r()])
            p.memset("pool", XR[:, PC0 + CTX:PL0], 0.0, w=[XR.r()])
            p.memset("pool", XR[:, PL0 + SEQ:PTOT], 0.0, w=[XR.r()])
            for bi, (t0, nb, isc) in enumerate(blocks):
                pc = pad_col(t0)
                for k in range(8):
                    p.mm(ps[0:NP, 4, 0:nb], wb[:, k, 0:NP], HT[:, k, t0:t0 + nb], start=(k == 0), stop=(k == 7),
                         r=[wb.r(), HT.r(bi)], w=[ps.r(4)])
                p.copy("dve", XR[:, pc:pc + nb], ps[0:NP, 4, 0:nb], r=[ps.r(4)], w=[XR.r()])
                for k in range(8):
                    p.mm(ps[0:NP, 5, 0:nb], wb[:, k, NP:2 * NP], HT[:, k, t0:t0 + nb], start=(k == 0), stop=(k == 7),
                         r=[wb.r(), HT.r(bi)], w=[ps.r(5)])
                p.act(G[:, pc:pc + nb], ps[0:NP, 5, 0:nb], AF.Silu, r=[ps.r(5)], w=[G.r()])
            p.ts("dve", X[:, 2:2 + L], XR[:, 0:L], lv[:, n, 0:1], ALU.mult, lv[:, n, 4:5], ALU.add,
                 r=[XR.r(), lv.r()], w=[X.r()])
            for j in range(1, 4):
                p.stt(X[:, 2:2 + L], XR[:, j:j + L], lv[:, n, j:j + 1], X[:, 2:2 + L], ALU.mult, ALU.add,
                      r=[XR.r(), lv.r(), X.r()], w=[X.r()])
            p.copy("pool", XB[:, 2:2 + L], X[:, 2:2 + L], r=[X.r()], w=[XB.r()])
            HS = [HS0, XR]
            for d in range(2):
                for bi, (t0, nb, isc) in enumerate(blocks):
                    pc = pad_col(t0)
                    p.mm(ps[0:NP, 4, 0:nb], gb[:, d * 2 + 0, :], XB[:, pc:pc + nb], r=[gb.r(), XB.r()], w=[ps.r(4)])
                    p.act(A[:, pc:pc + nb], ps[0:NP, 4, 0:nb], AF.Sigmoid, bias=lv[:, n, 5 + d * 2:6 + d * 2],
                          r=[ps.r(4), lv.r()], w=[A.r()])
                    p.mm(ps[0:NP, 5, 0:nb], gb[:, d * 2 + 1, :], XB[:, pc:pc + nb], r=[gb.r(), XB.r()], w=[ps.r(5)])
                    p.act(B[:, pc:pc + nb], ps[0:NP, 5, 0:nb], AF.Sigmoid, bias=lv[:, n, 6 + d * 2:7 + d * 2],
                          r=[ps.r(5), lv.r()], w=[B.r()])
                H = HS[d]
                p.act(A[:, 2:2 + L], A[:, 2:2 + L], AF.Exp, scale=cd[:, n, d:d + 1], r=[A.r(), cd.r()], w=[A.r()])
                p.tt("dve", H[:, 2:2 + L], A[:, 2:2 + L], A[:, 2:2 + L], ALU.mult, r=[A.r()], w=[H.r()])
                p.act(H[:, 2:2 + L], H[:, 2:2 + L], AF.Sqrt, bias=self.eps_t[0:NP, 1:2], scale=-1.0,
                      r=[H.r(), self.eps_t.r()], w=[H.r()])
                p.tt("pool", B[:, 2:2 + L], B[:, 2:2 + L], X[:, 2:2 + L], ALU.mult, r=[B.r(), X.r()], w=[B.r()])
                p.tt("dve", B[:, 2:2 + L], B[:, 2:2 + L], H[:, 2:2 + L], ALU.mult, r=[B.r(), H.r()], w=[B.r()])
                for si, (s0, sl) in enumerate(segs):
                    if d == 0:
                        o_ap, a_ap, b_ap = H[:, s0:s0 + sl], A[:, s0:s0 + sl], B[:, s0:s0 + sl]
                        init = 0.0 if si == 0 else H[:, PC0 + CTX - 1:PC0 + CTX]
                    else:
                        o_ap, a_ap, b_ap = (H[:, s0:s0 + sl][:, ::-1], A[:, s0:s0 + sl][:, ::-1],
                                            B[:, s0:s0 + sl][:, ::-1])
                        init = 0.0 if si == 0 else H[:, PC0:PC0 + 1]
                    p.op("dve", lambda e, o=o_ap, a=a_ap, b=b_ap, i=init: e.tensor_tensor_scan(
                        out=o, data0=a, data1=b, initial=i, op0=ALU.mult, op1=ALU.add),
                        r=[A.r(), B.r(), H.r()], w=[H.r()])
            p.tt("pool", HS0[:, 2:2 + L], HS0[:, 2:2 + L], XR[:, 2:2 + L], ALU.add, r=[HS0.r(), XR.r()], w=[HS0.r()])
            p.tt("dve", XB[:, 2:2 + L], HS0[:, 2:2 + L], G[:, 2:2 + L], ALU.mult, r=[HS0.r(), G.r()], w=[XB.r()])
            p.dma(UT.t[n, :, 0:CTX], XB[:, PC0:PC0 + CTX], r=[XB.r()], w=[UT.r(n)])
            p.dma(UT.t[n, :, CTX:T], XB[:, PL0:PL0 + SEQ], r=[XB.r()], w=[UT.r(n)])
        p.barrier()
        p.sb_reset(mark)
        p.sb_reset(self.base_mark)
        ub = [p.sb("ub%d" % i, [NP, LRU_NB, 512], BF16) for i in range(2)]
        ut_v = UT.t.rearrange("n c t -> c n t")

        def loader(bi, t0, nb):
            u = ub[bi % 2]
            p.dma(u[:, :, 0:nb], ut_v[:, :, t0:t0 + nb], r=[UT.r(n) for n in range(LRU_NB)], w=[u.r()])
            return u, [u.r()]

        self.stage_out(li, src, dst, last, loader, LRU_NB, NP, "w_out", LRU_W)

    def layer_natten(self, li, src, dst, last):
        p = self.p
        ps = self.ps
        pre = "l%d_" % li
        w_in = self.din(pre + "w_in", [D, 4 * D])
        ropeD = self.din("ropeCS", [128, 2, SEQ])
        permD = self.din("permm", [128, 128])
        rpbD = self.din(pre + "rpbT", [128, 8 * 15 * 64])
        qkgD = self.din(pre + "qkg", [128, 2])
        OGT = p.dram(pre + "OGT", [8, 128, T], BF16)
        HT = p.sb("HT", [128, 8, T], BF16)
        self.stage_norm(src, HT)
        rpb = p.sb("rpb", [128, 8, 15, 64], BF16)
        identb = p.sb("identb", [128, 128], BF16)
        bdb = p.sb("bdb", [128, 128], BF16)
        perm = p.sb("perm", [128, 128], BF16)
        qkg = p.sb("qkg", [128, 2], F32)
        rope = p.sb("rope", [128, 2, SEQ], F32)
        m2 = p.sb_mark()
        stg = p.sb("stg", [128, 8 * 15 * 64], F32)
        p.dma(stg[:, :], rpbD[:, :], w=[stg.r()])
        p.copy("pool", rpb[:, :, :, :], stg[:, :].rearrange("p (a b c) -> p a b c", a=8, b=15), r=[stg.r()], w=[rpb.r()])
        p.dma(stg[:, 0:1536], self.cmask_in[:, :], w=[stg.r()])
        p.copy("pool", identb[:, :], stg[:, 640:768], r=[stg.r()], w=[identb.r()])
        p.copy("pool", bdb[:, :], stg[:, 768:896], r=[stg.r()], w=[bdb.r()])
        p.dma(stg[:, 0:128], permD[:, :], w=[stg.r()])
        p.copy("pool", perm[:, :], stg[:, 0:128], r=[stg.r()], w=[perm.r()])
        p.dma(qkg[:, :], qkgD[:, :], w=[qkg.r()])
        p.ts("dve", qkg[:, 0:1], qkg[:, 0:1], 0.125, ALU.mult, r=[qkg.r()], w=[qkg.r()])
        p.dma(rope[:, :, :], ropeD[:, :, :], w=[rope.r()])
        p.sb_reset(m2)
        wst = p.sb("wst", [128, 8, 128], F32)
        wb = [p.sb("wb%d" % j, [128, 8, 128], BF16) for j in range(4)]
        QR = p.sb("QR", [128, SEQ], BF16)
        QP = p.sb("QP", [128, SEQ], BF16)
        KR = p.sb("KR", [128, SEQ], BF16)
        KN = p.sb("KN", [128, 512], BF16)
        QC = p.sb("QC", [128, CTX], BF16)
        KC = p.sb("KC", [128, CTX], BF16)
        Gp = p.sb("Gp", [128, T], BF16)
        V2 = p.sb("V2", [128, T // 64, 64], BF16)
        OGp = p.sb("OGp", [128, T], BF16)
        F = [p.sb("nF%d" % j, [128, 512], F32) for j in range(4)]
        sqb = p.sb("sqb", [128, 512], BF16)
        PT = p.sb("PT", [128, 768], BF16)
        rec = p.sb("rec", [128, 128], F32)
        of = p.sb("of", [128, 128], F32)
        w_v = w_in.t.rearrange("(k p) c -> p k c", p=128)
        blocks = tok_blocks()
        GW = 64
        for oc in range(8):
            for j in range(4):
                p.dma(wst[:, :, :], w_v[:, :, j * D + oc * 128:j * D + (oc + 1) * 128], w=[wst.r()])
                p.copy("pool", wb[j][:, :, :], wst[:, :, :], r=[wst.r()], w=[wb[j].r()])
            for bi, (t0, nb, isc) in enumerate(blocks):
                N = slice(0, nb)
                c0 = t0 - CTX
                for which in range(2):
                    qf, sd, t1, t2 = F
                    for k in range(8):
                        p.mm(ps[:, 0, N], wb[which][:, k, :], HT[:, k, t0:t0 + nb], start=(k == 0), stop=(k == 7),
                             r=[wb[which].r(), HT.r(bi)], w=[ps.r(0)])
                    p.act(sqb[:, N], ps[:, 0, N], AF.Square, r=[ps.r(0)], w=[sqb.r()])
                    p.copy("act", qf[:, N], ps[:, 0, N], r=[ps.r(0)], w=[qf.r()])
                    p.mm(ps[:, 1, N], bdb[:, :], sqb[:, N], r=[bdb.r(), sqb.r()], w=[ps.r(1)])
                    p.act(sd[:, N], ps[:, 1, N], AF.Sqrt, bias=self.eps_t[:, 0:1], scale=1.0 / 64, r=[ps.r(1), self.eps_t.r()], w=[sd.r()])
                    p.op("dve", lambda e, o=sd[:, N]: e.reciprocal(out=o, in_=o), r=[sd.r()], w=[sd.r()])
                    p.tt("dve", qf[:, N], qf[:, N], sd[:, N], ALU.mult, r=[qf.r(), sd.r()], w=[qf.r()])
                    if isc:
                        dstt = QC if which == 0 else KC
                        p.ts("pool", dstt[:, N], qf[:, N], qkg[:, which:which + 1], ALU.mult, r=[qf.r(), qkg.r()], w=[dstt.r()])
                    else:
                        nbt, nsl = (QP, slice(c0, c0 + nb)) if which == 0 else (KN, N)
                        p.ts("pool", nbt[:, nsl], qf[:, N], qkg[:, which:which + 1], ALU.mult, r=[qf.r(), qkg.r()], w=[nbt.r()])
                        p.mm(ps[:, 1, N], perm[:, :], nbt[:, nsl], r=[perm.r(), nbt.r()], w=[ps.r(1)])
                        p.tt("pool", t1[:, N], nbt[:, nsl], rope[:, 0, c0:c0 + nb], ALU.mult, r=[nbt.r(), rope.r()], w=[t1.r()])
                        p.tt("dve", t2[:, N], ps[:, 1, N], rope[:, 1, c0:c0 + nb], ALU.mult, r=[ps.r(1), rope.r()], w=[t2.r()])
                        rt = QR if which == 0 else KR
                        p.tt("dve", rt[:, c0:c0 + nb], t1[:, N], t2[:, N], ALU.add, r=[t1.r(), t2.r()], w=[rt.r()])
                for k in range(8):
                    p.mm(ps[:, 2, N], wb[3][:, k, :], HT[:, k, t0:t0 + nb], start=(k == 0), stop=(k == 7),
                         r=[wb[3].r(), HT.r(bi)], w=[ps.r(2)])
                p.act(Gp[:, t0:t0 + nb], ps[:, 2, N], AF.Silu, r=[ps.r(2)], w=[Gp.r()])
                nrow = nb // 64
                for i0 in range(0, nrow, 8):
                    ng = min(8, nrow - i0)
                    for i in range(ng):
                        tk = t0 + (i0 + i) * 64
                        for half in range(2):
                            hp = slice(half * 64, half * 64 + 64)
                            for k in range(8):
                                p.mm(ps[hp, 3, i * 64:(i + 1) * 64], HT[:, k, tk:tk + 64], wb[2][:, k, half * 64:(half + 1) * 64],
                                     start=(k == 0), stop=(k == 7), r=[wb[2].r(), HT.r(bi)], w=[ps.r(3)])
                    r0 = t0 // 64 + i0
                    p.copy("act", V2[:, r0:r0 + ng, :], ps[:, 3, 0:ng * 64].rearrange("p (a b) -> p a b", b=64), r=[ps.r(3)], w=[V2.r()])
            for r in range(GW):
                start = min(max(r - 4, 0), GW - 8)
                qs = slice(r * 64, (r + 1) * 64)
                for half in range(2):
                    hp = slice(half * 64, half * 64 + 64)
                    hc = slice(half * 64, half * 64 + 64)
                    b0 = half * 4
                    for i in range(8):
                        kr = start + i
                        dr = kr - r + 7
                        p.mm(ps[hp, b0, i * 64:(i + 1) * 64], KR[hp, kr * 64:(kr + 1) * 64], QR[hp, qs], start=True, stop=False,
                             r=[KR.r(), QR.r()], w=[ps.r(b0)])
                        p.mm(ps[hp, b0, i * 64:(i + 1) * 64], rpb[hp, oc, dr, :], identb[hp, hc], start=False, stop=True,
                             r=[rpb.r(), identb.r()], w=[ps.r(b0)])
                    for j in range(4):
                        p.mm(ps[hp, b0 + 1, j * 64:(j + 1) * 64], KC[hp, j * 64:(j + 1) * 64], QP[hp, qs],
                             r=[KC.r(), QP.r()], w=[ps.r(b0 + 1)])
                    p.act(PT[hp, 0:512], ps[hp, b0, :], AF.Exp, r=[ps.r(b0)], w=[PT.r(half)])
                    p.act(PT[hp, 512:768], ps[hp, b0 + 1, 0:256], AF.Exp, r=[ps.r(b0 + 1)], w=[PT.r(half)])
                    for c in range(12):
                        vrow = (4 + start + c) if c < 8 else (c - 8)
                        p.mm(ps[hp, b0 + 2, 0:64], V2[hp, vrow, :], PT[hp, c * 64:(c + 1) * 64], start=(c == 0), stop=(c == 11),
                             r=[V2.r(), PT.r(half)], w=[ps.r(b0 + 2)])
                    for c in range(12):
                        p.mm(ps[hp, b0 + 3, 0:64], bdb[hp, hc], PT[hp, c * 64:(c + 1) * 64], start=(c == 0), stop=(c == 11),
                             r=[bdb.r(), PT.r(half)], w=[ps.r(b0 + 3)])
                    p.op("dve", lambda e, o=rec[hp, 0:64], a=ps[hp, b0 + 3, 0:64]: e.reciprocal(out=o, in_=a), r=[ps.r(b0 + 3)], w=[rec.r(half)])
                    p.tt("dve", of[hp, 0:64], ps[hp, b0 + 2, 0:64], rec[hp, 0:64], ALU.mult, r=[ps.r(b0 + 2), rec.r(half)], w=[of.r(half)])
                    tq = CTX + r * 64
                    p.tt("pool", OGp[hp, tq:tq + 64], of[hp, 0:64], Gp[hp, tq:tq + 64], ALU.mult, r=[of.r(half), Gp.r()], w=[OGp.r()])
            if not last:
                for qh in range(2):
                    qs = slice(qh * 128, (qh + 1) * 128)
                    for half in range(2):
                        hp = slice(half * 64, half * 64 + 64)
                        hc = slice(half * 64, half * 64 + 64)
                        b0 = half * 4
                        for j in range(4):
                            p.mm(ps[hp, b0, j * 128:(j + 1) * 128], KC[hp, j * 64:(j + 1) * 64], QC[hp, qs], r=[KC.r(), QC.r()], w=[ps.r(b0)])
                        p.act(PT[hp, 0:512], ps[hp, b0, :], AF.Exp, r=[ps.r(b0)], w=[PT.r(half)])
                        for j in range(4):
                            p.mm(ps[hp, b0 + 2, 0:128], V2[hp, j, :], PT[hp, j * 128:(j + 1) * 128], start=(j == 0), stop=(j == 3),
                                 r=[V2.r(), PT.r(half)], w=[ps.r(b0 + 2)])
                        for j in range(4):
                            p.mm(ps[hp, b0 + 3, 0:128], bdb[hp, hc], PT[hp, j * 128:(j + 1) * 128], start=(j == 0), stop=(j == 3),
                                 r=[bdb.r(), PT.r(half)], w=[ps.r(b0 + 3)])
                        p.op("dve", lambda e, o=rec[hp, :], a=ps[hp, b0 + 3, 0:128]: e.reciprocal(out=o, in_=a), r=[ps.r(b0 + 3)], w=[rec.r(half)])
                        p.tt("dve", of[hp, :], ps[hp, b0 + 2, 0:128], rec[hp, :], ALU.mult, r=[ps.r(b0 + 2), rec.r(half)], w=[of.r(half)])
                        p.tt("pool", OGp[hp, qs], of[hp, :], Gp[hp, qs], ALU.mult, r=[of.r(half), Gp.r()], w=[OGp.r()])
            lo = 0 if not last else CTX
            p.dma(OGT.t[oc, :, lo:T], OGp[:, lo:T], r=[OGp.r()], w=[OGT.r(oc)])
        p.sb_reset(self.base_mark)
        ub = [p.sb("ub%d" % i, [128, 8, 512], BF16) for i in range(2)]
        og_v = OGT.t.rearrange("n c t -> c n t")

        def loader(bi, t0, nb):
            u = ub[bi % 2]
            p.dma(u[:, :, 0:nb], og_v[:, :, t0:t0 + nb], r=[OGT.r(n) for n in range(8)], w=[u.r()])
            return u, [u.r()]

        self.stage_out(li, src, dst, last, loader, 8, 128, "w_out", D)

    def layer_rwkv(self, li, src, dst, last):
        p = self.p
        ps = self.ps
        pre = "l%d_" % li
        w_in = self.din(pre + "w_in", [4, D, D])
        ldown = self.din(pre + "lora_down", [2, 2, D, 64])
        lup = self.din(pre + "lora_up", [2, 2, 64, D])
        muD = self.din(pre + "muv", [128, 8, 6])
        rvD = self.din(pre + "rv", [128, 8, 6])
        rkD = self.din(pre + "rksel", [128, 8, 2])
        gnD = self.din(pre + "gnrep", [128, 2, D])
        cmD = self.cmask_in
        HTd = p.dram(pre + "HTd", [D, T], BF16)
        YB = p.dram(pre + "YB", [8, T, 130], F32)
        OGT = p.dram(pre + "OGT", [8, 128, T], BF16)
        self.stage_norm(src, None, HTd=HTd)
        mark0 = p.sb_mark()
        cm = p.sb("cm", [128, 1536], F32)
        p.dma(cm[:, :], cmD[:, :], w=[cm.r()])
        M4 = cm[:, 0:512]
        MUs = cm[:, 0:128]
        M3 = cm[:, 128:512]
        MLs = cm[:, 512:640]
        RST = cm[:, 1024:1536]
        identb = p.sb("identb", [128, 128], BF16)
        bdb = p.sb("bdb", [128, 128], BF16)
        p.copy("pool", identb[:, :], cm[:, 640:768], r=[cm.r()], w=[identb.r()])
        p.copy("pool", bdb[:, :], cm[:, 768:896], r=[cm.r()], w=[bdb.r()])
        mu = p.sb("mu", [128, 8, 6], F32)
        rv = p.sb("rv", [128, 8, 6], F32)
        rkf = p.sb("rkf", [128, 8, 2], F32)
        rkb = p.sb("rkb", [128, 8, 2], BF16)
        p.dma(mu[:, :, :], muD[:, :, :], w=[mu.r()])
        p.dma(rv[:, :, :], rvD[:, :, :], w=[rv.r()])
        p.dma(rkf[:, :, :], rkD[:, :, :], w=[rkf.r()])
        p.copy("pool", rkb[:, :, :], rkf[:, :, :], r=[rkf.r()], w=[rkb.r()])
        gn = p.sb("gn", [128, 2, D], F32)
        p.dma(gn[:, :, :], gnD[:, :, :], w=[gn.r()])
        W = [p.sb("W%d" % j, [128, 8, D], BF16) for j in range(4)]
        dnb = p.sb("dnb", [128, 8, 2, 2, 64], BF16)
        upb = p.sb("upb", [64, 2, 2, D], BF16)
        markw = p.sb_mark()
        wst = p.sb("wst", [128, 8, 512], F32)
        for j in range(4):
            wv = w_in.t[j].rearrange("(k p) c -> p k c", p=128)
            for hf in range(2):
                p.dma(wst[:, :, :], wv[:, :, hf * 512:(hf + 1) * 512], w=[wst.r()])
                p.copy("pool" if hf else "dve", W[j][:, :, hf * 512:(hf + 1) * 512], wst[:, :, :], r=[wst.r()], w=[W[j].r()])
        for d in range(2):
            for q in range(2):
                p.dma(wst[:, :, 0:64], ldown.t[d, q].rearrange("(k p) c -> p k c", p=128), w=[wst.r()])
                p.copy("pool", dnb[:, :, d, q, :], wst[:, :, 0:64], r=[wst.r()], w=[dnb.r()])
                p.dma(wst[0:64, 0:2, :], lup.t[d, q].rearrange("c (a b) -> c a b", a=2), w=[wst.r()])
                p.copy("pool", upb[:, d, q, :].rearrange("c (a b) -> c a b", a=2), wst[0:64, 0:2, :], r=[wst.r()], w=[upb.r()])
        p.sb_reset(markw)
        NBM = 512
        NTM = 4
        hb = p.sb("hb", [128, 8, NBM + 2], BF16)
        xx = p.sb("xx", [128, 8, NBM], BF16)
        xr_t = p.sb("xr", [128, 8, NBM], BF16)
        xk_t = p.sb("xk", [128, 8, NBM], BF16)
        xt_t = p.sb("xt", [128, 8, NBM], BF16)
        Vt = p.sb("Vt", [128, NTM, D], BF16)
        Gt = p.sb("Gt", [128, NTM, D], BF16)
        dwa = p.sb("dwa", [64, 2, NBM], BF16)
        F = [p.sb("F%d" % j, [128, NBM], F32) for j in range(9)]
        sqb = p.sb("sqb", [128, NBM], BF16)
        zb = p.sb("zb", [128, NBM], BF16)
        AR = p.sb("AR", [128, NTM, 256], BF16)
        BT = p.sb("BT", [128, NBM], BF16)
        KT = p.sb("KT", [128, NBM], BF16)
        TOK = p.sb("TOK", [128, NTM, 384], BF16)
        XX = p.sb("XX", [128, 384], BF16)
        GM3 = p.sb("GM3", [128, 384], BF16)
        Zs = p.sb("Zs", [128, 64], BF16)
        W12 = p.sb("W12", [128, 128], BF16)
        GT = p.sb("GT", [128, 128], BF16)
        QT = p.sb("QT", [128, 128], BF16)
        Hs = p.sb("Hs", [128, 8, 64], BF16)
        Htmp = p.sb("Htmp", [128, 64], F32)
        Yoc = p.sb("Yoc", [128, NTM, 130], F32)
        YBt = p.sb("YBt", [128, NTM, 130], F32)
        st1 = p.sb("st1", [128, NTM, 2], F32)
        st2 = p.sb("st2", [128, NTM, 2], F32)
        res = p.sb("res", [128, NTM, 128], BF16)
        ogb = p.sb("ogb", [128, NBM], BF16)
        htd_v = HTd.t.rearrange("(k p) t -> p k t", p=128)
        XX3 = XX[:, :].rearrange("p (a b) -> p a b", b=128)

        blocks = tok_blocks()
        for d in (1, 0):
            passB = d == 0
            p.memset("pool", Hs[:, :, :], 0.0, w=[Hs.r()])
            order = [blocks[0]] + (blocks[1:] if d == 0 else blocks[:0:-1])
            for (t0, nb, isc) in order:
                bi = blocks.index((t0, nb, isc))
                nt = nb // 128
                seg0, seg1 = (0, CTX) if isc else (CTX, T)

                def sv(ap2):
                    return ap2 if d == 0 else ap2[:, ::-1]

                lo = t0 - 1 if t0 > seg0 else t0
                hi = t0 + nb + 1 if t0 + nb < seg1 else t0 + nb
                p.dma(hb[:, :, 1 - (t0 - lo):1 + nb + (hi - t0 - nb)], htd_v[:, :, lo:hi], r=[HTd.r(b) for b in range(9)], w=[hb.r()])
                if lo == t0:
                    p.memset("pool", hb[:, :, 0:1], 0.0, w=[hb.r()])
                if hi == t0 + nb:
                    p.memset("pool", hb[:, :, nb + 1:nb + 2], 0.0, w=[hb.r()])
                p.tt("pool", xx[:, :, 0:nb], hb[:, :, 0:nb], hb[:, :, 2:nb + 2], ALU.add, r=[hb.r()], w=[xx.r()])
                p.stt(xx[:, :, 0:nb], xx[:, :, 0:nb], 0.5, hb[:, :, 1:nb + 1], ALU.mult, ALU.subtract, r=[xx.r(), hb.r()], w=[xx.r()])
                def mkx(dst_t, j):
                    for k in range(8):
                        p.stt(sv(dst_t[:, k, 0:nb]), xx[:, k, 0:nb], mu[:, k, j:j + 1], hb[:, k, 1:nb + 1], ALU.mult, ALU.add,
                              r=[xx.r(), mu.r(), hb.r()], w=[dst_t.r()])

                mkx(xr_t, 0)
                mkx(xk_t, 2)
                mkx(xt_t, 3)
                for i in range(nt):
                    for hf in range(2):
                        bank = hf
                        for k in range(8):
                            p.mm(ps[:, bank, :], xt_t[:, k, i * 128:(i + 1) * 128], W[2][:, k, hf * 512:(hf + 1) * 512],
                                 start=(k == 0), stop=(k == 7), r=[xt_t.r(), W[2].r()], w=[ps.r(bank)])
                        p.copy("act", Vt[:, i, hf * 512:(hf + 1) * 512], ps[:, bank, :], r=[ps.r(bank)], w=[Vt.r()])
                if passB:
                    mkx(xt_t, 5)
                    for i in range(nt):
                        for hf in range(2):
                            bank = hf
                            for k in range(8):
                                p.mm(ps[:, bank, :], xt_t[:, k, i * 128:(i + 1) * 128], W[3][:, k, hf * 512:(hf + 1) * 512],
                                     start=(k == 0), stop=(k == 7), r=[xt_t.r(), W[3].r()], w=[ps.r(bank)])
                            p.act(Gt[:, i, hf * 512:(hf + 1) * 512], ps[:, bank, :], AF.Silu, r=[ps.r(bank)], w=[Gt.r()])
                for q, jx in ((0, 1), (1, 4)):
                    mkx(xt_t, jx)
                    for k in range(8):
                        p.mm(ps[0:64, 0, 0:nb], dnb[:, k, d, q, :], xt_t[:, k, 0:nb], start=(k == 0), stop=(k == 7),
                             r=[dnb.r(), xt_t.r()], w=[ps.r(0)])
                    p.act(dwa[:, q, 0:nb], ps[0:64, 0, 0:nb], AF.Tanh if q == 0 else AF.Copy, r=[ps.r(0)], w=[dwa.r()])
                self.chk("r1")
                for oc in range(8):
                    cs = slice(oc * 128, (oc + 1) * 128)
                    rf, kf, sg, af, cum, epos, eneg, eprev, kk = F
                    N = slice(0, nb)
                    for k in range(8):
                        p.mm(ps[:, 0, N], W[0][:, k, cs], xr_t[:, k, N], start=(k == 0), stop=(k == 7), r=[W[0].r(), xr_t.r()], w=[ps.r(0)])
                    p.copy("act", rf[:, N], ps[:, 0, N], r=[ps.r(0)], w=[rf.r()])
                    for k in range(8):
                        p.mm(ps[:, 1, N], W[1][:, k, cs], xk_t[:, k, N], start=(k == 0), stop=(k == 7), r=[W[1].r(), xk_t.r()], w=[ps.r(1)])
                    p.copy("act", kf[:, N], ps[:, 1, N], r=[ps.r(1)], w=[kf.r()])
                    p.mm(ps[:, 0, N], upb[:, d, 0, cs], dwa[:, 0, N], r=[upb.r(), dwa.r()], w=[ps.r(0)])
                    p.act(sg[:, N], ps[:, 0, N], AF.Sigmoid, bias=rv[:, oc, 2 * d:2 * d + 1], r=[ps.r(0), rv.r()], w=[sg.r()])
                    p.mm(ps[:, 1, N], upb[:, d, 1, cs], dwa[:, 1, N], r=[upb.r(), dwa.r()], w=[ps.r(1)])
                    p.act(af[:, N], ps[:, 1, N], AF.Sigmoid, bias=rv[:, oc, 2 * d + 1:2 * d + 2], r=[ps.r(1), rv.r()], w=[af.r()])
                    p.ts("pool", sg[:, N], sg[:, N], -0.6065306597126334, ALU.mult, r=[sg.r()], w=[sg.r()])
                    p.op("dve", lambda e, o=cum[:, N], a=RST[:, N], b=sg[:, N]: e.tensor_tensor_scan(
                        out=o, data0=a, data1=b, initial=0.0, op0=ALU.mult, op1=ALU.add), r=[cm.r(), sg.r()], w=[cum.r()])
                    p.act(epos[:, N], cum[:, N], AF.Exp, r=[cum.r()], w=[epos.r()])
                    p.act(eneg[:, N], cum[:, N], AF.Exp, scale=-1.0, r=[cum.r()], w=[eneg.r()])
                    p.tt("pool", cum[:, N], cum[:, N], sg[:, N], ALU.subtract, r=[cum.r(), sg.r()], w=[cum.r()])
                    p.act(eprev[:, N], cum[:, N], AF.Exp, r=[cum.r()], w=[eprev.r()])
                    p.ts("dve", kk[:, N], kf[:, N], rv[:, oc, 4:5], ALU.mult, r=[kf.r(), rv.r()], w=[kk.r()])
                    p.act(sqb[:, N], kk[:, N], AF.Square, r=[kk.r()], w=[sqb.r()])
                    p.mm(ps[:, 0, N], bdb[:, :], sqb[:, N], r=[bdb.r(), sqb.r()], w=[ps.r(0)])
                    p.act(cum[:, N], ps[:, 0, N], AF.Sqrt, r=[ps.r(0)], w=[cum.r()])
                    p.ts("dve", cum[:, N], cum[:, N], 1e-12, ALU.max, r=[cum.r()], w=[cum.r()])
                    p.op("dve", lambda e, o=cum[:, N]: e.reciprocal(out=o, in_=o), r=[cum.r()], w=[cum.r()])
                    p.tt("dve", kk[:, N], kk[:, N], cum[:, N], ALU.mult, r=[kk.r(), cum.r()], w=[kk.r()])
                    r3 = lambda ap2: ap2.rearrange("p (i t) -> p i t", t=128)
                    p.stt(AR[:, 0:nt, 0:128], r3(kk[:, N]), -1.0, r3(eprev[:, N]), ALU.mult, ALU.mult,
                          r=[kk.r(), eprev.r()], w=[AR.r()])
                    p.tt("pool", AR[:, 0:nt, 128:256], r3(rf[:, N]), r3(epos[:, N]), ALU.mult, r=[rf.r(), epos.r()], w=[AR.r()])
                    p.tt("pool", eprev[:, N], kk[:, N], af[:, N], ALU.mult, r=[kk.r(), af.r(), AR.r()], w=[eprev.r()])
                    p.tt("dve", BT[:, N], eprev[:, N], eneg[:, N], ALU.mult, r=[eprev.r(), eneg.r()], w=[BT.r()])
                    p.ts("pool", af[:, N], af[:, N], -1.0, ALU.add, rv[:, oc, 5:6], ALU.mult, r=[af.r(), rv.r(), eprev.r()], w=[af.r()])
                    p.stt(kf[:, N], af[:, N], 1.0, kf[:, N], ALU.add, ALU.mult, r=[af.r(), kf.r()], w=[kf.r()])
                    p.tt("pool", KT[:, N], kf[:, N], eneg[:, N], ALU.mult, r=[kf.r(), eneg.r()], w=[KT.r()])
                    p.tt("dve", zb[:, N], rf[:, N], kf[:, N], ALU.mult, r=[rf.r(), kf.r()], w=[zb.r()])
                    self.chk("r2")
                    for i in range(nt):
                        ts_ = slice(i * 128, (i + 1) * 128)
                        p.mm(ps[:, 1, i * 2:i * 2 + 2], zb[:, ts_], rkb[:, oc, :], r=[zb.r(), rkb.r()], w=[ps.r(1)])
                    p.copy("act", Yoc[:, 0:nt, 128:130], ps[:, 1, 0:2 * nt].rearrange("p (i c) -> p i c", c=2), r=[ps.r(1)], w=[Yoc.r("b")])
                    for i in range(nt):
                        ts_ = slice(i * 128, (i + 1) * 128)
                        p.mm(ps[:, 0, 0:128], AR[:, i, 0:128], identb[:, :], r=[AR.r(), identb.r()], w=[ps.r(0)])
                        p.mm(ps[:, 0, 128:256], BT[:, ts_], identb[:, :], r=[BT.r(), identb.r()], w=[ps.r(0)])
                        p.mm(ps[:, 0, 256:384], KT[:, ts_], identb[:, :], r=[KT.r(), identb.r()], w=[ps.r(0)])
                        p.copy("act", TOK[:, i, :], ps[:, 0, 0:384], r=[ps.r(0)], w=[TOK.r()])
                    self.chk("r3")
                    for i in range(nt):
                        ts_ = slice(i * 128, (i + 1) * 128)
                        for half in range(2):
                            h = 2 * oc + half
                            hp = slice(half * 64, half * 64 + 64)
                            hc = slice(half * 64, half * 64 + 64)
                            Vh = Vt[:, i, h * 64:(h + 1) * 64]
                            Atok = TOK[:, i, half * 64:half * 64 + 64]
                            gb = 0 if half == 0 else 5
                            p.mm(ps[:, gb, 0:256], BT[hp, ts_], AR[hp, i, :], r=[BT.r(), AR.r()], w=[ps.r(gb)])
                            p.mm(ps[:, gb, 256:512], KT[hp, ts_], AR[hp, i, :], r=[KT.r(), AR.r()], w=[ps.r(gb)])
                            p.tt("dve", XX[:, 0:128], ps[:, gb, 0:128], MUs, ALU.mult, r=[ps.r(gb), cm.r()], w=[XX.r()])
                            p.tt("dve", GM3[:, :], ps[:, gb, 128:512], M3, ALU.mult, r=[ps.r(gb), cm.r()], w=[GM3.r()])
                            p.mm(ps[:, 1, 0:128], XX[:, 0:128], identb[:, :], r=[XX.r(), identb.r()], w=[ps.r(1)])
                            p.copy("act", XX[:, 256:384], ps[:, 1, 0:128], r=[ps.r(1)], w=[XX.r()])
                            p.copy("pool", XX[:, 128:256], identb[:, :], r=[identb.r()], w=[XX.r()])
                            self.chk("r4")
                            for n in range(6):
                                p.mm(ps[:, 2, 0:256], XX[:, 256:384], XX[:, 0:256], r=[XX.r()], w=[ps.r(2)])
                                if n < 5:
                                    p.mm(ps[:, 2, 256:384], XX[:, 0:128], XX[:, 256:384], r=[XX.r()], w=[ps.r(2)])
                                p.tt("dve", XX[:, 128:256], XX[:, 128:256], ps[:, 2, 128:256], ALU.add, r=[XX.r(), ps.r(2)], w=[XX.r()])
                                if n < 5:
                                    p.copy("act", XX3[:, 0:3:2, :], ps[:, 2, 0:384].rearrange("p (a b) -> p a b", b=128)[:, 0:3:2, :],
                                           r=[ps.r(2)], w=[XX.r()])
                            TT = XX[:, 128:256]
                            self.chk("r5")
                            p.mm(ps[:, 1, 128:192], GM3[:, 128:256], Vh, r=[GM3.r(), Vt.r()], w=[ps.r(1)])
                            p.copy("act", Zs[:, :], ps[:, 1, 128:192], r=[ps.r(1)], w=[Zs.r()])
                            p.mm(ps[:, 1, 192:256], TT, Atok, r=[XX.r(), TOK.r()], w=[ps.r(1)])
                            p.mm(ps[:, 1, 256:320], TT, Zs[:, :], r=[XX.r(), Zs.r()], w=[ps.r(1)])
                            p.copy("dve", W12[:, :], ps[:, 1, 192:320], r=[ps.r(1)], w=[W12.r()])
                            self.chk("r6")
                            p.mm(ps[hp, 1, 320:448], W12[:, 0:64], GM3[:, 0:128], r=[W12.r(), GM3.r()], w=[ps.r(1)])
                            p.tt("dve", GT[hp, :], ps[hp, 1, 320:448], AR[hp, i, 128:256], ALU.add, r=[ps.r(1), AR.r()], w=[GT.r()])
                            p.mm(ps[hp, 1, 448:512], W12[0:64, 0:64], TOK[0:64, i, 128 + half * 64:192 + half * 64],
                                 r=[W12.r(), TOK.r()], w=[ps.r(1)])
                            p.copy("act", QT[hp, 0:64], ps[hp, 1, 448:512], r=[ps.r(1)], w=[QT.r()])
                            p.mm(ps[hp, 7, 0:64], W12[64:128, 0:64], TOK[64:128, i, 128 + half * 64:192 + half * 64],
                                 r=[W12.r(), TOK.r()], w=[ps.r(7)])
                            p.copy("act", QT[hp, 64:128], ps[hp, 7, 0:64], r=[ps.r(7)], w=[QT.r()])
                            self.chk("r7")
                            p.mm(ps[:, 3, 0:64], GM3[:, 0:128], W12[:, 64:128], start=True, stop=False, r=[GM3.r(), W12.r()], w=[ps.r(3)])
                            p.mm(ps[:, 3, 0:64], GM3[:, 256:384], Vh, start=False, stop=(half == 1), r=[GM3.r(), Vt.r()], w=[ps.r(3)])
                            for j in range(2):
                                rows = slice(j * 64, j * 64 + 64)
                                if half == 0:
                                    p.mm(ps[rows, 3, 0:64], GT[hp, j * 64:j * 64 + 64], Hs[hp, oc, :], start=False, stop=(j == 1),
                                         r=[GT.r(), Hs.r()], w=[ps.r(3)], skip_group_check=True)
                                else:
                                    p.mm(ps[rows, 6, 0:64], GT[hp, j * 64:j * 64 + 64], Hs[hp, oc, :], start=True, stop=True,
                                         r=[GT.r(), Hs.r()], w=[ps.r(6)])
                                cN = i * 128 + j * 64 + 63
                                pC = epos[hp, cN:cN + 1]
                                bh = 4 if half == 0 else 7
                                bj = 4 if j == 0 else 7
                                ch = slice(0, 64) if bh == 4 else slice(64, 128)
                                cj = slice(0, 64) if bj == 4 else slice(64, 128)
                                same = bh == bj
                                p.mm(ps[hp, bh, ch], identb[hp, hc], Hs[hp, oc, :], start=True, stop=False, r=[identb.r(), Hs.r()], w=[ps.r(bh)])
                                p.mm(ps[hp, bh, ch], QT[hp, j * 64:j * 64 + 64], Hs[hp, oc, :], start=False, stop=(not same), r=[QT.r(), Hs.r()], w=[ps.r(bh)])
                                p.mm(ps[hp, bj, cj], TOK[rows, i, 128 + half * 64:192 + half * 64], W12[rows, 64:128], start=(not same), stop=False,
                                     r=[TOK.r(), W12.r()], w=[ps.r(bj)])
                                p.mm(ps[hp, bj, cj], TOK[rows, i, 256 + half * 64:320 + half * 64], Vt[rows, i, h * 64:(h + 1) * 64], start=False, stop=True,
                                     r=[TOK.r(), Vt.r()], w=[ps.r(bj)])
                                if same:
                                    p.act(Hs[hp, oc, :], ps[hp, bh, ch], AF.Copy, scale=pC, r=[ps.r(bh), epos.r()], w=[Hs.r()])
                                else:
                                    p.act(Htmp[hp, :], ps[hp, bh, ch], AF.Copy, scale=pC, r=[ps.r(bh), epos.r()], w=[Htmp.r()])
                                    p.stt(Hs[hp, oc, :], ps[hp, bj, cj], pC, Htmp[hp, :], ALU.mult, ALU.add,
                                          r=[ps.r(bj), epos.r(), Htmp.r()], w=[Hs.r()])
                            p.copy("dve", Yoc[:, i, half * 64:half * 64 + 64], ps[:, 3, 0:64], r=[ps.r(3)], w=[Yoc.r("y")])
                            if half == 1:
                                p.tt("dve", Yoc[:, i, 64:128], Yoc[:, i, 64:128], ps[:, 6, 0:64], ALU.add, r=[Yoc.r("y"), ps.r(6)], w=[Yoc.r("y")])
                            self.chk("r8")
                    if not passB:
                        for i in range(nt):
                            bank = i % 2
                            p.mm(ps[:, bank, 0:130], cm[:, 896:1024], Yoc[:, i, :], r=[cm.r(), Yoc.r("y"), Yoc.r("b")], w=[ps.r(bank)])
                            p.copy("act", YBt[:, nt - 1 - i, :], ps[:, bank, 0:130], r=[ps.r(bank)], w=[YBt.r()])
                        yv = YB.t[oc, t0:t0 + nb, :].rearrange("(i q) c -> q i c", q=128)
                        p.dma(yv, YBt[:, 0:nt, :], r=[YBt.r()], w=[YB.r((oc, bi))])
                        self.chk("r9")
                    else:
                        yv = YB.t[oc, t0:t0 + nb, :].rearrange("(i q) c -> q i c", q=128)
                        p.dma(YBt[:, 0:nt, :], yv, r=[YB.r((oc, bi))], w=[YBt.r()])
                        p.tt("dve", Yoc[:, 0:nt, :], Yoc[:, 0:nt, :], YBt[:, 0:nt, :], ALU.add, r=[Yoc.r("y"), Yoc.r("b"), YBt.r()], w=[Yoc.r("y"), Yoc.r("b")])
                        y4 = Yoc[:, 0:nt, 0:128].rearrange("p i (g c) -> p i g c", c=64)
                        bc = lambda t_: t_[:, 0:nt, :].unsqueeze(3).to_broadcast([128, nt, 2, 64])
                        p.op("dve", lambda e, o=st1[:, 0:nt, :], a=y4: e.tensor_reduce(out=o, in_=a, axis=AX.X, op=ALU.add), r=[Yoc.r("y")], w=[st1.r()])
                        p.ts("dve", st1[:, 0:nt, :], st1[:, 0:nt, :], -1.0 / 64, ALU.mult, r=[st1.r()], w=[st1.r()])
                        p.tt("dve", y4, y4, bc(st1), ALU.add, r=[Yoc.r("y"), st1.r()], w=[Yoc.r("y")])
                        sq4 = YBt[:, 0:nt, 0:128].rearrange("p i (g c) -> p i g c", c=64)
                        p.tt("pool", sq4, y4, y4, ALU.mult, r=[Yoc.r("y")], w=[YBt.r()])
                        p.op("dve", lambda e, o=st2[:, 0:nt, :], a=sq4: e.tensor_reduce(out=o, in_=a, axis=AX.X, op=ALU.add), r=[YBt.r()], w=[st2.r()])
                        p.act(st2[:, 0:nt, :], st2[:, 0:nt, :], AF.Sqrt, bias=self.eps_t[:, 3:4], scale=1.0 / 64, r=[st2.r(), self.eps_t.r()], w=[st2.r()])
                        p.op("dve", lambda e, o=st2[:, 0:nt, :]: e.reciprocal(out=o, in_=o), r=[st2.r()], w=[st2.r()])
                        p.tt("dve", y4, y4, bc(st2), ALU.mult, r=[Yoc.r("y"), st2.r()], w=[Yoc.r("y")])
                        yn = Yoc[:, 0:nt, 0:128]
                        gw = gn[:, 0, cs].unsqueeze(1).to_broadcast([128, nt, 128])
                        gb_ = gn[:, 1, cs].unsqueeze(1).to_broadcast([128, nt, 128])
                        p.tt("pool", yn, yn, gw, ALU.mult, r=[Yoc.r("y"), gn.r()], w=[Yoc.r("y")])
                        p.tt("pool", yn, yn, gb_, ALU.add, r=[Yoc.r("y"), gn.r()], w=[Yoc.r("y")])
                        bs = Yoc[:, 0:nt, 128:130].unsqueeze(3).to_broadcast([128, nt, 2, 64])
                        v4 = Vt[:, 0:nt, cs].rearrange("p i (g c) -> p i g c", c=64)
                        s4 = YBt[:, 0:nt, 0:128].rearrange("p i (g c) -> p i g c", c=64)
                        p.tt("dve", s4, v4, bs, ALU.mult, r=[Vt.r(), Yoc.r("b")], w=[YBt.r()])
                        p.tt("dve", yn, yn, YBt[:, 0:nt, 0:128], ALU.add, r=[Yoc.r("y"), YBt.r()], w=[Yoc.r("y")])
                        p.tt("dve", res[:, 0:nt, :], yn, Gt[:, 0:nt, cs], ALU.mult, r=[Yoc.r("y"), Gt.r()], w=[res.r()])
                        for i in range(nt):
                            p.mm(ps[:, 0, i * 128:(i + 1) * 128], res[:, i, :], identb[:, :], r=[res.r(), identb.r()], w=[ps.r(0)])
                        p.copy("act", ogb[:, 0:nb], ps[:, 0, 0:nb], r=[ps.r(0)], w=[ogb.r()])
                        p.dma(OGT.t[oc, :, t0:t0 + nb], ogb[:, 0:nb], r=[ogb.r()], w=[OGT.r(oc)])
        p.sb_reset(mark0)
        ub = [p.sb("ub%d" % i, [128, 8, 512], BF16) for i in range(2)]
        og_v = OGT.t.rearrange("n c t -> c n t")

        def loader(bi, t0, nb):
            u = ub[bi % 2]
            p.dma(u[:, :, 0:nb], og_v[:, :, t0:t0 + nb], r=[OGT.r(n) for n in range(8)], w=[u.r()])
            return u, [u.r()]

        self.stage_out(li, src, dst, last, loader, 8, 128, "w_out", D)


def _fm(v):
    return np.ascontiguousarray(np.asarray(v, np.float32).reshape(8, 128).T)


def host_inputs(inputs, layers):
    x = np.asarray(inputs["x"], np.float32)
    ctx = np.asarray(inputs["ctx"], np.float32)
    c = np.asarray(inputs["c"], np.float32)
    c_ctx = np.asarray(inputs["c_ctx"], np.float32)
    shared = {}
    for li in layers:
        pre = "l%d_" % li
        g = lambda n: np.asarray(inputs[pre + n], np.float32)
        shared[pre + "ada_w"] = np.ascontiguousarray(g("ada_w"))
        av = np.zeros((128, 32), np.float32)
        av[:, 0:24] = g("ada_b").reshape(24, 128).T
        av[:, 24:32] = g("norm_g").reshape(8, 128).T
        shared[pre + "adavec"] = av
        if li % 3 == 0:
            shared[pre + "w_in"] = np.ascontiguousarray(g("w_in"))
            shared[pre + "lora_down"] = np.ascontiguousarray(g("lora_down"))
            shared[pre + "lora_up"] = np.ascontiguousarray(g("lora_up"))
            shared[pre + "w_out"] = np.ascontiguousarray(g("w_out"))
            shared[pre + "muv"] = np.ascontiguousarray(g("mu").reshape(6, 8, 128).transpose(2, 1, 0))
            b0 = g("lora_b0")
            rvv = np.stack([b0[0, 0], b0[0, 1], b0[1, 0], b0[1, 1], g("k_ka")[0], g("k_ka")[1]], 0)
            shared[pre + "rv"] = np.ascontiguousarray(rvv.reshape(6, 8, 128).transpose(2, 1, 0))
            rk = g("r_k")
            rks = np.zeros((128, 8, 2), np.float32)
            for oc in range(8):
                for j in range(2):
                    rks[j * 64:(j + 1) * 64, oc, j] = rk[2 * oc + j]
            shared[pre + "rksel"] = rks
            shared[pre + "gnrep"] = np.ascontiguousarray(np.broadcast_to(g("gn")[None], (128, 2, D)))
        if li % 3 == 2:
            shared[pre + "w_in"] = np.ascontiguousarray(g("w_in"))
            shared[pre + "w_out"] = np.ascontiguousarray(g("w_out"))
            qg = g("qk_g")
            shared[pre + "qkg"] = np.ascontiguousarray(np.stack([np.tile(qg[0], 2), np.tile(qg[1], 2)], 1))
            rpb = g("rpb")
            cols = np.arange(64)
            cst = np.clip(cols - 8, 0, 48)
            col_ok = (cols[None, :] >= cst[:, None]) & (cols[None, :] < cst[:, None] + 16)
            dc = np.clip(cols[None, :] - cols[:, None] + 15, 0, 30)
            tb = np.zeros((128, 8, 15, 64), np.float32)
            for oc in range(8):
                for hf in range(2):
                    gath = rpb[2 * oc + hf][:, dc]
                    gath = np.where(col_ok[None], gath, np.float32(-30000.0))
                    tb[hf * 64:(hf + 1) * 64, oc] = gath.transpose(1, 0, 2)
            shared[pre + "rpbT"] = np.ascontiguousarray(tb.reshape(128, 8 * 15 * 64))
        if li % 3 == 1:
            shared[pre + "w_in"] = np.ascontiguousarray(g("w_in"))
            shared[pre + "gate_w"] = np.ascontiguousarray(g("gate_w"))
            shared[pre + "w_out"] = np.ascontiguousarray(g("w_out"))
            lv = np.zeros((LRU_BD, LRU_NB, 11), np.float32)
            lv[:, :, 0:4] = g("conv_w").reshape(4, LRU_NB, LRU_BD).transpose(2, 1, 0)
            lv[:, :, 4] = g("conv_b").reshape(LRU_NB, LRU_BD).T
            lv[:, :, 5:9] = g("gate_b").reshape(4, LRU_NB, LRU_BD).transpose(2, 1, 0)
            lv[:, :, 9:11] = g("lam").reshape(2, LRU_NB, LRU_BD).transpose(2, 1, 0)
            shared[pre + "lruvec"] = lv
    idx = np.arange(128)
    same = (idx[:, None] // 64) == (idx[None, :] // 64)
    mus = (same & (idx[:, None] < idx[None, :])).astype(np.float32)
    mui = (same & (idx[:, None] <= idx[None, :])).astype(np.float32)
    cmk = np.zeros((128, 1536), np.float32)
    cmk[:, 0:128] = mus
    cmk[:, 128:256] = mui
    cmk[:, 256:384] = mus
    cmk[:, 384:512] = mui
    cmk[:, 512:640] = mus.T
    cmk[:, 640:768] = np.eye(128, dtype=np.float32)
    cmk[:, 768:896] = same.astype(np.float32)
    cmk[:, 896:1024] = np.eye(128, dtype=np.float32)[::-1]
    cmk[:, 1024:1536] = (np.arange(512) % 64 != 0).astype(np.float32)[None, :]
    shared["cmask"] = cmk
    pp = np.arange(128)
    dd = pp % 64
    ww = dd % 32
    ff = ww % 16
    first = ww < 16
    inv = (10000.0 ** (-np.arange(16, dtype=np.float32) / 16)).astype(np.float32)
    tpos = np.arange(SEQ)
    posr = (tpos // 64).astype(np.float32)
    posc = (tpos % 64).astype(np.float32)
    pos = np.where((dd // 32 == 0)[:, None], posr[None, :], posc[None, :])
    ang = (pos * inv[ff][:, None]).astype(np.float32)
    rcs = np.zeros((128, 2, SEQ), np.float32)
    rcs[:, 0] = np.cos(ang)
    rcs[:, 1] = np.where(first[:, None], -np.sin(ang), np.sin(ang))
    shared["ropeCS"] = rcs
    partner = np.where(first, pp + 16, pp - 16)
    pm = np.zeros((128, 128), np.float32)
    pm[partner, pp] = 1.0
    shared["permm"] = pm
    maps = []
    for b in range(NCORES):
        m = dict(shared)
        xs = np.concatenate([ctx[b], x[b]], axis=0)
        m["xT"] = np.ascontiguousarray(xs.T)
        cc = np.zeros((128, 8, 2), np.float32)
        cc[:, :, 0] = c[b].reshape(8, 128).T
        cc[:, :, 1] = c_ctx.reshape(8, 128).T
        m["cc"] = cc
        maps.append(m)
    return maps


_MODEL_CACHE = {}


def run_model(inputs, layers=(0, 1, 2, 3), cores=NCORES, stop=None):
    key = (tuple(layers), stop)
    if key not in _MODEL_CACHE:
        _MODEL_CACHE[key] = Model(layers, stop=stop)
    m = _MODEL_CACHE[key]
    maps = host_inputs(inputs, layers)[:cores]
    res = run_bass_kernel_spmd(m.p.nc, maps, core_ids=list(range(cores)))
    outs = [np.asarray(r["outT"]).T for r in res.results]
    return np.stack(outs, axis=0)


def kernel(**inputs):
    out = run_model(inputs)
    return np.ascontiguousarray(out.astype(np.float32))
```

```python
import numpy as np
import ml_dtypes
import concourse.bass as bass
import concourse.mybir as mybir
from concourse.bass_utils import run_bass_kernel_spmd

F32 = mybir.dt.float32
BF16 = mybir.dt.bfloat16
AF = mybir.ActivationFunctionType
ALU = mybir.AluOpType
AX = mybir.AxisListType

D = 1024
SEQ = 4096
CTX = 256
T = SEQ + CTX
NCORES = 8
ARENA0 = 16512
ARENA_END = 229000
NDMASEM = 32


class Res:
    __slots__ = ("name", "w", "rs")

    def __init__(self, name=""):
        self.name = name
        self.w = None
        self.rs = {}


class Op:
    __slots__ = ("eng", "fn", "deps", "needed", "sem", "val", "is_dma", "seq")

    def __init__(self, eng, fn, is_dma=False):
        self.eng = eng
        self.fn = fn
        self.deps = []
        self.needed = False
        self.sem = None
        self.val = 0
        self.is_dma = is_dma
        self.seq = 0


class Tile:
    def __init__(self, t, name):
        self.t = t
        self.name = name
        self.res = Res(name)
        self._sub = {}

    def __getitem__(self, k):
        return self.t[k]

    def r(self, key=None):
        if key is None:
            return self.res
        if key not in self._sub:
            self._sub[key] = Res("%s/%s" % (self.name, key))
        return self._sub[key]


class Prog:
    ENGS = ("pe", "act", "dve", "pool", "sp")

    def __init__(self):
        self.nc = bass.Bass("TRN2", target_bir_lowering=False)
        self.ops = {e: [] for e in self.ENGS}
        self.nops = 0
        self.sb_off = ARENA0
        self.ndma = 0
        self.dma_ops = []
        self.uid = 0
        self.pending_dma = []
        self.carry = {}

    def sb(self, name, shape, dtype):
        esz = 2 if dtype == BF16 else 4
        n = 1
        for s in shape[1:]:
            n *= s
        nbytes = (n * esz + 63) // 64 * 64
        off = self.sb_off
        assert off + nbytes <= ARENA_END, ("SBUF overflow", name, off, nbytes)
        self.sb_off += nbytes
        self.uid += 1
        t = self.nc.alloc_sbuf_tensor_at("%s_%d" % (name, self.uid), list(shape), dtype, offset=off)
        return Tile(t, name)

    def sb_mark(self):
        return self.sb_off

    def sb_reset(self, mark):
        if mark < self.sb_off:
            self.barrier()
        self.sb_off = mark

    def dram(self, name, shape, dtype, kind="Internal"):
        t = self.nc.dram_tensor(name, list(shape), dtype, kind=kind)
        return Tile(t.ap(), name)

    def op(self, eng, fn, r=(), w=(), is_dma=False):
        o = Op(eng, fn, is_dma)
        self.nops += 1
        o.seq = self.nops
        deps = {}

        def add(d):
            if d is None or d is o:
                return
            if d.eng == "pe" and eng == "pe":
                return
            deps[id(d)] = d

        for x in r:
            add(x.w)
        for x in w:
            add(x.w)
            for lst in x.rs.values():
                for d in lst:
                    add(d)
        if is_dma:
            i = self.ndma
            self.ndma += 1
            if i >= NDMASEM:
                add(self.dma_ops[i - NDMASEM])
            self.dma_ops.append(o)
            self.pending_dma.append(o)
        best = {}
        out = []
        for d in deps.values():
            if d.is_dma:
                out.append(d)
            else:
                b = best.get(d.eng)
                if b is None or d.seq > b.seq:
                    best[d.eng] = d
        out.extend(best.values())
        if self.carry.get(eng):
            have = set(id(d) for d in out)
            for d in self.carry[eng]:
                if id(d) not in have and d is not o:
                    out.append(d)
            self.carry[eng] = []
        o.deps = out
        for d in out:
            d.needed = True
        for x in r:
            if is_dma:
                x.rs.setdefault("dma", []).append(o)
            else:
                x.rs[eng] = [o]
        for x in w:
            x.w = o
            x.rs = {}
        self.ops[eng].append(o)
        return o

    def barrier(self, final=False):
        D = []
        for e in self.ENGS:
            if e == "sp":
                continue
            if self.ops[e]:
                D.append(self.ops[e][-1])
        D.extend(self.pending_dma)
        self.pending_dma = []
        for d in D:
            d.needed = True
        if final:
            o = Op("sp", lambda eng: eng.nop(), False)
            self.nops += 1
            o.seq = self.nops
            o.deps = list(D)
            self.ops["sp"].append(o)
            return
        for e in self.ENGS:
            self.carry[e] = self.carry.get(e, []) + [d for d in D if not (d.eng == e and not d.is_dma and e == "pe")]

    def dma(self, out, in_, r=(), w=(), eng="sp"):
        return self.op(eng, lambda e: e.dma_start(out=out, in_=in_), r, w, is_dma=True)

    def mm(self, out, lhsT, rhs, start=True, stop=True, r=(), w=(), **kw):
        return self.op("pe", lambda e: e.matmul(out, lhsT, rhs, start=start, stop=stop, **kw), r, w)

    def tr(self, out, in_, ident, r=(), w=()):
        return self.op("pe", lambda e: e.transpose(out, in_, ident), r, w)

    def act(self, out, in_, func, bias=0.0, scale=1.0, r=(), w=(), eng="act"):
        return self.op(eng, lambda e: e.activation(out=out, in_=in_, func=func, bias=bias, scale=scale), r, w)

    def ts(self, eng, out, in0, s1, op0, s2=None, op1=None, r=(), w=()):
        if op1 is None:
            return self.op(eng, lambda e: e.tensor_scalar(out=out, in0=in0, scalar1=s1, scalar2=None, op0=op0), r, w)
        return self.op(eng, lambda e: e.tensor_scalar(out=out, in0=in0, scalar1=s1, scalar2=s2, op0=op0, op1=op1), r, w)

    def tt(self, eng, out, in0, in1, op, r=(), w=()):
        return self.op(eng, lambda e: e.tensor_tensor(out=out, in0=in0, in1=in1, op=op), r, w)

    def stt(self, out, in0, scalar, in1, op0, op1, r=(), w=()):
        return self.op("dve", lambda e: e.scalar_tensor_tensor(out=out, in0=in0, scalar=scalar, in1=in1, op0=op0, op1=op1), r, w)

    def copy(self, eng, out, in_, r=(), w=()):
        if eng == "act":
            return self.op(eng, lambda e: e.copy(out=out, in_=in_), r, w)
        return self.op(eng, lambda e: e.tensor_copy(out=out, in_=in_), r, w)

    def memset(self, eng, ap, val, w=()):
        return self.op(eng, lambda e: e.memset(ap, val), (), w)

    def emit(self):
        nc = self.nc
        from contextlib import ExitStack
        with ExitStack() as st:
            esem = {e: st.enter_context(nc.semaphore("s_" + e)) for e in self.ENGS if e != "sp"}
            dsem = [st.enter_context(nc.semaphore("d_%d" % i)) for i in range(NDMASEM)]
            for e in self.ENGS:
                cnt = 0
                for o in self.ops[e]:
                    if o.is_dma:
                        continue
                    if e == "sp":
                        continue
                    if o.needed:
                        cnt += 1
                        o.sem = esem[e]
                        o.val = cnt
            for i, o in enumerate(self.dma_ops):
                o.sem = dsem[i % NDMASEM]
                o.val = 16 * (i // NDMASEM + 1)
            spsem = st.enter_context(nc.semaphore("s_sp"))
            cnt = 0
            for o in self.ops["sp"]:
                if not o.is_dma and o.needed:
                    cnt += 1
                    o.sem = spsem
                    o.val = cnt
            block = st.enter_context(nc.Block())

            def run(ename):
                def body(eng):
                    waited = {}
                    for o in self.ops[ename]:
                        for d in o.deps:
                            key = id(d.sem)
                            if waited.get(key, 0) >= d.val:
                                continue
                            eng.wait_ge(d.sem, d.val)
                            waited[key] = d.val
                        ins = o.fn(eng)
                        if o.is_dma:
                            ins.then_inc(o.sem, 16)
                        elif o.needed:
                            ins.then_inc(o.sem, 1)
                return body

            block.tensor(run("pe"))
            block.scalar(run("act"))
            block.vector(run("dve"))
            block.gpsimd(run("pool"))
            block.sync(run("sp"))
        return nc


LRU_W = 1408
LRU_NB = 16
LRU_BD = 88
PC0 = 2
PL0 = 261
PTOT = 4358


def tok_blocks():
    out = [(0, 256, True)]
    for i in range(8):
        out.append((256 + 512 * i, 512, False))
    return out


def pad_col(t0):
    return PC0 + t0 if t0 < CTX else PL0 + (t0 - CTX)


class StopBuild(Exception):
    pass


class Model:
    def __init__(self, layers=(0, 1, 2, 3), debug=False, stop=None):
        self.p = Prog()
        self.layers = layers
        self.debug = debug
        self.stop = stop
        try:
            self.build()
        except StopBuild:
            self.p.barrier(final=True)
            self.p.emit()

    def chk(self, name):
        if self.stop == name:
            raise StopBuild()

    def build(self):
        p = self.p
        nc = p.nc
        self.xT_in = p.dram("xT", [D, T], F32, kind="ExternalInput")
        self.cc_in = p.dram("cc", [128, 8, 2], F32, kind="ExternalInput")
        self.outT = p.dram("outT", [D, SEQ], F32, kind="ExternalOutput")
        self.XS = p.dram("XS", [D, T], F32)
        self.inp = {}
        self.cmask_in = p.dram("cmask", [128, 1536], F32, kind="ExternalInput")
        pst = nc.alloc_psum_tensor("ps", [128, 8, 512], F32)
        self.ps = Tile(pst, "ps")
        self.ones_f = p.sb("ones_f", [128, 128], F32)
        p.memset("pool", self.ones_f[:, :], 1.0, w=[self.ones_f.r()])
        self.consts()
        self.cc = p.sb("cc", [128, 8, 2], F32)
        p.dma(self.cc[:, :, :], self.cc_in[:, :, :], w=[self.cc.r()])
        self.cact = p.sb("cact", [128, 8, 2], F32)
        p.act(self.cact[:, :, :], self.cc[:, :, :], AF.Silu, r=[self.cc.r()], w=[self.cact.r()])
        self.mod = p.sb("mod", [128, 24, 2], F32)
        self.g1 = p.sb("g1", [128, 8, 2], F32)
        self.base_mark = p.sb_mark()
        self.chk("setup")
        src = self.xT_in
        for li in range(4):
            if li not in self.layers:
                continue
            p.sb_reset(self.base_mark)
            last = li == max(self.layers)
            dst = self.XS
            if self.debug != 3:
                self.layer_prologue(li)
            if li % 3 == 1:
                self.layer_rglru(li, src, dst, last)
            elif li % 3 == 2:
                self.layer_natten(li, src, dst, last)
            else:
                self.layer_rwkv(li, src, dst, last)
            src = dst
            p.barrier()
        p.barrier(final=True)
        p.emit()

    def din(self, name, shape, dtype=F32):
        t = self.p.dram(name, shape, dtype, kind="ExternalInput")
        self.inp[name] = t
        return t

    def layer_prologue(self, li):
        p = self.p
        pre = "l%d_" % li
        ada_w = self.din(pre + "ada_w", [D, 3 * D])
        vec = self.din(pre + "adavec", [128, 32])
        mark = p.sb_mark()
        av = p.sb("adavec", [128, 32], F32)
        p.dma(av[:, :], vec[:, :], w=[av.r()])
        ps = self.ps
        aw_v = ada_w.t.rearrange("(k p) c -> p k c", p=128)
        wst = [p.sb("adaw%d" % i, [128, 8, 512], F32) for i in range(2)]
        for oc4 in range(6):
            wt = wst[oc4 % 2]
            p.dma(wt[:, :, :], aw_v[:, :, oc4 * 512:(oc4 + 1) * 512], w=[wt.r()])
            for j in range(4):
                oc = oc4 * 4 + j
                for k in range(8):
                    p.mm(ps[:, 0, oc * 2:oc * 2 + 2], wt[:, k, j * 128:(j + 1) * 128], self.cact[:, k, :],
                         start=(k == 0), stop=(k == 7), r=[wt.r(), self.cact.r()], w=[ps.r(0)])
        p.tt("dve", self.mod[:, :, :], ps[:, 0, 0:48].rearrange("p (c two) -> p c two", two=2),
             av[:, 0:24].unsqueeze(2).to_broadcast([128, 24, 2]), ALU.add,
             r=[ps.r(0), av.r()], w=[self.mod.r()])
        p.ts("dve", self.g1[:, :, :], self.mod[:, 8:16, :], 1.0, ALU.add, r=[self.mod.r()], w=[self.g1.r()])
        p.tt("dve", self.g1[:, :, :], self.g1[:, :, :], av[:, 24:32].unsqueeze(2).to_broadcast([128, 8, 2]), ALU.mult,
             r=[self.g1.r(), av.r()], w=[self.g1.r()])
        p.barrier()
        p.sb_reset(mark)
        self.chk("prologue")

    def stage_norm(self, src, HT, pad=None, HTd=None):
        p = self.p
        ps = self.ps
        mark = p.sb_mark()
        xb = [p.sb("nx%d" % i, [128, 8, 512], F32) for i in range(2)]
        sq = p.sb("nsq", [128, 8, 512], F32)
        rs = p.sb("nrs", [128, 512], F32)
        tmp = p.sb("ntmp", [128, 512], F32)
        src_v = src.t.rearrange("(k p) t -> p k t", p=128)
        if HTd is not None:
            hbk = [p.sb("nhb%d" % i, [128, 8, 512], BF16) for i in range(2)]
            htd_v = HTd.t.rearrange("(k p) t -> p k t", p=128)
        for bi, (t0, nb, isc) in enumerate(tok_blocks()):
            if HTd is not None:
                HT = hbk[bi % 2]
            x = xb[bi % 2]
            mc = 1 if isc else 0
            p.dma(x[:, :, 0:nb], src_v[:, :, t0:t0 + nb], r=[src.r(bi)], w=[x.r()])
            self.chk("n1")
            p.act(sq[:, :, 0:nb], x[:, :, 0:nb], AF.Square, r=[x.r()], w=[sq.r()])
            self.chk("n2")
            for k in range(8):
                p.mm(ps[:, 1, 0:nb], self.ones_f[:, :], sq[:, k, 0:nb], start=(k == 0), stop=(k == 7),
                     r=[sq.r(), self.ones_f.r()], w=[ps.r(1)])
            self.chk("n3")
            if self.debug == 1:
                p.act(rs[:, 0:nb], sq[:, 0, 0:nb], AF.Sqrt, bias=self.eps_t[:, 0:1], scale=1.0 / D,
                      r=[ps.r(1), self.eps_t.r()], w=[rs.r()])
            elif self.debug in (2, 3):
                p.copy("dve", rs[:, 0:nb], ps[:, 1, 0:nb], r=[ps.r(1), self.eps_t.r()], w=[rs.r()])
            else:
                p.act(rs[:, 0:nb], ps[:, 1, 0:nb], AF.Sqrt, bias=self.eps_t[:, 0:1], scale=1.0 / D,
                      r=[ps.r(1), self.eps_t.r()], w=[rs.r()])
            self.chk("n4")
            p.op("dve", lambda e, o=rs[:, 0:nb]: e.reciprocal(out=o, in_=o), r=[rs.r()], w=[rs.r()])
            self.chk("n5")
            c0 = t0 if pad is None else pad_col(t0)
            if HTd is not None:
                c0 = 0
            for k in range(8):
                p.tt("dve", tmp[:, 0:nb], x[:, k, 0:nb], rs[:, 0:nb], ALU.mult, r=[x.r(), rs.r()], w=[tmp.r()])
                p.ts("pool", HT[:, k, c0:c0 + nb], tmp[:, 0:nb], self.g1[:, k, mc:mc + 1], ALU.mult,
                     self.mod[:, k, mc:mc + 1], ALU.add, r=[tmp.r(), self.g1.r(), self.mod.r()],
                     w=[HT.r(bi) if HTd is None else HT.r()])
            if HTd is not None:
                p.dma(htd_v[:, :, t0:t0 + nb], HT[:, :, 0:nb], r=[HT.r()], w=[HTd.r(bi)])
            self.chk("n6")
        p.sb_reset(mark)

    def stage_out(self, li, src, dst, last, U_loader, nK, KP, w_out_name, w_rows):
        p = self.p
        ps = self.ps
        pre = "l%d_" % li
        w_out = self.din(pre + w_out_name, [w_rows, D])
        mark = p.sb_mark()
        wst = p.sb("wo_st", [128, D], F32)
        wo = p.sb("wo", [128, nK, D], BF16)
        for kc in range(nK):
            p.dma(wst[0:KP, :], w_out[kc * KP:(kc + 1) * KP, :], w=[wst.r()])
            p.copy("pool", wo[0:KP, kc, :], wst[0:KP, :], r=[wst.r()], w=[wo.r()])
        xb = [p.sb("ox%d" % i, [128, 8, 512], F32) for i in range(2)]
        src_v = src.t.rearrange("(k p) t -> p k t", p=128)
        dst_v = dst.t.rearrange("(k p) t -> p k t", p=128)
        out_v = self.outT.t.rearrange("(k p) t -> p k t", p=128)
        for bi, (t0, nb, isc) in enumerate(tok_blocks()):
            if last and isc:
                continue
            mc = 1 if isc else 0
            U, ures = U_loader(bi, t0, nb)
            x = xb[bi % 2]
            p.dma(x[:, :, 0:nb], src_v[:, :, t0:t0 + nb], r=[src.r(bi)], w=[x.r()])
            for fc in range(8):
                bank = 2 + (fc % 2)
                for kc in range(nK):
                    p.mm(ps[:, bank, 0:nb], wo[0:KP, kc, fc * 128:(fc + 1) * 128], U[0:KP, kc, 0:nb],
                         start=(kc == 0), stop=(kc == nK - 1), r=[wo.r()] + ures, w=[ps.r(bank)])
                p.stt(x[:, fc, 0:nb], ps[:, bank, 0:nb], self.mod[:, 16 + fc, mc:mc + 1], x[:, fc, 0:nb],
                      ALU.mult, ALU.add, r=[ps.r(bank), self.mod.r(), x.r()], w=[x.r()])
            if last:
                p.dma(out_v[:, :, t0 - CTX:t0 - CTX + nb], x[:, :, 0:nb], r=[x.r()], w=[self.outT.r(bi)])
            else:
                p.dma(dst_v[:, :, t0:t0 + nb], x[:, :, 0:nb], r=[x.r()], w=[dst.r(bi)])
        p.sb_reset(mark)

    def consts(self):
        p = self.p
        if hasattr(self, "eps_t"):
            return
        self.eps_t = p.sb("eps", [128, 4], F32)
        p.memset("pool", self.eps_t[:, 0:1], 1e-6, w=[self.eps_t.r()])
        p.memset("pool", self.eps_t[:, 1:2], 1.0, w=[self.eps_t.r()])
        p.memset("pool", self.eps_t[:, 2:3], 0.0, w=[self.eps_t.r()])
        p.memset("pool", self.eps_t[:, 3:4], 64e-5, w=[self.eps_t.r()])

    def layer_rglru(self, li, src, dst, last):
        p = self.p
        ps = self.ps
        pre = "l%d_" % li
        w_in = self.din(pre + "w_in", [D, 2 * LRU_W])
        gate_w = self.din(pre + "gate_w", [2, 2, LRU_NB, LRU_BD, LRU_BD])
        lvec = self.din(pre + "lruvec", [LRU_BD, LRU_NB, 11])
        UT = p.dram(pre + "UT", [LRU_NB, LRU_BD, T], BF16)
        NP = LRU_BD
        HT = p.sb("HT", [128, 8, T], BF16)
        self.stage_norm(src, HT)
        self.chk("norm")
        mark = p.sb_mark()
        lv = p.sb("lv", [NP, LRU_NB, 11], F32)
        p.dma(lv[:, :, :], lvec[:, :, :], w=[lv.r()])
        cd = p.sb("cd", [NP, LRU_NB, 2], F32)
        p.act(cd[:, :, :], lv[:, :, 9:11], AF.Exp, scale=-1.0, r=[lv.r()], w=[cd.r()])
        p.act(cd[:, :, :], cd[:, :, :], AF.Ln, bias=self.eps_t[0:NP, 1:2], r=[cd.r(), self.eps_t.r()], w=[cd.r()])
        p.ts("dve", cd[:, :, :], cd[:, :, :], -8.0, ALU.mult, r=[cd.r()], w=[cd.r()])
        XR = p.sb("XR", [NP, PTOT], F32)
        X = p.sb("X", [NP, PTOT], F32)
        XB = p.sb("XB", [NP, PTOT], BF16)
        G = p.sb("G", [NP, PTOT], BF16)
        A = p.sb("A", [NP, PTOT], F32)
        B = p.sb("B", [NP, PTOT], F32)
        HS0 = p.sb("HS0", [NP, PTOT], F32)
        wst = p.sb("wst", [128, 8, 2 * NP], F32)
        wb = p.sb("wb", [128, 8, 2 * NP], BF16)
        gst = p.sb("gst", [NP, 4, NP], F32)
        gb = p.sb("gb", [NP, 4, NP], BF16)
        p.memset("pool", XR[:, :], 0.0, w=[XR.r()])
        w_v = w_in.t.rearrange("(k p) c -> p k c", p=128)
        blocks = tok_blocks()
        L = PTOT - 3
        segs = [(PC0, CTX), (PL0, SEQ)]
        for n in range(LRU_NB):
            c0 = n * NP
            p.dma(wst[:, :, 0:NP], w_v[:, :, c0:c0 + NP], w=[wst.r()])
            p.dma(wst[:, :, NP:2 * NP], w_v[:, :, LRU_W + c0:LRU_W + c0 + NP], w=[wst.r()])
            p.copy("pool", wb[:, :, :], wst[:, :, :], r=[wst.r()], w=[wb.r()])
            p.dma(gst[:, :, :], gate_w.t[:, :, n].rearrange("d g c e -> c (d g) e"), w=[gst.r()])
            p.copy("pool", gb[:, :, :], gst[:, :, :], r=[gst.r()], w=[gb.r()])
            p.memset("pool", XR[:, 0:PC0], 0.0, w=[XR.r()])
            p.memset("pool", XR[:, PC0 + CTX:PL0], 0.0, w=[XR.r()])
            p.memset("pool", XR[:, PL0 + SEQ:PTOT], 0.0, w=[XR.r()])
            for bi, (t0, nb, isc) in enumerate(blocks):
                pc = pad_col(t0)
                for k in range(8):
                    p.mm(ps[0:NP, 4, 0:nb], wb[:, k, 0:NP], HT[:, k, t0:t0 + nb], start=(k == 0), stop=(k == 7),
                         r=[wb.r(), HT.r(bi)], w=[ps.r(4)])
                p.copy("dve", XR[:, pc:pc + nb], ps[0:NP, 4, 0:nb], r=[ps.r(4)], w=[XR.r()])
                for k in range(8):
                    p.mm(ps[0:NP, 5, 0:nb], wb[:, k, NP:2 * NP], HT[:, k, t0:t0 + nb], start=(k == 0), stop=(k == 7),
                         r=[wb.r(), HT.r(bi)], w=[ps.r(5)])
                p.act(G[:, pc:pc + nb], ps[0:NP, 5, 0:nb], AF.Silu, r=[ps.r(5)], w=[G.r()])
            p.ts("dve", X[:, 2:2 + L], XR[:, 0:L], lv[:, n, 0:1], ALU.mult, lv[:, n, 4:5], ALU.add,
                 r=[XR.r(), lv.r()], w=[X.r()])
            for j in range(1, 4):
                p.stt(X[:, 2:2 + L], XR[:, j:j + L], lv[:, n, j:j + 1], X[:, 2:2 + L], ALU.mult, ALU.add,
                      r=[XR.r(), lv.r(), X.r()], w=[X.r()])
            p.copy("pool", XB[:, 2:2 + L], X[:, 2:2 + L], r=[X.r()], w=[XB.r()])
            HS = [HS0, XR]
            for d in range(2):
                for bi, (t0, nb, isc) in enumerate(blocks):
                    pc = pad_col(t0)
                    p.mm(ps[0:NP, 4, 0:nb], gb[:, d * 2 + 0, :], XB[:, pc:pc + nb], r=[gb.r(), XB.r()], w=[ps.r(4)])
                    p.act(A[:, pc:pc + nb], ps[0:NP, 4, 0:nb], AF.Sigmoid, bias=lv[:, n, 5 + d * 2:6 + d * 2],
                          r=[ps.r(4), lv.r()], w=[A.r()])
                    p.mm(ps[0:NP, 5, 0:nb], gb[:, d * 2 + 1, :], XB[:, pc:pc + nb], r=[gb.r(), XB.r()], w=[ps.r(5)])
                    p.act(B[:, pc:pc + nb], ps[0:NP, 5, 0:nb], AF.Sigmoid, bias=lv[:, n, 6 + d * 2:7 + d * 2],
                          r=[ps.r(5), lv.r()], w=[B.r()])
                H = HS[d]
                p.act(A[:, 2:2 + L], A[:, 2:2 + L], AF.Exp, scale=cd[:, n, d:d + 1], r=[A.r(), cd.r()], w=[A.r()])
                p.tt("dve", H[:, 2:2 + L], A[:, 2:2 + L], A[:, 2:2 + L], ALU.mult, r=[A.r()], w=[H.r()])
                p.act(H[:, 2:2 + L], H[:, 2:2 + L], AF.Sqrt, bias=self.eps_t[0:NP, 1:2], scale=-1.0,
                      r=[H.r(), self.eps_t.r()], w=[H.r()])
                p.tt("pool", B[:, 2:2 + L], B[:, 2:2 + L], X[:, 2:2 + L], ALU.mult, r=[B.r(), X.r()], w=[B.r()])
                p.tt("dve", B[:, 2:2 + L], B[:, 2:2 + L], H[:, 2:2 + L], ALU.mult, r=[B.r(), H.r()], w=[B.r()])
                for si, (s0, sl) in enumerate(segs):
                    if d == 0:
                        o_ap, a_ap, b_ap = H[:, s0:s0 + sl], A[:, s0:s0 + sl], B[:, s0:s0 + sl]
                        init = 0.0 if si == 0 else H[:, PC0 + CTX - 1:PC0 + CTX]
                    else:
                        o_ap, a_ap, b_ap = (H[:, s0:s0 + sl][:, ::-1], A[:, s0:s0 + sl][:, ::-1],
                                            B[:, s0:s0 + sl][:, ::-1])
                        init = 0.0 if si == 0 else H[:, PC0:PC0 + 1]
                    p.op("dve", lambda e, o=o_ap, a=a_ap, b=b_ap, i=init: e.tensor_tensor_scan(
                        out=o, data0=a, data1=b, initial=i, op0=ALU.mult, op1=ALU.add),
                        r=[A.r(), B.r(), H.r()], w=[H.r()])
            p.tt("pool", HS0[:, 2:2 + L], HS0[:, 2:2 + L], XR[:, 2:2 + L], ALU.add, r=[HS0.r(), XR.r()], w=[HS0.r()])
            p.tt("dve", XB[:, 2:2 + L], HS0[:, 2:2 + L], G[:, 2:2 + L], ALU.mult, r=[HS0.r(), G.r()], w=[XB.r()])
            p.dma(UT.t[n, :, 0:CTX], XB[:, PC0:PC0 + CTX], r=[XB.r()], w=[UT.r(n)])
            p.dma(UT.t[n, :, CTX:T], XB[:, PL0:PL0 + SEQ], r=[XB.r()], w=[UT.r(n)])
        p.barrier()
        p.sb_reset(mark)
        p.sb_reset(self.base_mark)
        ub = [p.sb("ub%d" % i, [NP, LRU_NB, 512], BF16) for i in range(2)]
        ut_v = UT.t.rearrange("n c t -> c n t")

        def loader(bi, t0, nb):
            u = ub[bi % 2]
            p.dma(u[:, :, 0:nb], ut_v[:, :, t0:t0 + nb], r=[UT.r(n) for n in range(LRU_NB)], w=[u.r()])
            return u, [u.r()]

        self.stage_out(li, src, dst, last, loader, LRU_NB, NP, "w_out", LRU_W)

    def layer_natten(self, li, src, dst, last):
        p = self.p
        ps = self.ps
        pre = "l%d_" % li
        w_in = self.din(pre + "w_in", [D, 4 * D])
        ropeD = self.din("ropeCS", [128, 2, SEQ])
        permD = self.din("permm", [128, 128])
        rpbD = self.din(pre + "rpbT", [128, 8 * 15 * 64])
        qkgD = self.din(pre + "qkg", [128, 2])
        OGT = p.dram(pre + "OGT", [8, 128, T], BF16)
        HT = p.sb("HT", [128, 8, T], BF16)
        self.stage_norm(src, HT)
        rpb = p.sb("rpb", [128, 8, 15, 64], BF16)
        identb = p.sb("identb", [128, 128], BF16)
        bdb = p.sb("bdb", [128, 128], BF16)
        perm = p.sb("perm", [128, 128], BF16)
        qkg = p.sb("qkg", [128, 2], F32)
        rope = p.sb("rope", [128, 2, SEQ], F32)
        m2 = p.sb_mark()
        stg = p.sb("stg", [128, 8 * 15 * 64], F32)
        p.dma(stg[:, :], rpbD[:, :], w=[stg.r()])
        p.copy("pool", rpb[:, :, :, :], stg[:, :].rearrange("p (a b c) -> p a b c", a=8, b=15), r=[stg.r()], w=[rpb.r()])
        p.dma(stg[:, 0:1536], self.cmask_in[:, :], w=[stg.r()])
        p.copy("pool", identb[:, :], stg[:, 640:768], r=[stg.r()], w=[identb.r()])
        p.copy("pool", bdb[:, :], stg[:, 768:896], r=[stg.r()], w=[bdb.r()])
        p.dma(stg[:, 0:128], permD[:, :], w=[stg.r()])
        p.copy("pool", perm[:, :], stg[:, 0:128], r=[stg.r()], w=[perm.r()])
        p.dma(qkg[:, :], qkgD[:, :], w=[qkg.r()])
        p.ts("dve", qkg[:, 0:1], qkg[:, 0:1], 0.125, ALU.mult, r=[qkg.r()], w=[qkg.r()])
        p.dma(rope[:, :, :], ropeD[:, :, :], w=[rope.r()])
        p.sb_reset(m2)
        wst = p.sb("wst", [128, 8, 128], F32)
        wb = [p.sb("wb%d" % j, [128, 8, 128], BF16) for j in range(4)]
        QR = p.sb("QR", [128, SEQ], BF16)
        QP = p.sb("QP", [128, SEQ], BF16)
        KR = p.sb("KR", [128, SEQ], BF16)
        KN = p.sb("KN", [128, 512], BF16)
        QC = p.sb("QC", [128, CTX], BF16)
        KC = p.sb("KC", [128, CTX], BF16)
        Gp = p.sb("Gp", [128, T], BF16)
        V2 = p.sb("V2", [128, T // 64, 64], BF16)
        OGp = p.sb("OGp", [128, T], BF16)
        F = [p.sb("nF%d" % j, [128, 512], F32) for j in range(4)]
        sqb = p.sb("sqb", [128, 512], BF16)
        PT = p.sb("PT", [128, 768], BF16)
        rec = p.sb("rec", [128, 128], F32)
        of = p.sb("of", [128, 128], F32)
        w_v = w_in.t.rearrange("(k p) c -> p k c", p=128)
        blocks = tok_blocks()
        GW = 64
        for oc in range(8):
            for j in range(4):
                p.dma(wst[:, :, :], w_v[:, :, j * D + oc * 128:j * D + (oc + 1) * 128], w=[wst.r()])
                p.copy("pool", wb[j][:, :, :], wst[:, :, :], r=[wst.r()], w=[wb[j].r()])
            for bi, (t0, nb, isc) in enumerate(blocks):
                N = slice(0, nb)
                c0 = t0 - CTX
                for which in range(2):
                    qf, sd, t1, t2 = F
                    for k in range(8):
                        p.mm(ps[:, 0, N], wb[which][:, k, :], HT[:, k, t0:t0 + nb], start=(k == 0), stop=(k == 7),
                             r=[wb[which].r(), HT.r(bi)], w=[ps.r(0)])
                    p.act(sqb[:, N], ps[:, 0, N], AF.Square, r=[ps.r(0)], w=[sqb.r()])
                    p.copy("act", qf[:, N], ps[:, 0, N], r=[ps.r(0)], w=[qf.r()])
                    p.mm(ps[:, 1, N], bdb[:, :], sqb[:, N], r=[bdb.r(), sqb.r()], w=[ps.r(1)])
                    p.act(sd[:, N], ps[:, 1, N], AF.Sqrt, bias=self.eps_t[:, 0:1], scale=1.0 / 64, r=[ps.r(1), self.eps_t.r()], w=[sd.r()])
                    p.op("dve", lambda e, o=sd[:, N]: e.reciprocal(out=o, in_=o), r=[sd.r()], w=[sd.r()])
                    p.tt("dve", qf[:, N], qf[:, N], sd[:, N], ALU.mult, r=[qf.r(), sd.r()], w=[qf.r()])
                    if isc:
                        dstt = QC if which == 0 else KC
                        p.ts("pool", dstt[:, N], qf[:, N], qkg[:, which:which + 1], ALU.mult, r=[qf.r(), qkg.r()], w=[dstt.r()])
                    else:
                        nbt, nsl = (QP, slice(c0, c0 + nb)) if which == 0 else (KN, N)
                        p.ts("pool", nbt[:, nsl], qf[:, N], qkg[:, which:which + 1], ALU.mult, r=[qf.r(), qkg.r()], w=[nbt.r()])
                        p.mm(ps[:, 1, N], perm[:, :], nbt[:, nsl], r=[perm.r(), nbt.r()], w=[ps.r(1)])
                        p.tt("pool", t1[:, N], nbt[:, nsl], rope[:, 0, c0:c0 + nb], ALU.mult, r=[nbt.r(), rope.r()], w=[t1.r()])
                        p.tt("dve", t2[:, N], ps[:, 1, N], rope[:, 1, c0:c0 + nb], ALU.mult, r=[ps.r(1), rope.r()], w=[t2.r()])
                        rt = QR if which == 0 else KR
                        p.tt("dve", rt[:, c0:c0 + nb], t1[:, N], t2[:, N], ALU.add, r=[t1.r(), t2.r()], w=[rt.r()])
                for k in range(8):
                    p.mm(ps[:, 2, N], wb[3][:, k, :], HT[:, k, t0:t0 + nb], start=(k == 0), stop=(k == 7),
                         r=[wb[3].r(), HT.r(bi)], w=[ps.r(2)])
                p.act(Gp[:, t0:t0 + nb], ps[:, 2, N], AF.Silu, r=[ps.r(2)], w=[Gp.r()])
                nrow = nb // 64
                for i0 in range(0, nrow, 8):
                    ng = min(8, nrow - i0)
                    for i in range(ng):
                        tk = t0 + (i0 + i) * 64
                        for half in range(2):
                            hp = slice(half * 64, half * 64 + 64)
                            for k in range(8):
                                p.mm(ps[hp, 3, i * 64:(i + 1) * 64], HT[:, k, tk:tk + 64], wb[2][:, k, half * 64:(half + 1) * 64],
                                     start=(k == 0), stop=(k == 7), r=[wb[2].r(), HT.r(bi)], w=[ps.r(3)])
                    r0 = t0 // 64 + i0
                    p.copy("act", V2[:, r0:r0 + ng, :], ps[:, 3, 0:ng * 64].rearrange("p (a b) -> p a b", b=64), r=[ps.r(3)], w=[V2.r()])
            for r in range(GW):
                start = min(max(r - 4, 0), GW - 8)
                qs = slice(r * 64, (r + 1) * 64)
                for half in range(2):
                    hp = slice(half * 64, half * 64 + 64)
                    hc = slice(half * 64, half * 64 + 64)
                    b0 = half * 4
                    for i in range(8):
                        kr = start + i
                        dr = kr - r + 7
                        p.mm(ps[hp, b0, i * 64:(i + 1) * 64], KR[hp, kr * 64:(kr + 1) * 64], QR[hp, qs], start=True, stop=False,
                             r=[KR.r(), QR.r()], w=[ps.r(b0)])
                        p.mm(ps[hp, b0, i * 64:(i + 1) * 64], rpb[hp, oc, dr, :], identb[hp, hc], start=False, stop=True,
                             r=[rpb.r(), identb.r()], w=[ps.r(b0)])
                    for j in range(4):
                        p.mm(ps[hp, b0 + 1, j * 64:(j + 1) * 64], KC[hp, j * 64:(j + 1) * 64], QP[hp, qs],
                             r=[KC.r(), QP.r()], w=[ps.r(b0 + 1)])
                    p.act(PT[hp, 0:512], ps[hp, b0, :], AF.Exp, r=[ps.r(b0)], w=[PT.r(half)])
                    p.act(PT[hp, 512:768], ps[hp, b0 + 1, 0:256], AF.Exp, r=[ps.r(b0 + 1)], w=[PT.r(half)])
                    for c in range(12):
                        vrow = (4 + start + c) if c < 8 else (c - 8)
                        p.mm(ps[hp, b0 + 2, 0:64], V2[hp, vrow, :], PT[hp, c * 64:(c + 1) * 64], start=(c == 0), stop=(c == 11),
                             r=[V2.r(), PT.r(half)], w=[ps.r(b0 + 2)])
                    for c in range(12):
                        p.mm(ps[hp, b0 + 3, 0:64], bdb[hp, hc], PT[hp, c * 64:(c + 1) * 64], start=(c == 0), stop=(c == 11),
                             r=[bdb.r(), PT.r(half)], w=[ps.r(b0 + 3)])
                    p.op("dve", lambda e, o=rec[hp, 0:64], a=ps[hp, b0 + 3, 0:64]: e.reciprocal(out=o, in_=a), r=[ps.r(b0 + 3)], w=[rec.r(half)])
                    p.tt("dve", of[hp, 0:64], ps[hp, b0 + 2, 0:64], rec[hp, 0:64], ALU.mult, r=[ps.r(b0 + 2), rec.r(half)], w=[of.r(half)])
                    tq = CTX + r * 64
                    p.tt("pool", OGp[hp, tq:tq + 64], of[hp, 0:64], Gp[hp, tq:tq + 64], ALU.mult, r=[of.r(half), Gp.r()], w=[OGp.r()])
            if not last:
                for qh in range(2):
                    qs = slice(qh * 128, (qh + 1) * 128)
                    for half in range(2):
                        hp = slice(half * 64, half * 64 + 64)
                        hc = slice(half * 64, half * 64 + 64)
                        b0 = half * 4
                        for j in range(4):
                            p.mm(ps[hp, b0, j * 128:(j + 1) * 128], KC[hp, j * 64:(j + 1) * 64], QC[hp, qs], r=[KC.r(), QC.r()], w=[ps.r(b0)])
                        p.act(PT[hp, 0:512], ps[hp, b0, :], AF.Exp, r=[ps.r(b0)], w=[PT.r(half)])
                        for j in range(4):
                            p.mm(ps[hp, b0 + 2, 0:128], V2[hp, j, :], PT[hp, j * 128:(j + 1) * 128], start=(j == 0), stop=(j == 3),
                                 r=[V2.r(), PT.r(half)], w=[ps.r(b0 + 2)])
                        for j in range(4):
                            p.mm(ps[hp, b0 + 3, 0:128], bdb[hp, hc], PT[hp, j * 128:(j + 1) * 128], start=(j == 0), stop=(j == 3),
                                 r=[bdb.r(), PT.r(half)], w=[ps.r(b0 + 3)])
                        p.op("dve", lambda e, o=rec[hp, :], a=ps[hp, b0 + 3, 0:128]: e.reciprocal(out=o, in_=a), r=[ps.r(b0 + 3)], w=[rec.r(half)])
                        p.tt("dve", of[hp, :], ps[hp, b0 + 2, 0:128], rec[hp, :], ALU.mult, r=[ps.r(b0 + 2), rec.r(half)], w=[of.r(half)])
                        p.tt("pool", OGp[hp, qs], of[hp, :], Gp[hp, qs], ALU.mult, r=[of.r(half), Gp.r()], w=[OGp.r()])
            lo = 0 if not last else CTX
            p.dma(OGT.t[oc, :, lo:T], OGp[:, lo:T], r=[OGp.r()], w=[OGT.r(oc)])
        p.sb_reset(self.base_mark)
        ub = [p.sb("ub%d" % i, [128, 8, 512], BF16) for i in range(2)]
        og_v = OGT.t.rearrange("n c t -> c n t")

        def loader(bi, t0, nb):
            u = ub[bi % 2]
            p.dma(u[:, :, 0:nb], og_v[:, :, t0:t0 + nb], r=[OGT.r(n) for n in range(8)], w=[u.r()])
            return u, [u.r()]

        self.stage_out(li, src, dst, last, loader, 8, 128, "w_out", D)

    def layer_rwkv(self, li, src, dst, last):
        p = self.p
        ps = self.ps
        pre = "l%d_" % li
        w_in = self.din(pre + "w_in", [4, D, D])
        ldown = self.din(pre + "lora_down", [2, 2, D, 64])
        lup = self.din(pre + "lora_up", [2, 2, 64, D])
        muD = self.din(pre + "muv", [128, 8, 6])
        rvD = self.din(pre + "rv", [128, 8, 6])
        rkD = self.din(pre + "rksel", [128, 8, 2])
        gnD = self.din(pre + "gnrep", [128, 2, D])
        cmD = self.cmask_in
        HTd = p.dram(pre + "HTd", [D, T], BF16)
        YB = p.dram(pre + "YB", [8, T, 130], F32)
        OGT = p.dram(pre + "OGT", [8, 128, T], BF16)
        self.stage_norm(src, None, HTd=HTd)
        mark0 = p.sb_mark()
        cm = p.sb("cm", [128, 1536], F32)
        p.dma(cm[:, :], cmD[:, :], w=[cm.r()])
        M4 = cm[:, 0:512]
        MUs = cm[:, 0:128]
        M3 = cm[:, 128:512]
        MLs = cm[:, 512:640]
        RST = cm[:, 1024:1536]
        identb = p.sb("identb", [128, 128], BF16)
        bdb = p.sb("bdb", [128, 128], BF16)
        p.copy("pool", identb[:, :], cm[:, 640:768], r=[cm.r()], w=[identb.r()])
        p.copy("pool", bdb[:, :], cm[:, 768:896], r=[cm.r()], w=[bdb.r()])
        mu = p.sb("mu", [128, 8, 6], F32)
        rv = p.sb("rv", [128, 8, 6], F32)
        rkf = p.sb("rkf", [128, 8, 2], F32)
        rkb = p.sb("rkb", [128, 8, 2], BF16)
        p.dma(mu[:, :, :], muD[:, :, :], w=[mu.r()])
        p.dma(rv[:, :, :], rvD[:, :, :], w=[rv.r()])
        p.dma(rkf[:, :, :], rkD[:, :, :], w=[rkf.r()])
        p.copy("pool", rkb[:, :, :], rkf[:, :, :], r=[rkf.r()], w=[rkb.r()])
        gn = p.sb("gn", [128, 2, D], F32)
        p.dma(gn[:, :, :], gnD[:, :, :], w=[gn.r()])
        W = [p.sb("W%d" % j, [128, 8, D], BF16) for j in range(4)]
        dnb = p.sb("dnb", [128, 8, 2, 2, 64], BF16)
        upb = p.sb("upb", [64, 2, 2, D], BF16)
        markw = p.sb_mark()
        wst = p.sb("wst", [128, 8, 512], F32)
        for j in range(4):
            wv = w_in.t[j].rearrange("(k p) c -> p k c", p=128)
            for hf in range(2):
                p.dma(wst[:, :, :], wv[:, :, hf * 512:(hf + 1) * 512], w=[wst.r()])
                p.copy("pool" if hf else "dve", W[j][:, :, hf * 512:(hf + 1) * 512], wst[:, :, :], r=[wst.r()], w=[W[j].r()])
        for d in range(2):
            for q in range(2):
                p.dma(wst[:, :, 0:64], ldown.t[d, q].rearrange("(k p) c -> p k c", p=128), w=[wst.r()])
                p.copy("pool", dnb[:, :, d, q, :], wst[:, :, 0:64], r=[wst.r()], w=[dnb.r()])
                p.dma(wst[0:64, 0:2, :], lup.t[d, q].rearrange("c (a b) -> c a b", a=2), w=[wst.r()])
                p.copy("pool", upb[:, d, q, :].rearrange("c (a b) -> c a b", a=2), wst[0:64, 0:2, :], r=[wst.r()], w=[upb.r()])
        p.sb_reset(markw)
        NBM = 512
        NTM = 4
        hb = p.sb("hb", [128, 8, NBM + 2], BF16)
        xx = p.sb("xx", [128, 8, NBM], BF16)
        xr_t = p.sb("xr", [128, 8, NBM], BF16)
        xk_t = p.sb("xk", [128, 8, NBM], BF16)
        xt_t = p.sb("xt", [128, 8, NBM], BF16)
        Vt = p.sb("Vt", [128, NTM, D], BF16)
        Gt = p.sb("Gt", [128, NTM, D], BF16)
        dwa = p.sb("dwa", [64, 2, NBM], BF16)
        F = [p.sb("F%d" % j, [128, NBM], F32) for j in range(9)]
        sqb = p.sb("sqb", [128, NBM], BF16)
        zb = p.sb("zb", [128, NBM], BF16)
        AR = p.sb("AR", [128, NTM, 256], BF16)
        BT = p.sb("BT", [128, NBM], BF16)
        KT = p.sb("KT", [128, NBM], BF16)
        TOK = p.sb("TOK", [128, NTM, 384], BF16)
        XX = p.sb("XX", [128, 384], F32)
        TOKAf = p.sb("TOKAf", [128, NTM, 128], F32)
        identf = cm[:, 640:768]
        GM3 = p.sb("GM3", [128, 384], BF16)
        Zs = p.sb("Zs", [128, 64], F32)
        W12 = p.sb("W12", [128, 128], BF16)
        GT = p.sb("GT", [128, 128], BF16)
        QT = p.sb("QT", [128, 128], BF16)
        Hs = p.sb("Hs", [128, 8, 64], BF16)
        Htmp = p.sb("Htmp", [128, 64], F32)
        Yoc = p.sb("Yoc", [128, NTM, 130], F32)
        YBt = p.sb("YBt", [128, NTM, 130], F32)
        st1 = p.sb("st1", [128, NTM, 2], F32)
        st2 = p.sb("st2", [128, NTM, 2], F32)
        res = p.sb("res", [128, NTM, 128], BF16)
        ogb = p.sb("ogb", [128, NBM], BF16)
        htd_v = HTd.t.rearrange("(k p) t -> p k t", p=128)
        XX3 = XX[:, :].rearrange("p (a b) -> p a b", b=128)
        XXb = p.sb("XXb", [128, 384], F32)
        GM3b = p.sb("GM3b", [128, 384], BF16)
        XXh = [XX, XXb]
        GM3h = [GM3, GM3b]
        XX3h = [XX3, XXb[:, :].rearrange("p (a b) -> p a b", b=128)]

        blocks = tok_blocks()
        for d in (1, 0):
            passB = d == 0
            p.memset("pool", Hs[:, :, :], 0.0, w=[Hs.r()])
            order = [blocks[0]] + (blocks[1:] if d == 0 else blocks[:0:-1])
            for (t0, nb, isc) in order:
                bi = blocks.index((t0, nb, isc))
                nt = nb // 128
                seg0, seg1 = (0, CTX) if isc else (CTX, T)

                def sv(ap2):
                    return ap2 if d == 0 else ap2[:, ::-1]

                lo = t0 - 1 if t0 > seg0 else t0
                hi = t0 + nb + 1 if t0 + nb < seg1 else t0 + nb
                p.dma(hb[:, :, 1 - (t0 - lo):1 + nb + (hi - t0 - nb)], htd_v[:, :, lo:hi], r=[HTd.r(b) for b in range(9)], w=[hb.r()])
                if lo == t0:
                    p.memset("pool", hb[:, :, 0:1], 0.0, w=[hb.r()])
                if hi == t0 + nb:
                    p.memset("pool", hb[:, :, nb + 1:nb + 2], 0.0, w=[hb.r()])
                p.tt("pool", xx[:, :, 0:nb], hb[:, :, 0:nb], hb[:, :, 2:nb + 2], ALU.add, r=[hb.r()], w=[xx.r()])
                p.stt(xx[:, :, 0:nb], xx[:, :, 0:nb], 0.5, hb[:, :, 1:nb + 1], ALU.mult, ALU.subtract, r=[xx.r(), hb.r()], w=[xx.r()])
                def mkx(dst_t, j):
                    for k in range(8):
                        p.stt(sv(dst_t[:, k, 0:nb]), xx[:, k, 0:nb], mu[:, k, j:j + 1], hb[:, k, 1:nb + 1], ALU.mult, ALU.add,
                              r=[xx.r(), mu.r(), hb.r()], w=[dst_t.r()])

                mkx(xr_t, 0)
                mkx(xk_t, 2)
                mkx(xt_t, 3)
                for i in range(nt):
                    for hf in range(2):
                        bank = hf
                        for k in range(8):
                            p.mm(ps[:, bank, :], xt_t[:, k, i * 128:(i + 1) * 128], W[2][:, k, hf * 512:(hf + 1) * 512],
                                 start=(k == 0), stop=(k == 7), r=[xt_t.r(), W[2].r()], w=[ps.r(bank)])
                        p.copy("act", Vt[:, i, hf * 512:(hf + 1) * 512], ps[:, bank, :], r=[ps.r(bank)], w=[Vt.r()])
                if passB:
                    mkx(xt_t, 5)
                    for i in range(nt):
                        for hf in range(2):
                            bank = hf
                            for k in range(8):
                                p.mm(ps[:, bank, :], xt_t[:, k, i * 128:(i + 1) * 128], W[3][:, k, hf * 512:(hf + 1) * 512],
                                     start=(k == 0), stop=(k == 7), r=[xt_t.r(), W[3].r()], w=[ps.r(bank)])
                            p.act(Gt[:, i, hf * 512:(hf + 1) * 512], ps[:, bank, :], AF.Silu, r=[ps.r(bank)], w=[Gt.r()])
                for q, jx in ((0, 1), (1, 4)):
                    mkx(xt_t, jx)
                    for k in range(8):
                        p.mm(ps[0:64, 0, 0:nb], dnb[:, k, d, q, :], xt_t[:, k, 0:nb], start=(k == 0), stop=(k == 7),
                             r=[dnb.r(), xt_t.r()], w=[ps.r(0)])
                    p.act(dwa[:, q, 0:nb], ps[0:64, 0, 0:nb], AF.Tanh if q == 0 else AF.Copy, r=[ps.r(0)], w=[dwa.r()])
                self.chk("r1")
                for oc in range(8):
                    cs = slice(oc * 128, (oc + 1) * 128)
                    rf, kf, sg, af, cum, epos, eneg, eprev, kk = F
                    N = slice(0, nb)
                    for k in range(8):
                        p.mm(ps[:, 0, N], W[0][:, k, cs], xr_t[:, k, N], start=(k == 0), stop=(k == 7), r=[W[0].r(), xr_t.r()], w=[ps.r(0)])
                    p.copy("act", rf[:, N], ps[:, 0, N], r=[ps.r(0)], w=[rf.r()])
                    for k in range(8):
                        p.mm(ps[:, 1, N], W[1][:, k, cs], xk_t[:, k, N], start=(k == 0), stop=(k == 7), r=[W[1].r(), xk_t.r()], w=[ps.r(1)])
                    p.copy("act", kf[:, N], ps[:, 1, N], r=[ps.r(1)], w=[kf.r()])
                    p.mm(ps[:, 0, N], upb[:, d, 0, cs], dwa[:, 0, N], r=[upb.r(), dwa.r()], w=[ps.r(0)])
                    p.act(sg[:, N], ps[:, 0, N], AF.Sigmoid, bias=rv[:, oc, 2 * d:2 * d + 1], r=[ps.r(0), rv.r()], w=[sg.r()])
                    p.mm(ps[:, 1, N], upb[:, d, 1, cs], dwa[:, 1, N], r=[upb.r(), dwa.r()], w=[ps.r(1)])
                    p.act(af[:, N], ps[:, 1, N], AF.Sigmoid, bias=rv[:, oc, 2 * d + 1:2 * d + 2], r=[ps.r(1), rv.r()], w=[af.r()])
                    p.ts("pool", sg[:, N], sg[:, N], -0.6065306597126334, ALU.mult, r=[sg.r()], w=[sg.r()])
                    p.op("dve", lambda e, o=cum[:, N], a=RST[:, N], b=sg[:, N]: e.tensor_tensor_scan(
                        out=o, data0=a, data1=b, initial=0.0, op0=ALU.mult, op1=ALU.add), r=[cm.r(), sg.r()], w=[cum.r()])
                    p.act(epos[:, N], cum[:, N], AF.Exp, r=[cum.r()], w=[epos.r()])
                    p.act(eneg[:, N], cum[:, N], AF.Exp, scale=-1.0, r=[cum.r()], w=[eneg.r()])
                    p.tt("pool", cum[:, N], cum[:, N], sg[:, N], ALU.subtract, r=[cum.r(), sg.r()], w=[cum.r()])
                    p.act(eprev[:, N], cum[:, N], AF.Exp, r=[cum.r()], w=[eprev.r()])
                    p.ts("dve", kk[:, N], kf[:, N], rv[:, oc, 4:5], ALU.mult, r=[kf.r(), rv.r()], w=[kk.r()])
                    p.act(sqb[:, N], kk[:, N], AF.Square, r=[kk.r()], w=[sqb.r()])
                    p.mm(ps[:, 0, N], bdb[:, :], sqb[:, N], r=[bdb.r(), sqb.r()], w=[ps.r(0)])
                    p.act(cum[:, N], ps[:, 0, N], AF.Sqrt, r=[ps.r(0)], w=[cum.r()])
                    p.ts("dve", cum[:, N], cum[:, N], 1e-12, ALU.max, r=[cum.r()], w=[cum.r()])
                    p.op("dve", lambda e, o=cum[:, N]: e.reciprocal(out=o, in_=o), r=[cum.r()], w=[cum.r()])
                    p.tt("dve", kk[:, N], kk[:, N], cum[:, N], ALU.mult, r=[kk.r(), cum.r()], w=[kk.r()])
                    r3 = lambda ap2: ap2.rearrange("p (i t) -> p i t", t=128)
                    p.stt(AR[:, 0:nt, 0:128], r3(kk[:, N]), -1.0, r3(eprev[:, N]), ALU.mult, ALU.mult,
                          r=[kk.r(), eprev.r()], w=[AR.r()])
                    p.tt("pool", AR[:, 0:nt, 128:256], r3(rf[:, N]), r3(epos[:, N]), ALU.mult, r=[rf.r(), epos.r()], w=[AR.r()])
                    p.tt("pool", eprev[:, N], kk[:, N], af[:, N], ALU.mult, r=[kk.r(), af.r(), AR.r()], w=[eprev.r()])
                    p.tt("dve", BT[:, N], eprev[:, N], eneg[:, N], ALU.mult, r=[eprev.r(), eneg.r()], w=[BT.r()])
                    p.ts("pool", af[:, N], af[:, N], -1.0, ALU.add, rv[:, oc, 5:6], ALU.mult, r=[af.r(), rv.r(), eprev.r()], w=[af.r()])
                    p.stt(kf[:, N], af[:, N], 1.0, kf[:, N], ALU.add, ALU.mult, r=[af.r(), kf.r()], w=[kf.r()])
                    p.tt("pool", KT[:, N], kf[:, N], eneg[:, N], ALU.mult, r=[kf.r(), eneg.r()], w=[KT.r()])
                    p.tt("dve", zb[:, N], rf[:, N], kf[:, N], ALU.mult, r=[rf.r(), kf.r()], w=[zb.r()])
                    self.chk("r2")
                    for i in range(nt):
                        ts_ = slice(i * 128, (i + 1) * 128)
                        p.mm(ps[:, 1, i * 2:i * 2 + 2], zb[:, ts_], rkb[:, oc, :], r=[zb.r(), rkb.r()], w=[ps.r(1)])
                    p.copy("act", Yoc[:, 0:nt, 128:130], ps[:, 1, 0:2 * nt].rearrange("p (i c) -> p i c", c=2), r=[ps.r(1)], w=[Yoc.r("b")])
                    for i in range(nt):
                        ts_ = slice(i * 128, (i + 1) * 128)
                        p.mm(ps[:, 0, 0:128], AR[:, i, 0:128], identb[:, :], r=[AR.r(), identb.r()], w=[ps.r(0)])
                        p.mm(ps[:, 0, 128:256], BT[:, ts_], identb[:, :], r=[BT.r(), identb.r()], w=[ps.r(0)])
                        p.mm(ps[:, 0, 256:384], KT[:, ts_], identb[:, :], r=[KT.r(), identb.r()], w=[ps.r(0)])
                        p.copy("act", TOK[:, i, :], ps[:, 0, 0:384], r=[ps.r(0)], w=[TOK.r()])
                        p.copy("pool", TOKAf[:, i, :], TOK[:, i, 0:128], r=[TOK.r()], w=[TOKAf.r()])
                    self.chk("r3")
                    for i in range(nt):
                        ts_ = slice(i * 128, (i + 1) * 128)
                        for half in range(2):
                            hp = slice(half * 64, half * 64 + 64)
                            XXc, GMc = XXh[half], GM3h[half]
                            gb = 0 if half == 0 else 5
                            p.mm(ps[:, gb, 0:256], BT[hp, ts_], AR[hp, i, :], r=[BT.r(), AR.r()], w=[ps.r(gb)])
                            p.mm(ps[:, gb, 256:512], KT[hp, ts_], AR[hp, i, :], r=[KT.r(), AR.r()], w=[ps.r(gb)])
                            p.tt("dve", XXc[:, 0:128], ps[:, gb, 0:128], MUs, ALU.mult, r=[ps.r(gb), cm.r()], w=[XXc.r()])
                            p.tt("dve", GMc[:, :], ps[:, gb, 128:512], M3, ALU.mult, r=[ps.r(gb), cm.r()], w=[GMc.r()])
                            p.mm(ps[:, 1, 0:128], XXc[:, 0:128], identf, r=[XXc.r(), cm.r()], w=[ps.r(1)])
                            p.copy("act", XXc[:, 256:384], ps[:, 1, 0:128], r=[ps.r(1)], w=[XXc.r()])
                            p.copy("pool", XXc[:, 128:256], identf, r=[cm.r()], w=[XXc.r()])
                        for n in range(6):
                            for half in range(2):
                                XXc = XXh[half]
                                cb = 2 if half == 0 else 4
                                p.mm(ps[:, cb, 0:256], XXc[:, 256:384], XXc[:, 0:256], r=[XXc.r()], w=[ps.r(cb)])
                                if n < 5:
                                    p.mm(ps[:, cb, 256:384], XXc[:, 0:128], XXc[:, 256:384], r=[XXc.r()], w=[ps.r(cb)])
                                p.tt("dve", XXc[:, 128:256], XXc[:, 128:256], ps[:, cb, 128:256], ALU.add, r=[XXc.r(), ps.r(cb)], w=[XXc.r()])
                                if n < 5:
                                    p.copy("act", XX3h[half][:, 0:3:2, :], ps[:, cb, 0:384].rearrange("p (a b) -> p a b", b=128)[:, 0:3:2, :],
                                           r=[ps.r(cb)], w=[XXc.r()])
                        for half in range(2):
                            h = 2 * oc + half
                            hp = slice(half * 64, half * 64 + 64)
                            hc = slice(half * 64, half * 64 + 64)
                            Vh = Vt[:, i, h * 64:(h + 1) * 64]
                            XXc, GMc = XXh[half], GM3h[half]
                            TT = XXc[:, 128:256]
                            p.mm(ps[:, 1, 128:192], GMc[:, 128:256], Vh, r=[GMc.r(), Vt.r()], w=[ps.r(1)])
                            p.copy("act", Zs[:, :], ps[:, 1, 128:192], r=[ps.r(1)], w=[Zs.r()])
                            p.mm(ps[:, 1, 192:256], TT, TOKAf[:, i, half * 64:half * 64 + 64], r=[XXc.r(), TOKAf.r()], w=[ps.r(1)])
                            p.mm(ps[:, 1, 256:320], TT, Zs[:, :], r=[XXc.r(), Zs.r()], w=[ps.r(1)])
                            p.copy("dve", W12[:, :], ps[:, 1, 192:320], r=[ps.r(1)], w=[W12.r()])
                            p.mm(ps[hp, 1, 320:448], W12[:, 0:64], GMc[:, 0:128], r=[W12.r(), GMc.r()], w=[ps.r(1)])
                            p.tt("dve", GT[hp, :], ps[hp, 1, 320:448], AR[hp, i, 128:256], ALU.add, r=[ps.r(1), AR.r()], w=[GT.r()])
                            p.mm(ps[hp, 1, 448:512], W12[0:64, 0:64], TOK[0:64, i, 128 + half * 64:192 + half * 64],
                                 r=[W12.r(), TOK.r()], w=[ps.r(1)])
                            p.copy("act", QT[hp, 0:64], ps[hp, 1, 448:512], r=[ps.r(1)], w=[QT.r()])
                            p.mm(ps[hp, 7, 0:64], W12[64:128, 0:64], TOK[64:128, i, 128 + half * 64:192 + half * 64],
                                 r=[W12.r(), TOK.r()], w=[ps.r(7)])
                            p.copy("act", QT[hp, 64:128], ps[hp, 7, 0:64], r=[ps.r(7)], w=[QT.r()])
                            p.mm(ps[:, 3, 0:64], GMc[:, 0:128], W12[:, 64:128], start=True, stop=False, r=[GMc.r(), W12.r()], w=[ps.r(3)])
                            p.mm(ps[:, 3, 0:64], GMc[:, 256:384], Vh, start=False, stop=(half == 1), r=[GMc.r(), Vt.r()], w=[ps.r(3)])
                            for j in range(2):
                                rows = slice(j * 64, j * 64 + 64)
                                if half == 0:
                                    p.mm(ps[rows, 3, 0:64], GT[hp, j * 64:j * 64 + 64], Hs[hp, oc, :], start=False, stop=True,
                                         r=[GT.r(), Hs.r()], w=[ps.r(3)], skip_group_check=True)
                                else:
                                    p.mm(ps[rows, 6, 0:64], GT[hp, j * 64:j * 64 + 64], Hs[hp, oc, :], start=True, stop=True,
                                         r=[GT.r(), Hs.r()], w=[ps.r(6)])
                                cN = i * 128 + j * 64 + 63
                                pC = epos[hp, cN:cN + 1]
                                bh = 4 if half == 0 else 7
                                bj = 4 if j == 0 else 7
                                ch = slice(0, 64) if bh == 4 else slice(64, 128)
                                cj = slice(0, 64) if bj == 4 else slice(64, 128)
                                same = bh == bj
                                p.mm(ps[hp, bh, ch], identb[hp, hc], Hs[hp, oc, :], start=True, stop=False, r=[identb.r(), Hs.r()], w=[ps.r(bh)])
                                p.mm(ps[hp, bh, ch], QT[hp, j * 64:j * 64 + 64], Hs[hp, oc, :], start=False, stop=(not same), r=[QT.r(), Hs.r()], w=[ps.r(bh)])
                                p.mm(ps[hp, bj, cj], TOK[rows, i, 128 + half * 64:192 + half * 64], W12[rows, 64:128], start=(not same), stop=False,
                                     r=[TOK.r(), W12.r()], w=[ps.r(bj)])
                                p.mm(ps[hp, bj, cj], TOK[rows, i, 256 + half * 64:320 + half * 64], Vt[rows, i, h * 64:(h + 1) * 64], start=False, stop=True,
                                     r=[TOK.r(), Vt.r()], w=[ps.r(bj)])
                                if same:
                                    p.act(Hs[hp, oc, :], ps[hp, bh, ch], AF.Copy, scale=pC, r=[ps.r(bh), epos.r()], w=[Hs.r()])
                                else:
                                    p.act(Htmp[hp, :], ps[hp, bh, ch], AF.Copy, scale=pC, r=[ps.r(bh), epos.r()], w=[Htmp.r()])
                                    p.stt(Hs[hp, oc, :], ps[hp, bj, cj], pC, Htmp[hp, :], ALU.mult, ALU.add,
                                          r=[ps.r(bj), epos.r(), Htmp.r()], w=[Hs.r()])
                            p.copy("dve", Yoc[:, i, half * 64:half * 64 + 64], ps[:, 3, 0:64], r=[ps.r(3)], w=[Yoc.r("y")])
                            if half == 1:
                                p.tt("dve", Yoc[:, i, 64:128], Yoc[:, i, 64:128], ps[:, 6, 0:64], ALU.add, r=[Yoc.r("y"), ps.r(6)], w=[Yoc.r("y")])
                            self.chk("r8")
                    if not passB:
                        for i in range(nt):
                            bank = i % 2
                            p.mm(ps[:, bank, 0:130], cm[:, 896:1024], Yoc[:, i, :], r=[cm.r(), Yoc.r("y"), Yoc.r("b")], w=[ps.r(bank)])
                            p.copy("act", YBt[:, nt - 1 - i, :], ps[:, bank, 0:130], r=[ps.r(bank)], w=[YBt.r()])
                        yv = YB.t[oc, t0:t0 + nb, :].rearrange("(i q) c -> q i c", q=128)
                        p.dma(yv, YBt[:, 0:nt, :], r=[YBt.r()], w=[YB.r((oc, bi))])
                        self.chk("r9")
                    else:
                        yv = YB.t[oc, t0:t0 + nb, :].rearrange("(i q) c -> q i c", q=128)
                        p.dma(YBt[:, 0:nt, :], yv, r=[YB.r((oc, bi))], w=[YBt.r()])
                        p.tt("dve", Yoc[:, 0:nt, :], Yoc[:, 0:nt, :], YBt[:, 0:nt, :], ALU.add, r=[Yoc.r("y"), Yoc.r("b"), YBt.r()], w=[Yoc.r("y"), Yoc.r("b")])
                        y4 = Yoc[:, 0:nt, 0:128].rearrange("p i (g c) -> p i g c", c=64)
                        bc = lambda t_: t_[:, 0:nt, :].unsqueeze(3).to_broadcast([128, nt, 2, 64])
                        p.op("dve", lambda e, o=st1[:, 0:nt, :], a=y4: e.tensor_reduce(out=o, in_=a, axis=AX.X, op=ALU.add), r=[Yoc.r("y")], w=[st1.r()])
                        p.ts("dve", st1[:, 0:nt, :], st1[:, 0:nt, :], -1.0 / 64, ALU.mult, r=[st1.r()], w=[st1.r()])
                        p.tt("dve", y4, y4, bc(st1), ALU.add, r=[Yoc.r("y"), st1.r()], w=[Yoc.r("y")])
                        sq4 = YBt[:, 0:nt, 0:128].rearrange("p i (g c) -> p i g c", c=64)
                        p.tt("pool", sq4, y4, y4, ALU.mult, r=[Yoc.r("y")], w=[YBt.r()])
                        p.op("dve", lambda e, o=st2[:, 0:nt, :], a=sq4: e.tensor_reduce(out=o, in_=a, axis=AX.X, op=ALU.add), r=[YBt.r()], w=[st2.r()])
                        p.act(st2[:, 0:nt, :], st2[:, 0:nt, :], AF.Sqrt, bias=self.eps_t[:, 3:4], scale=1.0 / 64, r=[st2.r(), self.eps_t.r()], w=[st2.r()])
                        p.op("dve", lambda e, o=st2[:, 0:nt, :]: e.reciprocal(out=o, in_=o), r=[st2.r()], w=[st2.r()])
                        p.tt("dve", y4, y4, bc(st2), ALU.mult, r=[Yoc.r("y"), st2.r()], w=[Yoc.r("y")])
                        yn = Yoc[:, 0:nt, 0:128]
                        gw = gn[:, 0, cs].unsqueeze(1).to_broadcast([128, nt, 128])
                        gb_ = gn[:, 1, cs].unsqueeze(1).to_broadcast([128, nt, 128])
                        p.tt("pool", yn, yn, gw, ALU.mult, r=[Yoc.r("y"), gn.r()], w=[Yoc.r("y")])
                        p.tt("pool", yn, yn, gb_, ALU.add, r=[Yoc.r("y"), gn.r()], w=[Yoc.r("y")])
                        bs = Yoc[:, 0:nt, 128:130].unsqueeze(3).to_broadcast([128, nt, 2, 64])
                        v4 = Vt[:, 0:nt, cs].rearrange("p i (g c) -> p i g c", c=64)
                        s4 = YBt[:, 0:nt, 0:128].rearrange("p i (g c) -> p i g c", c=64)
                        p.tt("dve", s4, v4, bs, ALU.mult, r=[Vt.r(), Yoc.r("b")], w=[YBt.r()])
                        p.tt("dve", yn, yn, YBt[:, 0:nt, 0:128], ALU.add, r=[Yoc.r("y"), YBt.r()], w=[Yoc.r("y")])
                        p.tt("dve", res[:, 0:nt, :], yn, Gt[:, 0:nt, cs], ALU.mult, r=[Yoc.r("y"), Gt.r()], w=[res.r()])
                        for i in range(nt):
                            p.mm(ps[:, 0, i * 128:(i + 1) * 128], res[:, i, :], identb[:, :], r=[res.r(), identb.r()], w=[ps.r(0)])
                        p.copy("act", ogb[:, 0:nb], ps[:, 0, 0:nb], r=[ps.r(0)], w=[ogb.r()])
                        p.dma(OGT.t[oc, :, t0:t0 + nb], ogb[:, 0:nb], r=[ogb.r()], w=[OGT.r(oc)])
        p.sb_reset(mark0)
        ub = [p.sb("ub%d" % i, [128, 8, 512], BF16) for i in range(2)]
        og_v = OGT.t.rearrange("n c t -> c n t")

        def loader(bi, t0, nb):
            u = ub[bi % 2]
            p.dma(u[:, :, 0:nb], og_v[:, :, t0:t0 + nb], r=[OGT.r(n) for n in range(8)], w=[u.r()])
            return u, [u.r()]

        self.stage_out(li, src, dst, last, loader, 8, 128, "w_out", D)


def _fm(v):
    return np.ascontiguousarray(np.asarray(v, np.float32).reshape(8, 128).T)


def host_inputs(inputs, layers):
    x = np.asarray(inputs["x"], np.float32)
    ctx = np.asarray(inputs["ctx"], np.float32)
    c = np.asarray(inputs["c"], np.float32)
    c_ctx = np.asarray(inputs["c_ctx"], np.float32)
    shared = {}
    for li in layers:
        pre = "l%d_" % li
        g = lambda n: np.asarray(inputs[pre + n], np.float32)
        shared[pre + "ada_w"] = np.ascontiguousarray(g("ada_w"))
        av = np.zeros((128, 32), np.float32)
        av[:, 0:24] = g("ada_b").reshape(24, 128).T
        av[:, 24:32] = g("norm_g").reshape(8, 128).T
        shared[pre + "adavec"] = av
        if li % 3 == 0:
            shared[pre + "w_in"] = np.ascontiguousarray(g("w_in"))
            shared[pre + "lora_down"] = np.ascontiguousarray(g("lora_down"))
            shared[pre + "lora_up"] = np.ascontiguousarray(g("lora_up"))
            shared[pre + "w_out"] = np.ascontiguousarray(g("w_out"))
            shared[pre + "muv"] = np.ascontiguousarray(g("mu").reshape(6, 8, 128).transpose(2, 1, 0))
            b0 = g("lora_b0")
            rvv = np.stack([b0[0, 0], b0[0, 1], b0[1, 0], b0[1, 1], g("k_ka")[0], g("k_ka")[1]], 0)
            shared[pre + "rv"] = np.ascontiguousarray(rvv.reshape(6, 8, 128).transpose(2, 1, 0))
            rk = g("r_k")
            rks = np.zeros((128, 8, 2), np.float32)
            for oc in range(8):
                for j in range(2):
                    rks[j * 64:(j + 1) * 64, oc, j] = rk[2 * oc + j]
            shared[pre + "rksel"] = rks
            shared[pre + "gnrep"] = np.ascontiguousarray(np.broadcast_to(g("gn")[None], (128, 2, D)))
        if li % 3 == 2:
            shared[pre + "w_in"] = np.ascontiguousarray(g("w_in"))
            shared[pre + "w_out"] = np.ascontiguousarray(g("w_out"))
            qg = g("qk_g")
            shared[pre + "qkg"] = np.ascontiguousarray(np.stack([np.tile(qg[0], 2), np.tile(qg[1], 2)], 1))
            rpb = g("rpb")
            cols = np.arange(64)
            cst = np.clip(cols - 8, 0, 48)
            col_ok = (cols[None, :] >= cst[:, None]) & (cols[None, :] < cst[:, None] + 16)
            dc = np.clip(cols[None, :] - cols[:, None] + 15, 0, 30)
            tb = np.zeros((128, 8, 15, 64), np.float32)
            for oc in range(8):
                for hf in range(2):
                    gath = rpb[2 * oc + hf][:, dc]
                    gath = np.where(col_ok[None], gath, np.float32(-30000.0))
                    tb[hf * 64:(hf + 1) * 64, oc] = gath.transpose(1, 0, 2)
            shared[pre + "rpbT"] = np.ascontiguousarray(tb.reshape(128, 8 * 15 * 64))
        if li % 3 == 1:
            shared[pre + "w_in"] = np.ascontiguousarray(g("w_in"))
            shared[pre + "gate_w"] = np.ascontiguousarray(g("gate_w"))
            shared[pre + "w_out"] = np.ascontiguousarray(g("w_out"))
            lv = np.zeros((LRU_BD, LRU_NB, 11), np.float32)
            lv[:, :, 0:4] = g("conv_w").reshape(4, LRU_NB, LRU_BD).transpose(2, 1, 0)
            lv[:, :, 4] = g("conv_b").reshape(LRU_NB, LRU_BD).T
            lv[:, :, 5:9] = g("gate_b").reshape(4, LRU_NB, LRU_BD).transpose(2, 1, 0)
            lv[:, :, 9:11] = g("lam").reshape(2, LRU_NB, LRU_BD).transpose(2, 1, 0)
            shared[pre + "lruvec"] = lv
    idx = np.arange(128)
    same = (idx[:, None] // 64) == (idx[None, :] // 64)
    mus = (same & (idx[:, None] < idx[None, :])).astype(np.float32)
    mui = (same & (idx[:, None] <= idx[None, :])).astype(np.float32)
    cmk = np.zeros((128, 1536), np.float32)
    cmk[:, 0:128] = mus
    cmk[:, 128:256] = mui
    cmk[:, 256:384] = mus
    cmk[:, 384:512] = mui
    cmk[:, 512:640] = mus.T
    cmk[:, 640:768] = np.eye(128, dtype=np.float32)
    cmk[:, 768:896] = same.astype(np.float32)
    cmk[:, 896:1024] = np.eye(128, dtype=np.float32)[::-1]
    cmk[:, 1024:1536] = (np.arange(512) % 64 != 0).astype(np.float32)[None, :]
    shared["cmask"] = cmk
    pp = np.arange(128)
    dd = pp % 64
    ww = dd % 32
    ff = ww % 16
    first = ww < 16
    inv = (10000.0 ** (-np.arange(16, dtype=np.float32) / 16)).astype(np.float32)
    tpos = np.arange(SEQ)
    posr = (tpos // 64).astype(np.float32)
    posc = (tpos % 64).astype(np.float32)
    pos = np.where((dd // 32 == 0)[:, None], posr[None, :], posc[None, :])
    ang = (pos * inv[ff][:, None]).astype(np.float32)
    rcs = np.zeros((128, 2, SEQ), np.float32)
    rcs[:, 0] = np.cos(ang)
    rcs[:, 1] = np.where(first[:, None], -np.sin(ang), np.sin(ang))
    shared["ropeCS"] = rcs
    partner = np.where(first, pp + 16, pp - 16)
    pm = np.zeros((128, 128), np.float32)
    pm[partner, pp] = 1.0
    shared["permm"] = pm
    maps = []
    for b in range(NCORES):
        m = dict(shared)
        xs = np.concatenate([ctx[b], x[b]], axis=0)
        m["xT"] = np.ascontiguousarray(xs.T)
        cc = np.zeros((128, 8, 2), np.float32)
        cc[:, :, 0] = c[b].reshape(8, 128).T
        cc[:, :, 1] = c_ctx.reshape(8, 128).T
        m["cc"] = cc
        maps.append(m)
    return maps


_MODEL_CACHE = {}


def run_model(inputs, layers=(0, 1, 2, 3), cores=NCORES, stop=None):
    key = (tuple(layers), stop)
    if key not in _MODEL_CACHE:
        _MODEL_CACHE[key] = Model(layers, stop=stop)
    m = _MODEL_CACHE[key]
    maps = host_inputs(inputs, layers)[:cores]
    res = run_bass_kernel_spmd(m.p.nc, maps, core_ids=list(range(cores)))
    outs = [np.asarray(r["outT"]).T for r in res.results]
    return np.stack(outs, axis=0)


def kernel(**inputs):
    out = run_model(inputs)
    return np.ascontiguousarray(out.astype(np.float32))
```

```python
import numpy as np
import ml_dtypes
import concourse.bass as bass
import concourse.mybir as mybir
from concourse.bass_utils import run_bass_kernel_spmd

F32 = mybir.dt.float32
BF16 = mybir.dt.bfloat16
AF = mybir.ActivationFunctionType
ALU = mybir.AluOpType
AX = mybir.AxisListType

D = 1024
SEQ = 4096
CTX = 256
T = SEQ + CTX
NCORES = 8
ARENA0 = 16512
ARENA_END = 229000
NDMASEM = 32


class Res:
    __slots__ = ("name", "w", "rs")

    def __init__(self, name=""):
        self.name = name
        self.w = None
        self.rs = {}


class Op:
    __slots__ = ("eng", "fn", "deps", "needed", "sem", "val", "is_dma", "seq")

    def __init__(self, eng, fn, is_dma=False):
        self.eng = eng
        self.fn = fn
        self.deps = []
        self.needed = False
        self.sem = None
        self.val = 0
        self.is_dma = is_dma
        self.seq = 0


class Tile:
    def __init__(self, t, name):
        self.t = t
        self.name = name
        self.res = Res(name)
        self._sub = {}

    def __getitem__(self, k):
        return self.t[k]

    def r(self, key=None):
        if key is None:
            return self.res
        if key not in self._sub:
            self._sub[key] = Res("%s/%s" % (self.name, key))
        return self._sub[key]


class Prog:
    ENGS = ("pe", "act", "dve", "pool", "sp")

    def __init__(self):
        self.nc = bass.Bass("TRN2", target_bir_lowering=False)
        self.ops = {e: [] for e in self.ENGS}
        self.nops = 0
        self.sb_off = ARENA0
        self.ndma = 0
        self.dma_ops = []
        self.uid = 0
        self.pending_dma = []
        self.carry = {}

    def sb(self, name, shape, dtype):
        esz = 2 if dtype == BF16 else 4
        n = 1
        for s in shape[1:]:
            n *= s
        nbytes = (n * esz + 63) // 64 * 64
        off = self.sb_off
        assert off + nbytes <= ARENA_END, ("SBUF overflow", name, off, nbytes)
        self.sb_off += nbytes
        self.uid += 1
        t = self.nc.alloc_sbuf_tensor_at("%s_%d" % (name, self.uid), list(shape), dtype, offset=off)
        return Tile(t, name)

    def sb_mark(self):
        return self.sb_off

    def sb_reset(self, mark):
        if mark < self.sb_off:
            self.barrier()
        self.sb_off = mark

    def dram(self, name, shape, dtype, kind="Internal"):
        t = self.nc.dram_tensor(name, list(shape), dtype, kind=kind)
        return Tile(t.ap(), name)

    def op(self, eng, fn, r=(), w=(), is_dma=False):
        o = Op(eng, fn, is_dma)
        self.nops += 1
        o.seq = self.nops
        deps = {}

        def add(d):
            if d is None or d is o:
                return
            if d.eng == "pe" and eng == "pe":
                return
            deps[id(d)] = d

        for x in r:
            add(x.w)
        for x in w:
            add(x.w)
            for lst in x.rs.values():
                for d in lst:
                    add(d)
        if is_dma:
            i = self.ndma
            self.ndma += 1
            if i >= NDMASEM:
                add(self.dma_ops[i - NDMASEM])
            self.dma_ops.append(o)
            self.pending_dma.append(o)
        best = {}
        out = []
        for d in deps.values():
            if d.is_dma:
                out.append(d)
            else:
                b = best.get(d.eng)
                if b is None or d.seq > b.seq:
                    best[d.eng] = d
        out.extend(best.values())
        if self.carry.get(eng):
            have = set(id(d) for d in out)
            for d in self.carry[eng]:
                if id(d) not in have and d is not o:
                    out.append(d)
            self.carry[eng] = []
        o.deps = out
        for d in out:
            d.needed = True
        for x in r:
            if is_dma:
                x.rs.setdefault("dma", []).append(o)
            else:
                x.rs[eng] = [o]
        for x in w:
            x.w = o
            x.rs = {}
        self.ops[eng].append(o)
        return o

    def barrier(self, final=False):
        D = []
        for e in self.ENGS:
            if e == "sp":
                continue
            if self.ops[e]:
                D.append(self.ops[e][-1])
        D.extend(self.pending_dma)
        self.pending_dma = []
        for d in D:
            d.needed = True
        if final:
            o = Op("sp", lambda eng: eng.nop(), False)
            self.nops += 1
            o.seq = self.nops
            o.deps = list(D)
            self.ops["sp"].append(o)
            return
        for e in self.ENGS:
            self.carry[e] = self.carry.get(e, []) + [d for d in D if not (d.eng == e and not d.is_dma and e == "pe")]

    def dma(self, out, in_, r=(), w=(), eng="sp"):
        return self.op(eng, lambda e: e.dma_start(out=out, in_=in_), r, w, is_dma=True)

    def mm(self, out, lhsT, rhs, start=True, stop=True, r=(), w=(), **kw):
        return self.op("pe", lambda e: e.matmul(out, lhsT, rhs, start=start, stop=stop, **kw), r, w)

    def tr(self, out, in_, ident, r=(), w=()):
        return self.op("pe", lambda e: e.transpose(out, in_, ident), r, w)

    def act(self, out, in_, func, bias=0.0, scale=1.0, r=(), w=(), eng="act"):
        return self.op(eng, lambda e: e.activation(out=out, in_=in_, func=func, bias=bias, scale=scale), r, w)

    def ts(self, eng, out, in0, s1, op0, s2=None, op1=None, r=(), w=()):
        if op1 is None:
            return self.op(eng, lambda e: e.tensor_scalar(out=out, in0=in0, scalar1=s1, scalar2=None, op0=op0), r, w)
        return self.op(eng, lambda e: e.tensor_scalar(out=out, in0=in0, scalar1=s1, scalar2=s2, op0=op0, op1=op1), r, w)

    def tt(self, eng, out, in0, in1, op, r=(), w=()):
        return self.op(eng, lambda e: e.tensor_tensor(out=out, in0=in0, in1=in1, op=op), r, w)

    def stt(self, out, in0, scalar, in1, op0, op1, r=(), w=()):
        return self.op("dve", lambda e: e.scalar_tensor_tensor(out=out, in0=in0, scalar=scalar, in1=in1, op0=op0, op1=op1), r, w)

    def copy(self, eng, out, in_, r=(), w=()):
        if eng == "act":
            return self.op(eng, lambda e: e.copy(out=out, in_=in_), r, w)
        return self.op(eng, lambda e: e.tensor_copy(out=out, in_=in_), r, w)

    def memset(self, eng, ap, val, w=()):
        return self.op(eng, lambda e: e.memset(ap, val), (), w)

    def emit(self):
        nc = self.nc
        from contextlib import ExitStack
        with ExitStack() as st:
            esem = {e: st.enter_context(nc.semaphore("s_" + e)) for e in self.ENGS if e != "sp"}
            dsem = [st.enter_context(nc.semaphore("d_%d" % i)) for i in range(NDMASEM)]
            for e in self.ENGS:
                cnt = 0
                for o in self.ops[e]:
                    if o.is_dma:
                        continue
                    if e == "sp":
                        continue
                    if o.needed:
                        cnt += 1
                        o.sem = esem[e]
                        o.val = cnt
            for i, o in enumerate(self.dma_ops):
                o.sem = dsem[i % NDMASEM]
                o.val = 16 * (i // NDMASEM + 1)
            spsem = st.enter_context(nc.semaphore("s_sp"))
            cnt = 0
            for o in self.ops["sp"]:
                if not o.is_dma and o.needed:
                    cnt += 1
                    o.sem = spsem
                    o.val = cnt
            block = st.enter_context(nc.Block())

            def run(ename):
                def body(eng):
                    waited = {}
                    for o in self.ops[ename]:
                        for d in o.deps:
                            key = id(d.sem)
                            if waited.get(key, 0) >= d.val:
                                continue
                            eng.wait_ge(d.sem, d.val)
                            waited[key] = d.val
                        ins = o.fn(eng)
                        if o.is_dma:
                            ins.then_inc(o.sem, 16)
                        elif o.needed:
                            ins.then_inc(o.sem, 1)
                return body

            block.tensor(run("pe"))
            block.scalar(run("act"))
            block.vector(run("dve"))
            block.gpsimd(run("pool"))
            block.sync(run("sp"))
        return nc


LRU_W = 1408
LRU_NB = 16
LRU_BD = 88
PC0 = 2
PL0 = 261
PTOT = 4358


def tok_blocks():
    out = [(0, 256, True)]
    for i in range(8):
        out.append((256 + 512 * i, 512, False))
    return out


def pad_col(t0):
    return PC0 + t0 if t0 < CTX else PL0 + (t0 - CTX)


class StopBuild(Exception):
    pass


class Model:
    def __init__(self, layers=(0, 1, 2, 3), debug=False, stop=None):
        self.p = Prog()
        self.layers = layers
        self.debug = debug
        self.stop = stop
        try:
            self.build()
        except StopBuild:
            self.p.barrier(final=True)
            self.p.emit()

    def chk(self, name):
        if self.stop == name:
            raise StopBuild()

    def build(self):
        p = self.p
        nc = p.nc
        self.xT_in = p.dram("xT", [D, T], F32, kind="ExternalInput")
        self.cc_in = p.dram("cc", [128, 8, 2], F32, kind="ExternalInput")
        self.outT = p.dram("outT", [D, SEQ], F32, kind="ExternalOutput")
        self.XS = p.dram("XS", [D, T], F32)
        self.inp = {}
        self.cmask_in = p.dram("cmask", [128, 1536], F32, kind="ExternalInput")
        pst = nc.alloc_psum_tensor("ps", [128, 8, 512], F32)
        self.ps = Tile(pst, "ps")
        self.ones_f = p.sb("ones_f", [128, 128], F32)
        p.memset("pool", self.ones_f[:, :], 1.0, w=[self.ones_f.r()])
        self.consts()
        self.cc = p.sb("cc", [128, 8, 2], F32)
        p.dma(self.cc[:, :, :], self.cc_in[:, :, :], w=[self.cc.r()])
        self.cact = p.sb("cact", [128, 8, 2], F32)
        p.act(self.cact[:, :, :], self.cc[:, :, :], AF.Silu, r=[self.cc.r()], w=[self.cact.r()])
        self.mod = p.sb("mod", [128, 24, 2], F32)
        self.g1 = p.sb("g1", [128, 8, 2], F32)
        self.base_mark = p.sb_mark()
        self.chk("setup")
        src = self.xT_in
        for li in range(4):
            if li not in self.layers:
                continue
            p.sb_reset(self.base_mark)
            last = li == max(self.layers)
            dst = self.XS
            if self.debug != 3:
                self.layer_prologue(li)
            if li % 3 == 1:
                self.layer_rglru(li, src, dst, last)
            elif li % 3 == 2:
                self.layer_natten(li, src, dst, last)
            else:
                self.layer_rwkv(li, src, dst, last)
            src = dst
            p.barrier()
        p.barrier(final=True)
        p.emit()

    def din(self, name, shape, dtype=F32):
        t = self.p.dram(name, shape, dtype, kind="ExternalInput")
        self.inp[name] = t
        return t

    def layer_prologue(self, li):
        p = self.p
        pre = "l%d_" % li
        ada_w = self.din(pre + "ada_w", [D, 3 * D])
        vec = self.din(pre + "adavec", [128, 32])
        mark = p.sb_mark()
        av = p.sb("adavec", [128, 32], F32)
        p.dma(av[:, :], vec[:, :], w=[av.r()])
        ps = self.ps
        aw_v = ada_w.t.rearrange("(k p) c -> p k c", p=128)
        wst = [p.sb("adaw%d" % i, [128, 8, 512], F32) for i in range(2)]
        for oc4 in range(6):
            wt = wst[oc4 % 2]
            p.dma(wt[:, :, :], aw_v[:, :, oc4 * 512:(oc4 + 1) * 512], w=[wt.r()])
            for j in range(4):
                oc = oc4 * 4 + j
                for k in range(8):
                    p.mm(ps[:, 0, oc * 2:oc * 2 + 2], wt[:, k, j * 128:(j + 1) * 128], self.cact[:, k, :],
                         start=(k == 0), stop=(k == 7), r=[wt.r(), self.cact.r()], w=[ps.r(0)])
        p.tt("dve", self.mod[:, :, :], ps[:, 0, 0:48].rearrange("p (c two) -> p c two", two=2),
             av[:, 0:24].unsqueeze(2).to_broadcast([128, 24, 2]), ALU.add,
             r=[ps.r(0), av.r()], w=[self.mod.r()])
        p.ts("dve", self.g1[:, :, :], self.mod[:, 8:16, :], 1.0, ALU.add, r=[self.mod.r()], w=[self.g1.r()])
        p.tt("dve", self.g1[:, :, :], self.g1[:, :, :], av[:, 24:32].unsqueeze(2).to_broadcast([128, 8, 2]), ALU.mult,
             r=[self.g1.r(), av.r()], w=[self.g1.r()])
        p.barrier()
        p.sb_reset(mark)
        self.chk("prologue")

    def stage_norm(self, src, HT, pad=None, HTd=None):
        p = self.p
        ps = self.ps
        mark = p.sb_mark()
        xb = [p.sb("nx%d" % i, [128, 8, 512], F32) for i in range(2)]
        sq = p.sb("nsq", [128, 8, 512], F32)
        rs = p.sb("nrs", [128, 512], F32)
        tmp = p.sb("ntmp", [128, 512], F32)
        src_v = src.t.rearrange("(k p) t -> p k t", p=128)
        if HTd is not None:
            hbk = [p.sb("nhb%d" % i, [128, 8, 512], BF16) for i in range(2)]
            htd_v = HTd.t.rearrange("(k p) t -> p k t", p=128)
        for bi, (t0, nb, isc) in enumerate(tok_blocks()):
            if HTd is not None:
                HT = hbk[bi % 2]
            x = xb[bi % 2]
            mc = 1 if isc else 0
            p.dma(x[:, :, 0:nb], src_v[:, :, t0:t0 + nb], r=[src.r(bi)], w=[x.r()])
            self.chk("n1")
            p.act(sq[:, :, 0:nb], x[:, :, 0:nb], AF.Square, r=[x.r()], w=[sq.r()])
            self.chk("n2")
            for k in range(8):
                p.mm(ps[:, 1, 0:nb], self.ones_f[:, :], sq[:, k, 0:nb], start=(k == 0), stop=(k == 7),
                     r=[sq.r(), self.ones_f.r()], w=[ps.r(1)])
            self.chk("n3")
            if self.debug == 1:
                p.act(rs[:, 0:nb], sq[:, 0, 0:nb], AF.Sqrt, bias=self.eps_t[:, 0:1], scale=1.0 / D,
                      r=[ps.r(1), self.eps_t.r()], w=[rs.r()])
            elif self.debug in (2, 3):
                p.copy("dve", rs[:, 0:nb], ps[:, 1, 0:nb], r=[ps.r(1), self.eps_t.r()], w=[rs.r()])
            else:
                p.act(rs[:, 0:nb], ps[:, 1, 0:nb], AF.Sqrt, bias=self.eps_t[:, 0:1], scale=1.0 / D,
                      r=[ps.r(1), self.eps_t.r()], w=[rs.r()])
            self.chk("n4")
            p.op("dve", lambda e, o=rs[:, 0:nb]: e.reciprocal(out=o, in_=o), r=[rs.r()], w=[rs.r()])
            self.chk("n5")
            c0 = t0 if pad is None else pad_col(t0)
            if HTd is not None:
                c0 = 0
            for k in range(8):
                p.tt("dve", tmp[:, 0:nb], x[:, k, 0:nb], rs[:, 0:nb], ALU.mult, r=[x.r(), rs.r()], w=[tmp.r()])
                p.ts("pool", HT[:, k, c0:c0 + nb], tmp[:, 0:nb], self.g1[:, k, mc:mc + 1], ALU.mult,
                     self.mod[:, k, mc:mc + 1], ALU.add, r=[tmp.r(), self.g1.r(), self.mod.r()],
                     w=[HT.r(bi) if HTd is None else HT.r()])
            if HTd is not None:
                p.dma(htd_v[:, :, t0:t0 + nb], HT[:, :, 0:nb], r=[HT.r()], w=[HTd.r(bi)])
            self.chk("n6")
        p.sb_reset(mark)

    def stage_out(self, li, src, dst, last, U_loader, nK, KP, w_out_name, w_rows):
        p = self.p
        ps = self.ps
        pre = "l%d_" % li
        w_out = self.din(pre + w_out_name, [w_rows, D])
        mark = p.sb_mark()
        wst = p.sb("wo_st", [128, D], F32)
        wo = p.sb("wo", [128, nK, D], BF16)
        for kc in range(nK):
            p.dma(wst[0:KP, :], w_out[kc * KP:(kc + 1) * KP, :], w=[wst.r()])
            p.copy("pool", wo[0:KP, kc, :], wst[0:KP, :], r=[wst.r()], w=[wo.r()])
        xb = [p.sb("ox%d" % i, [128, 8, 512], F32) for i in range(2)]
        src_v = src.t.rearrange("(k p) t -> p k t", p=128)
        dst_v = dst.t.rearrange("(k p) t -> p k t", p=128)
        out_v = self.outT.t.rearrange("(k p) t -> p k t", p=128)
        for bi, (t0, nb, isc) in enumerate(tok_blocks()):
            if last and isc:
                continue
            mc = 1 if isc else 0
            U, ures = U_loader(bi, t0, nb)
            x = xb[bi % 2]
            p.dma(x[:, :, 0:nb], src_v[:, :, t0:t0 + nb], r=[src.r(bi)], w=[x.r()])
            for fc in range(8):
                bank = 2 + (fc % 2)
                for kc in range(nK):
                    p.mm(ps[:, bank, 0:nb], wo[0:KP, kc, fc * 128:(fc + 1) * 128], U[0:KP, kc, 0:nb],
                         start=(kc == 0), stop=(kc == nK - 1), r=[wo.r()] + ures, w=[ps.r(bank)])
                p.stt(x[:, fc, 0:nb], ps[:, bank, 0:nb], self.mod[:, 16 + fc, mc:mc + 1], x[:, fc, 0:nb],
                      ALU.mult, ALU.add, r=[ps.r(bank), self.mod.r(), x.r()], w=[x.r()])
            if last:
                p.dma(out_v[:, :, t0 - CTX:t0 - CTX + nb], x[:, :, 0:nb], r=[x.r()], w=[self.outT.r(bi)])
            else:
                p.dma(dst_v[:, :, t0:t0 + nb], x[:, :, 0:nb], r=[x.r()], w=[dst.r(bi)])
        p.sb_reset(mark)

    def consts(self):
        p = self.p
        if hasattr(self, "eps_t"):
            return
        self.eps_t = p.sb("eps", [128, 4], F32)
        p.memset("pool", self.eps_t[:, 0:1], 1e-6, w=[self.eps_t.r()])
        p.memset("pool", self.eps_t[:, 1:2], 1.0, w=[self.eps_t.r()])
        p.memset("pool", self.eps_t[:, 2:3], 0.0, w=[self.eps_t.r()])
        p.memset("pool", self.eps_t[:, 3:4], 64e-5, w=[self.eps_t.r()])

    def layer_rglru(self, li, src, dst, last):
        p = self.p
        ps = self.ps
        pre = "l%d_" % li
        w_in = self.din(pre + "w_in", [D, 2 * LRU_W])
        gate_w = self.din(pre + "gate_w", [2, 2, LRU_NB, LRU_BD, LRU_BD])
        lvec = self.din(pre + "lruvec", [LRU_BD, LRU_NB, 11])
        UT = p.dram(pre + "UT", [LRU_NB, LRU_BD, T], BF16)
        NP = LRU_BD
        HT = p.sb("HT", [128, 8, T], BF16)
        self.stage_norm(src, HT)
        self.chk("norm")
        mark = p.sb_mark()
        lv = p.sb("lv", [NP, LRU_NB, 11], F32)
        p.dma(lv[:, :, :], lvec[:, :, :], w=[lv.r()])
        cd = p.sb("cd", [NP, LRU_NB, 2], F32)
        p.act(cd[:, :, :], lv[:, :, 9:11], AF.Exp, scale=-1.0, r=[lv.r()], w=[cd.r()])
        p.act(cd[:, :, :], cd[:, :, :], AF.Ln, bias=self.eps_t[0:NP, 1:2], r=[cd.r(), self.eps_t.r()], w=[cd.r()])
        p.ts("dve", cd[:, :, :], cd[:, :, :], -8.0, ALU.mult, r=[cd.r()], w=[cd.r()])
        XR = p.sb("XR", [NP, PTOT], F32)
        X = p.sb("X", [NP, PTOT], F32)
        XB = p.sb("XB", [NP, PTOT], BF16)
        G = p.sb("G", [NP, PTOT], BF16)
        A = p.sb("A", [NP, PTOT], F32)
        B = p.sb("B", [NP, PTOT], F32)
        HS0 = p.sb("HS0", [NP, PTOT], F32)
        wst = p.sb("wst", [128, 8, 2 * NP], F32)
        wb = p.sb("wb", [128, 8, 2 * NP], BF16)
        gst = p.sb("gst", [NP, 4, NP], F32)
        gb = p.sb("gb", [NP, 4, NP], BF16)
        p.memset("pool", XR[:, :], 0.0, w=[XR.r()])
        w_v = w_in.t.rearrange("(k p) c -> p k c", p=128)
        blocks = tok_blocks()
        L = PTOT - 3
        segs = [(PC0, CTX), (PL0, SEQ)]
        for n in range(LRU_NB):
            c0 = n * NP
            p.dma(wst[:, :, 0:NP], w_v[:, :, c0:c0 + NP], w=[wst.r()])
            p.dma(wst[:, :, NP:2 * NP], w_v[:, :, LRU_W + c0:LRU_W + c0 + NP], w=[wst.r()])
            p.copy("pool", wb[:, :, :], wst[:, :, :], r=[wst.r()], w=[wb.r()])
            p.dma(gst[:, :, :], gate_w.t[:, :, n].rearrange("d g c e -> c (d g) e"), w=[gst.r()])
            p.copy("pool", gb[:, :, :], gst[:, :, :], r=[gst.r()], w=[gb.r()])
            p.memset("pool", XR[:, 0:PC0], 0.0, w=[XR.r()])
            p.memset("pool", XR[:, PC0 + CTX:PL0], 0.0, w=[XR.r()])
            p.memset("pool", XR[:, PL0 + SEQ:PTOT], 0.0, w=[XR.r()])
            for bi, (t0, nb, isc) in enumerate(blocks):
                pc = pad_col(t0)
                for k in range(8):
                    p.mm(ps[0:NP, 4, 0:nb], wb[:, k, 0:NP], HT[:, k, t0:t0 + nb], start=(k == 0), stop=(k == 7),
                         r=[wb.r(), HT.r(bi)], w=[ps.r(4)])
                p.copy("dve", XR[:, pc:pc + nb], ps[0:NP, 4, 0:nb], r=[ps.r(4)], w=[XR.r()])
                for k in range(8):
                    p.mm(ps[0:NP, 5, 0:nb], wb[:, k, NP:2 * NP], HT[:, k, t0:t0 + nb], start=(k == 0), stop=(k == 7),
                         r=[wb.r(), HT.r(bi)], w=[ps.r(5)])
                p.act(G[:, pc:pc + nb], ps[0:NP, 5, 0:nb], AF.Silu, r=[ps.r(5)], w=[G.r()])
            p.ts("dve", X[:, 2:2 + L], XR[:, 0:L], lv[:, n, 0:1], ALU.mult, lv[:, n, 4:5], ALU.add,
                 r=[XR.r(), lv.r()], w=[X.r()])
            for j in range(1, 4):
                p.stt(X[:, 2:2 + L], XR[:, j:j + L], lv[:, n, j:j + 1], X[:, 2:2 + L], ALU.mult, ALU.add,
                      r=[XR.r(), lv.r(), X.r()], w=[X.r()])
            p.copy("pool", XB[:, 2:2 + L], X[:, 2:2 + L], r=[X.r()], w=[XB.r()])
            HS = [HS0, XR]
            for d in range(2):
                for bi, (t0, nb, isc) in enumerate(blocks):
                    pc = pad_col(t0)
                    p.mm(ps[0:NP, 4, 0:nb], gb[:, d * 2 + 0, :], XB[:, pc:pc + nb], r=[gb.r(), XB.r()], w=[ps.r(4)])
                    p.act(A[:, pc:pc + nb], ps[0:NP, 4, 0:nb], AF.Sigmoid, bias=lv[:, n, 5 + d * 2:6 + d * 2],
                          r=[ps.r(4), lv.r()], w=[A.r()])
                    p.mm(ps[0:NP, 5, 0:nb], gb[:, d * 2 + 1, :], XB[:, pc:pc + nb], r=[gb.r(), XB.r()], w=[ps.r(5)])
                    p.act(B[:, pc:pc + nb], ps[0:NP, 5, 0:nb], AF.Sigmoid, bias=lv[:, n, 6 + d * 2:7 + d * 2],
                          r=[ps.r(5), lv.r()], w=[B.r()])
                H = HS[d]
                p.act(A[:, 2:2 + L], A[:, 2:2 + L], AF.Exp, scale=cd[:, n, d:d + 1], r=[A.r(), cd.r()], w=[A.r()])
                p.tt("dve", H[:, 2:2 + L], A[:, 2:2 + L], A[:, 2:2 + L], ALU.mult, r=[A.r()], w=[H.r()])
                p.act(H[:, 2:2 + L], H[:, 2:2 + L], AF.Sqrt, bias=self.eps_t[0:NP, 1:2], scale=-1.0,
                      r=[H.r(), self.eps_t.r()], w=[H.r()])
                p.tt("pool", B[:, 2:2 + L], B[:, 2:2 + L], X[:, 2:2 + L], ALU.mult, r=[B.r(), X.r()], w=[B.r()])
                p.tt("dve", B[:, 2:2 + L], B[:, 2:2 + L], H[:, 2:2 + L], ALU.mult, r=[B.r(), H.r()], w=[B.r()])
                for si, (s0, sl) in enumerate(segs):
                    if d == 0:
                        o_ap, a_ap, b_ap = H[:, s0:s0 + sl], A[:, s0:s0 + sl], B[:, s0:s0 + sl]
                        init = 0.0 if si == 0 else H[:, PC0 + CTX - 1:PC0 + CTX]
                    else:
                        o_ap, a_ap, b_ap = (H[:, s0:s0 + sl][:, ::-1], A[:, s0:s0 + sl][:, ::-1],
                                            B[:, s0:s0 + sl][:, ::-1])
                        init = 0.0 if si == 0 else H[:, PC0:PC0 + 1]
                    p.op("dve", lambda e, o=o_ap, a=a_ap, b=b_ap, i=init: e.tensor_tensor_scan(
                        out=o, data0=a, data1=b, initial=i, op0=ALU.mult, op1=ALU.add),
                        r=[A.r(), B.r(), H.r()], w=[H.r()])
            p.tt("pool", HS0[:, 2:2 + L], HS0[:, 2:2 + L], XR[:, 2:2 + L], ALU.add, r=[HS0.r(), XR.r()], w=[HS0.r()])
            p.tt("dve", XB[:, 2:2 + L], HS0[:, 2:2 + L], G[:, 2:2 + L], ALU.mult, r=[HS0.r(), G.r()], w=[XB.r()])
            p.dma(UT.t[n, :, 0:CTX], XB[:, PC0:PC0 + CTX], r=[XB.r()], w=[UT.r(n)])
            p.dma(UT.t[n, :, CTX:T], XB[:, PL0:PL0 + SEQ], r=[XB.r()], w=[UT.r(n)])
        p.barrier()
        p.sb_reset(mark)
        p.sb_reset(self.base_mark)
        ub = [p.sb("ub%d" % i, [NP, LRU_NB, 512], BF16) for i in range(2)]
        ut_v = UT.t.rearrange("n c t -> c n t")

        def loader(bi, t0, nb):
            u = ub[bi % 2]
            p.dma(u[:, :, 0:nb], ut_v[:, :, t0:t0 + nb], r=[UT.r(n) for n in range(LRU_NB)], w=[u.r()])
            return u, [u.r()]

        self.stage_out(li, src, dst, last, loader, LRU_NB, NP, "w_out", LRU_W)

    def layer_natten(self, li, src, dst, last):
        p = self.p
        ps = self.ps
        pre = "l%d_" % li
        w_in = self.din(pre + "w_in", [D, 4 * D])
        ropeD = self.din("ropeCS", [128, 2, SEQ])
        permD = self.din("permm", [128, 128])
        rpbD = self.din(pre + "rpbT", [128, 8 * 15 * 64])
        qkgD = self.din(pre + "qkg", [128, 2])
        OGT = p.dram(pre + "OGT", [8, 128, T], BF16)
        HT = p.sb("HT", [128, 8, T], BF16)
        self.stage_norm(src, HT)
        rpb = p.sb("rpb", [128, 8, 15, 64], BF16)
        identb = p.sb("identb", [128, 128], BF16)
        bdb = p.sb("bdb", [128, 128], BF16)
        perm = p.sb("perm", [128, 128], BF16)
        qkg = p.sb("qkg", [128, 2], F32)
        rope = p.sb("rope", [128, 2, SEQ], F32)
        m2 = p.sb_mark()
        stg = p.sb("stg", [128, 8 * 15 * 64], F32)
        p.dma(stg[:, :], rpbD[:, :], w=[stg.r()])
        p.copy("pool", rpb[:, :, :, :], stg[:, :].rearrange("p (a b c) -> p a b c", a=8, b=15), r=[stg.r()], w=[rpb.r()])
        p.dma(stg[:, 0:1536], self.cmask_in[:, :], w=[stg.r()])
        p.copy("pool", identb[:, :], stg[:, 640:768], r=[stg.r()], w=[identb.r()])
        p.copy("pool", bdb[:, :], stg[:, 768:896], r=[stg.r()], w=[bdb.r()])
        p.dma(stg[:, 0:128], permD[:, :], w=[stg.r()])
        p.copy("pool", perm[:, :], stg[:, 0:128], r=[stg.r()], w=[perm.r()])
        p.dma(qkg[:, :], qkgD[:, :], w=[qkg.r()])
        p.ts("dve", qkg[:, 0:1], qkg[:, 0:1], 0.125, ALU.mult, r=[qkg.r()], w=[qkg.r()])
        p.dma(rope[:, :, :], ropeD[:, :, :], w=[rope.r()])
        p.sb_reset(m2)
        wst = p.sb("wst", [128, 8, 128], F32)
        wb = [p.sb("wb%d" % j, [128, 8, 128], BF16) for j in range(4)]
        QR = p.sb("QR", [128, SEQ], BF16)
        QP = p.sb("QP", [128, SEQ], BF16)
        KR = p.sb("KR", [128, SEQ], BF16)
        KN = p.sb("KN", [128, 512], BF16)
        QC = p.sb("QC", [128, CTX], BF16)
        KC = p.sb("KC", [128, CTX], BF16)
        Gp = p.sb("Gp", [128, T], BF16)
        V2 = p.sb("V2", [128, T // 64, 64], BF16)
        OGp = p.sb("OGp", [128, T], BF16)
        F = [p.sb("nF%d" % j, [128, 512], F32) for j in range(4)]
        sqb = p.sb("sqb", [128, 512], BF16)
        PT = p.sb("PT", [128, 768], BF16)
        rec = p.sb("rec", [128, 128], F32)
        of = p.sb("of", [128, 128], F32)
        w_v = w_in.t.rearrange("(k p) c -> p k c", p=128)
        blocks = tok_blocks()
        GW = 64
        for oc in range(8):
            for j in range(4):
                p.dma(wst[:, :, :], w_v[:, :, j * D + oc * 128:j * D + (oc + 1) * 128], w=[wst.r()])
                p.copy("pool", wb[j][:, :, :], wst[:, :, :], r=[wst.r()], w=[wb[j].r()])
            for bi, (t0, nb, isc) in enumerate(blocks):
                N = slice(0, nb)
                c0 = t0 - CTX
                for which in range(2):
                    qf, sd, t1, t2 = F
                    for k in range(8):
                        p.mm(ps[:, 0, N], wb[which][:, k, :], HT[:, k, t0:t0 + nb], start=(k == 0), stop=(k == 7),
                             r=[wb[which].r(), HT.r(bi)], w=[ps.r(0)])
                    p.act(sqb[:, N], ps[:, 0, N], AF.Square, r=[ps.r(0)], w=[sqb.r()])
                    p.copy("act", qf[:, N], ps[:, 0, N], r=[ps.r(0)], w=[qf.r()])
                    p.mm(ps[:, 1, N], bdb[:, :], sqb[:, N], r=[bdb.r(), sqb.r()], w=[ps.r(1)])
                    p.act(sd[:, N], ps[:, 1, N], AF.Sqrt, bias=self.eps_t[:, 0:1], scale=1.0 / 64, r=[ps.r(1), self.eps_t.r()], w=[sd.r()])
                    p.op("dve", lambda e, o=sd[:, N]: e.reciprocal(out=o, in_=o), r=[sd.r()], w=[sd.r()])
                    p.tt("dve", qf[:, N], qf[:, N], sd[:, N], ALU.mult, r=[qf.r(), sd.r()], w=[qf.r()])
                    if isc:
                        dstt = QC if which == 0 else KC
                        p.ts("pool", dstt[:, N], qf[:, N], qkg[:, which:which + 1], ALU.mult, r=[qf.r(), qkg.r()], w=[dstt.r()])
                    else:
                        nbt, nsl = (QP, slice(c0, c0 + nb)) if which == 0 else (KN, N)
                        p.ts("pool", nbt[:, nsl], qf[:, N], qkg[:, which:which + 1], ALU.mult, r=[qf.r(), qkg.r()], w=[nbt.r()])
                        p.mm(ps[:, 1, N], perm[:, :], nbt[:, nsl], r=[perm.r(), nbt.r()], w=[ps.r(1)])
                        p.tt("pool", t1[:, N], nbt[:, nsl], rope[:, 0, c0:c0 + nb], ALU.mult, r=[nbt.r(), rope.r()], w=[t1.r()])
                        p.tt("dve", t2[:, N], ps[:, 1, N], rope[:, 1, c0:c0 + nb], ALU.mult, r=[ps.r(1), rope.r()], w=[t2.r()])
                        rt = QR if which == 0 else KR
                        p.tt("dve", rt[:, c0:c0 + nb], t1[:, N], t2[:, N], ALU.add, r=[t1.r(), t2.r()], w=[rt.r()])
                for k in range(8):
                    p.mm(ps[:, 2, N], wb[3][:, k, :], HT[:, k, t0:t0 + nb], start=(k == 0), stop=(k == 7),
                         r=[wb[3].r(), HT.r(bi)], w=[ps.r(2)])
                p.act(Gp[:, t0:t0 + nb], ps[:, 2, N], AF.Silu, r=[ps.r(2)], w=[Gp.r()])
                nrow = nb // 64
                for i0 in range(0, nrow, 8):
                    ng = min(8, nrow - i0)
                    for i in range(ng):
                        tk = t0 + (i0 + i) * 64
                        for half in range(2):
                            hp = slice(half * 64, half * 64 + 64)
                            for k in range(8):
                                p.mm(ps[hp, 3, i * 64:(i + 1) * 64], HT[:, k, tk:tk + 64], wb[2][:, k, half * 64:(half + 1) * 64],
                                     start=(k == 0), stop=(k == 7), r=[wb[2].r(), HT.r(bi)], w=[ps.r(3)])
                    r0 = t0 // 64 + i0
                    p.copy("act", V2[:, r0:r0 + ng, :], ps[:, 3, 0:ng * 64].rearrange("p (a b) -> p a b", b=64), r=[ps.r(3)], w=[V2.r()])
            for r in range(GW):
                start = min(max(r - 4, 0), GW - 8)
                qs = slice(r * 64, (r + 1) * 64)
                for half in range(2):
                    hp = slice(half * 64, half * 64 + 64)
                    hc = slice(half * 64, half * 64 + 64)
                    b0 = half * 4
                    for i in range(8):
                        kr = start + i
                        dr = kr - r + 7
                        p.mm(ps[hp, b0, i * 64:(i + 1) * 64], KR[hp, kr * 64:(kr + 1) * 64], QR[hp, qs], start=True, stop=False,
                             r=[KR.r(), QR.r()], w=[ps.r(b0)])
                        p.mm(ps[hp, b0, i * 64:(i + 1) * 64], rpb[hp, oc, dr, :], identb[hp, hc], start=False, stop=True,
                             r=[rpb.r(), identb.r()], w=[ps.r(b0)])
                    for j in range(4):
                        p.mm(ps[hp, b0 + 1, j * 64:(j + 1) * 64], KC[hp, j * 64:(j + 1) * 64], QP[hp, qs],
                             r=[KC.r(), QP.r()], w=[ps.r(b0 + 1)])
                    p.act(PT[hp, 0:512], ps[hp, b0, :], AF.Exp, r=[ps.r(b0)], w=[PT.r(half)])
                    p.act(PT[hp, 512:768], ps[hp, b0 + 1, 0:256], AF.Exp, r=[ps.r(b0 + 1)], w=[PT.r(half)])
                    for c in range(12):
                        vrow = (4 + start + c) if c < 8 else (c - 8)
                        p.mm(ps[hp, b0 + 2, 0:64], V2[hp, vrow, :], PT[hp, c * 64:(c + 1) * 64], start=(c == 0), stop=(c == 11),
                             r=[V2.r(), PT.r(half)], w=[ps.r(b0 + 2)])
                    for c in range(12):
                        p.mm(ps[hp, b0 + 3, 0:64], bdb[hp, hc], PT[hp, c * 64:(c + 1) * 64], start=(c == 0), stop=(c == 11),
                             r=[bdb.r(), PT.r(half)], w=[ps.r(b0 + 3)])
                    p.op("dve", lambda e, o=rec[hp, 0:64], a=ps[hp, b0 + 3, 0:64]: e.reciprocal(out=o, in_=a), r=[ps.r(b0 + 3)], w=[rec.r(half)])
                    p.tt("dve", of[hp, 0:64], ps[hp, b0 + 2, 0:64], rec[hp, 0:64], ALU.mult, r=[ps.r(b0 + 2), rec.r(half)], w=[of.r(half)])
                    tq = CTX + r * 64
                    p.tt("pool", OGp[hp, tq:tq + 64], of[hp, 0:64], Gp[hp, tq:tq + 64], ALU.mult, r=[of.r(half), Gp.r()], w=[OGp.r()])
            if not last:
                for qh in range(2):
                    qs = slice(qh * 128, (qh + 1) * 128)
                    for half in range(2):
                        hp = slice(half * 64, half * 64 + 64)
                        hc = slice(half * 64, half * 64 + 64)
                        b0 = half * 4
                        for j in range(4):
                            p.mm(ps[hp, b0, j * 128:(j + 1) * 128], KC[hp, j * 64:(j + 1) * 64], QC[hp, qs], r=[KC.r(), QC.r()], w=[ps.r(b0)])
                        p.act(PT[hp, 0:512], ps[hp, b0, :], AF.Exp, r=[ps.r(b0)], w=[PT.r(half)])
                        for j in range(4):
                            p.mm(ps[hp, b0 + 2, 0:128], V2[hp, j, :], PT[hp, j * 128:(j + 1) * 128], start=(j == 0), stop=(j == 3),
                                 r=[V2.r(), PT.r(half)], w=[ps.r(b0 + 2)])
                        for j in range(4):
                            p.mm(ps[hp, b0 + 3, 0:128], bdb[hp, hc], PT[hp, j * 128:(j + 1) * 128], start=(j == 0), stop=(j == 3),
                                 r=[bdb.r(), PT.r(half)], w=[ps.r(b0 + 3)])
                        p.op("dve", lambda e, o=rec[hp, :], a=ps[hp, b0 + 3, 0:128]: e.reciprocal(out=o, in_=a), r=[ps.r(b0 + 3)], w=[rec.r(half)])
                        p.tt("dve", of[hp, :], ps[hp, b0 + 2, 0:128], rec[hp, :], ALU.mult, r=[ps.r(b0 + 2), rec.r(half)], w=[of.r(half)])
                        p.tt("pool", OGp[hp, qs], of[hp, :], Gp[hp, qs], ALU.mult, r=[of.r(half), Gp.r()], w=[OGp.r()])
            lo = 0 if not last else CTX
            p.dma(OGT.t[oc, :, lo:T], OGp[:, lo:T], r=[OGp.r()], w=[OGT.r(oc)])
        p.sb_reset(self.base_mark)
        ub = [p.sb("ub%d" % i, [128, 8, 512], BF16) for i in range(2)]
        og_v = OGT.t.rearrange("n c t -> c n t")

        def loader(bi, t0, nb):
            u = ub[bi % 2]
            p.dma(u[:, :, 0:nb], og_v[:, :, t0:t0 + nb], r=[OGT.r(n) for n in range(8)], w=[u.r()])
            return u, [u.r()]

        self.stage_out(li, src, dst, last, loader, 8, 128, "w_out", D)

    def layer_rwkv(self, li, src, dst, last):
        p = self.p
        ps = self.ps
        pre = "l%d_" % li
        w_in = self.din(pre + "w_in", [4, D, D])
        ldown = self.din(pre + "lora_down", [2, 2, D, 64])
        lup = self.din(pre + "lora_up", [2, 2, 64, D])
        muD = self.din(pre + "muv", [128, 8, 6])
        rvD = self.din(pre + "rv", [128, 8, 6])
        rkD = self.din(pre + "rksel", [128, 8, 2])
        gnD = self.din(pre + "gnrep", [128, 2, D])
        cmD = self.cmask_in
        HTd = p.dram(pre + "HTd", [D, T], BF16)
        YB = p.dram(pre + "YB", [8, T, 130], F32)
        OGT = p.dram(pre + "OGT", [8, 128, T], BF16)
        self.stage_norm(src, None, HTd=HTd)
        mark0 = p.sb_mark()
        cm = p.sb("cm", [128, 1536], F32)
        p.dma(cm[:, :], cmD[:, :], w=[cm.r()])
        M4 = cm[:, 0:512]
        MUs = cm[:, 0:128]
        M3 = cm[:, 128:512]
        MLs = cm[:, 512:640]
        RST = cm[:, 1024:1536]
        identb = p.sb("identb", [128, 128], BF16)
        bdb = p.sb("bdb", [128, 128], BF16)
        p.copy("pool", identb[:, :], cm[:, 640:768], r=[cm.r()], w=[identb.r()])
        p.copy("pool", bdb[:, :], cm[:, 768:896], r=[cm.r()], w=[bdb.r()])
        mu = p.sb("mu", [128, 8, 6], F32)
        rv = p.sb("rv", [128, 8, 6], F32)
        rkf = p.sb("rkf", [128, 8, 2], F32)
        rkb = p.sb("rkb", [128, 8, 2], BF16)
        p.dma(mu[:, :, :], muD[:, :, :], w=[mu.r()])
        p.dma(rv[:, :, :], rvD[:, :, :], w=[rv.r()])
        p.dma(rkf[:, :, :], rkD[:, :, :], w=[rkf.r()])
        p.copy("pool", rkb[:, :, :], rkf[:, :, :], r=[rkf.r()], w=[rkb.r()])
        gn = p.sb("gn", [128, 2, D], F32)
        p.dma(gn[:, :, :], gnD[:, :, :], w=[gn.r()])
        W = [p.sb("W%d" % j, [128, 8, D], BF16) for j in range(4)]
        dnb = p.sb("dnb", [128, 8, 2, 2, 64], BF16)
        upb = p.sb("upb", [64, 2, 2, D], BF16)
        markw = p.sb_mark()
        wst = p.sb("wst", [128, 8, 512], F32)
        for j in range(4):
            wv = w_in.t[j].rearrange("(k p) c -> p k c", p=128)
            for hf in range(2):
                p.dma(wst[:, :, :], wv[:, :, hf * 512:(hf + 1) * 512], w=[wst.r()])
                p.copy("pool" if hf else "dve", W[j][:, :, hf * 512:(hf + 1) * 512], wst[:, :, :], r=[wst.r()], w=[W[j].r()])
        for d in range(2):
            for q in range(2):
                p.dma(wst[:, :, 0:64], ldown.t[d, q].rearrange("(k p) c -> p k c", p=128), w=[wst.r()])
                p.copy("pool", dnb[:, :, d, q, :], wst[:, :, 0:64], r=[wst.r()], w=[dnb.r()])
                p.dma(wst[0:64, 0:2, :], lup.t[d, q].rearrange("c (a b) -> c a b", a=2), w=[wst.r()])
                p.copy("pool", upb[:, d, q, :].rearrange("c (a b) -> c a b", a=2), wst[0:64, 0:2, :], r=[wst.r()], w=[upb.r()])
        p.sb_reset(markw)
        NBM = 512
        NTM = 4
        hb = p.sb("hb", [128, 8, NBM + 2], BF16)
        xx = p.sb("xx", [128, 8, NBM], BF16)
        xr_t = p.sb("xr", [128, 8, NBM], BF16)
        xk_t = p.sb("xk", [128, 8, NBM], BF16)
        xt_t = p.sb("xt", [128, 8, NBM], BF16)
        Vt = p.sb("Vt", [128, NTM, D], BF16)
        Gt = p.sb("Gt", [128, NTM, D], BF16)
        dwa = p.sb("dwa", [64, 2, NBM], BF16)
        F = [p.sb("F%d" % j, [128, NBM], F32) for j in range(9)]
        sqb = p.sb("sqb", [128, NBM], BF16)
        zb = p.sb("zb", [128, NBM], BF16)
        AR = p.sb("AR", [128, NTM, 256], BF16)
        BT = p.sb("BT", [128, NBM], BF16)
        KT = p.sb("KT", [128, NBM], BF16)
        TOK = p.sb("TOK", [128, NTM, 384], BF16)
        XX = p.sb("XX", [128, 384], F32)
        TOKAf = p.sb("TOKAf", [128, NTM, 128], F32)
        identf = cm[:, 640:768]
        GM3 = p.sb("GM3", [128, 384], BF16)
        Zs = p.sb("Zs", [128, 64], F32)
        W12 = p.sb("W12", [128, 128], BF16)
        GT = p.sb("GT", [128, 128], BF16)
        QT = p.sb("QT", [128, 128], BF16)
        Hs = p.sb("Hs", [128, 8, 64], BF16)
        Htmp = p.sb("Htmp", [128, 64], F32)
        Yoc = p.sb("Yoc", [128, NTM, 130], F32)
        YBt = p.sb("YBt", [128, NTM, 130], F32)
        st1 = p.sb("st1", [128, NTM, 2], F32)
        st2 = p.sb("st2", [128, NTM, 2], F32)
        res = p.sb("res", [128, NTM, 128], BF16)
        ogb = p.sb("ogb", [128, NBM], BF16)
        htd_v = HTd.t.rearrange("(k p) t -> p k t", p=128)
        XX3 = XX[:, :].rearrange("p (a b) -> p a b", b=128)
        XXb = p.sb("XXb", [128, 384], F32)
        GM3b = p.sb("GM3b", [128, 384], BF16)
        XXh = [XX, XXb]
        GM3h = [GM3, GM3b]
        XX3h = [XX3, XXb[:, :].rearrange("p (a b) -> p a b", b=128)]

        blocks = tok_blocks()
        for d in (1, 0):
            passB = d == 0
            p.memset("pool", Hs[:, :, :], 0.0, w=[Hs.r()])
            order = [blocks[0]] + (blocks[1:] if d == 0 else blocks[:0:-1])
            for (t0, nb, isc) in order:
                bi = blocks.index((t0, nb, isc))
                nt = nb // 128
                seg0, seg1 = (0, CTX) if isc else (CTX, T)

                def sv(ap2):
                    return ap2 if d == 0 else ap2[:, ::-1]

                lo = t0 - 1 if t0 > seg0 else t0
                hi = t0 + nb + 1 if t0 + nb < seg1 else t0 + nb
                p.dma(hb[:, :, 1 - (t0 - lo):1 + nb + (hi - t0 - nb)], htd_v[:, :, lo:hi], r=[HTd.r(b) for b in range(9)], w=[hb.r()])
                if lo == t0:
                    p.memset("pool", hb[:, :, 0:1], 0.0, w=[hb.r()])
                if hi == t0 + nb:
                    p.memset("pool", hb[:, :, nb + 1:nb + 2], 0.0, w=[hb.r()])
                p.tt("pool", xx[:, :, 0:nb], hb[:, :, 0:nb], hb[:, :, 2:nb + 2], ALU.add, r=[hb.r()], w=[xx.r()])
                p.stt(xx[:, :, 0:nb], xx[:, :, 0:nb], 0.5, hb[:, :, 1:nb + 1], ALU.mult, ALU.subtract, r=[xx.r(), hb.r()], w=[xx.r()])
                def mkx(dst_t, j):
                    for k in range(8):
                        p.stt(sv(dst_t[:, k, 0:nb]), xx[:, k, 0:nb], mu[:, k, j:j + 1], hb[:, k, 1:nb + 1], ALU.mult, ALU.add,
                              r=[xx.r(), mu.r(), hb.r()], w=[dst_t.r()])

                mkx(xr_t, 0)
                mkx(xk_t, 2)
                mkx(xt_t, 3)
                for i in range(nt):
                    for hf in range(2):
                        bank = hf
                        for k in range(8):
                            p.mm(ps[:, bank, :], xt_t[:, k, i * 128:(i + 1) * 128], W[2][:, k, hf * 512:(hf + 1) * 512],
                                 start=(k == 0), stop=(k == 7), r=[xt_t.r(), W[2].r()], w=[ps.r(bank)])
                        p.copy("act", Vt[:, i, hf * 512:(hf + 1) * 512], ps[:, bank, :], r=[ps.r(bank)], w=[Vt.r()])
                if passB:
                    mkx(xt_t, 5)
                    for i in range(nt):
                        for hf in range(2):
                            bank = hf
                            for k in range(8):
                                p.mm(ps[:, bank, :], xt_t[:, k, i * 128:(i + 1) * 128], W[3][:, k, hf * 512:(hf + 1) * 512],
                                     start=(k == 0), stop=(k == 7), r=[xt_t.r(), W[3].r()], w=[ps.r(bank)])
                            p.act(Gt[:, i, hf * 512:(hf + 1) * 512], ps[:, bank, :], AF.Silu, r=[ps.r(bank)], w=[Gt.r()])
                for q, jx in ((0, 1), (1, 4)):
                    mkx(xt_t, jx)
                    for k in range(8):
                        p.mm(ps[0:64, 0, 0:nb], dnb[:, k, d, q, :], xt_t[:, k, 0:nb], start=(k == 0), stop=(k == 7),
                             r=[dnb.r(), xt_t.r()], w=[ps.r(0)])
                    p.act(dwa[:, q, 0:nb], ps[0:64, 0, 0:nb], AF.Tanh if q == 0 else AF.Copy, r=[ps.r(0)], w=[dwa.r()])
                self.chk("r1")
                for oc in range(8):
                    cs = slice(oc * 128, (oc + 1) * 128)
                    rf, kf, sg, af, cum, epos, eneg, eprev, kk = F
                    N = slice(0, nb)
                    for k in range(8):
                        p.mm(ps[:, 0, N], W[0][:, k, cs], xr_t[:, k, N], start=(k == 0), stop=(k == 7), r=[W[0].r(), xr_t.r()], w=[ps.r(0)])
                    p.copy("act", rf[:, N], ps[:, 0, N], r=[ps.r(0)], w=[rf.r()])
                    for k in range(8):
                        p.mm(ps[:, 1, N], W[1][:, k, cs], xk_t[:, k, N], start=(k == 0), stop=(k == 7), r=[W[1].r(), xk_t.r()], w=[ps.r(1)])
                    p.copy("act", kf[:, N], ps[:, 1, N], r=[ps.r(1)], w=[kf.r()])
                    p.mm(ps[:, 0, N], upb[:, d, 0, cs], dwa[:, 0, N], r=[upb.r(), dwa.r()], w=[ps.r(0)])
                    p.act(sg[:, N], ps[:, 0, N], AF.Sigmoid, bias=rv[:, oc, 2 * d:2 * d + 1], r=[ps.r(0), rv.r()], w=[sg.r()])
                    p.mm(ps[:, 1, N], upb[:, d, 1, cs], dwa[:, 1, N], r=[upb.r(), dwa.r()], w=[ps.r(1)])
                    p.act(af[:, N], ps[:, 1, N], AF.Sigmoid, bias=rv[:, oc, 2 * d + 1:2 * d + 2], r=[ps.r(1), rv.r()], w=[af.r()])
                    p.ts("pool", sg[:, N], sg[:, N], -0.6065306597126334, ALU.mult, r=[sg.r()], w=[sg.r()])
                    p.op("dve", lambda e, o=cum[:, N], a=RST[:, N], b=sg[:, N]: e.tensor_tensor_scan(
                        out=o, data0=a, data1=b, initial=0.0, op0=ALU.mult, op1=ALU.add), r=[cm.r(), sg.r()], w=[cum.r()])
                    p.act(epos[:, N], cum[:, N], AF.Exp, r=[cum.r()], w=[epos.r()])
                    p.act(eneg[:, N], cum[:, N], AF.Exp, scale=-1.0, r=[cum.r()], w=[eneg.r()])
                    p.tt("pool", cum[:, N], cum[:, N], sg[:, N], ALU.subtract, r=[cum.r(), sg.r()], w=[cum.r()])
                    p.act(eprev[:, N], cum[:, N], AF.Exp, r=[cum.r()], w=[eprev.r()])
                    p.ts("dve", kk[:, N], kf[:, N], rv[:, oc, 4:5], ALU.mult, r=[kf.r(), rv.r()], w=[kk.r()])
                    p.act(sqb[:, N], kk[:, N], AF.Square, r=[kk.r()], w=[sqb.r()])
                    p.mm(ps[:, 0, N], bdb[:, :], sqb[:, N], r=[bdb.r(), sqb.r()], w=[ps.r(0)])
                    p.act(cum[:, N], ps[:, 0, N], AF.Sqrt, r=[ps.r(0)], w=[cum.r()])
                    p.ts("dve", cum[:, N], cum[:, N], 1e-12, ALU.max, r=[cum.r()], w=[cum.r()])
                    p.op("dve", lambda e, o=cum[:, N]: e.reciprocal(out=o, in_=o), r=[cum.r()], w=[cum.r()])
                    p.tt("dve", kk[:, N], kk[:, N], cum[:, N], ALU.mult, r=[kk.r(), cum.r()], w=[kk.r()])
                    r3 = lambda ap2: ap2.rearrange("p (i t) -> p i t", t=128)
                    p.stt(AR[:, 0:nt, 0:128], r3(kk[:, N]), -1.0, r3(eprev[:, N]), ALU.mult, ALU.mult,
                          r=[kk.r(), eprev.r()], w=[AR.r()])
                    p.tt("pool", AR[:, 0:nt, 128:256], r3(rf[:, N]), r3(epos[:, N]), ALU.mult, r=[rf.r(), epos.r()], w=[AR.r()])
                    p.tt("pool", eprev[:, N], kk[:, N], af[:, N], ALU.mult, r=[kk.r(), af.r(), AR.r()], w=[eprev.r()])
                    p.tt("dve", BT[:, N], eprev[:, N], eneg[:, N], ALU.mult, r=[eprev.r(), eneg.r()], w=[BT.r()])
                    p.ts("pool", af[:, N], af[:, N], -1.0, ALU.add, rv[:, oc, 5:6], ALU.mult, r=[af.r(), rv.r(), eprev.r()], w=[af.r()])
                    p.stt(kf[:, N], af[:, N], 1.0, kf[:, N], ALU.add, ALU.mult, r=[af.r(), kf.r()], w=[kf.r()])
                    p.tt("pool", KT[:, N], kf[:, N], eneg[:, N], ALU.mult, r=[kf.r(), eneg.r()], w=[KT.r()])
                    p.tt("dve", zb[:, N], rf[:, N], kf[:, N], ALU.mult, r=[rf.r(), kf.r()], w=[zb.r()])
                    self.chk("r2")
                    for i in range(nt):
                        ts_ = slice(i * 128, (i + 1) * 128)
                        p.mm(ps[:, 1, i * 2:i * 2 + 2], zb[:, ts_], rkb[:, oc, :], r=[zb.r(), rkb.r()], w=[ps.r(1)])
                    p.copy("act", Yoc[:, 0:nt, 128:130], ps[:, 1, 0:2 * nt].rearrange("p (i c) -> p i c", c=2), r=[ps.r(1)], w=[Yoc.r("b")])
                    for i in range(nt):
                        ts_ = slice(i * 128, (i + 1) * 128)
                        p.mm(ps[:, 0, 0:128], AR[:, i, 0:128], identb[:, :], r=[AR.r(), identb.r()], w=[ps.r(0)])
                        p.mm(ps[:, 0, 128:256], BT[:, ts_], identb[:, :], r=[BT.r(), identb.r()], w=[ps.r(0)])
                        p.mm(ps[:, 0, 256:384], KT[:, ts_], identb[:, :], r=[KT.r(), identb.r()], w=[ps.r(0)])
                        p.copy("act", TOK[:, i, :], ps[:, 0, 0:384], r=[ps.r(0)], w=[TOK.r()])
                        p.copy("pool", TOKAf[:, i, :], TOK[:, i, 0:128], r=[TOK.r()], w=[TOKAf.r()])
                    self.chk("r3")
                    for i in range(nt):
                        ts_ = slice(i * 128, (i + 1) * 128)
                        for half in range(2):
                            hp = slice(half * 64, half * 64 + 64)
                            XXc, GMc = XXh[half], GM3h[half]
                            gb = 0 if half == 0 else 5
                            p.mm(ps[:, gb, 0:256], BT[hp, ts_], AR[hp, i, :], r=[BT.r(), AR.r()], w=[ps.r(gb)])
                            p.mm(ps[:, gb, 256:512], KT[hp, ts_], AR[hp, i, :], r=[KT.r(), AR.r()], w=[ps.r(gb)])
                            p.tt("dve", XXc[:, 0:128], ps[:, gb, 0:128], MUs, ALU.mult, r=[ps.r(gb), cm.r()], w=[XXc.r()])
                            p.tt("dve", GMc[:, :], ps[:, gb, 128:512], M3, ALU.mult, r=[ps.r(gb), cm.r()], w=[GMc.r()])
                            p.mm(ps[:, 1, 0:128], XXc[:, 0:128], identf, r=[XXc.r(), cm.r()], w=[ps.r(1)])
                            p.copy("act", XXc[:, 256:384], ps[:, 1, 0:128], r=[ps.r(1)], w=[XXc.r()])
                            p.tt("pool", XXc[:, 128:256], XXc[:, 0:128], identf, ALU.add, r=[XXc.r(), cm.r()], w=[XXc.r()])
                        for n in range(6):
                            for half in range(2):
                                XXc = XXh[half]
                                cb = 2 if half == 0 else 4
                                if n == 0:
                                    p.mm(ps[:, cb, 0:128], XXc[:, 256:384], XXc[:, 0:128], r=[XXc.r()], w=[ps.r(cb)])
                                elif n == 5:
                                    p.mm(ps[:, cb, 128:256], XXc[:, 256:384], XXc[:, 128:256], r=[XXc.r()], w=[ps.r(cb)])
                                else:
                                    p.mm(ps[:, cb, 0:256], XXc[:, 256:384], XXc[:, 0:256], r=[XXc.r()], w=[ps.r(cb)])
                                if n < 5:
                                    p.mm(ps[:, cb, 256:384], XXc[:, 0:128], XXc[:, 256:384], r=[XXc.r()], w=[ps.r(cb)])
                                if n > 0:
                                    p.tt("dve", XXc[:, 128:256], XXc[:, 128:256], ps[:, cb, 128:256], ALU.add, r=[XXc.r(), ps.r(cb)], w=[XXc.r()])
                                if n < 5:
                                    p.copy("act", XX3h[half][:, 0:3:2, :], ps[:, cb, 0:384].rearrange("p (a b) -> p a b", b=128)[:, 0:3:2, :],
                                           r=[ps.r(cb)], w=[XXc.r()])
                        for half in range(2):
                            h = 2 * oc + half
                            hp = slice(half * 64, half * 64 + 64)
                            hc = slice(half * 64, half * 64 + 64)
                            Vh = Vt[:, i, h * 64:(h + 1) * 64]
                            XXc, GMc = XXh[half], GM3h[half]
                            TT = XXc[:, 128:256]
                            p.mm(ps[:, 1, 128:192], GMc[:, 128:256], Vh, r=[GMc.r(), Vt.r()], w=[ps.r(1)])
                            p.copy("act", Zs[:, :], ps[:, 1, 128:192], r=[ps.r(1)], w=[Zs.r()])
                            p.mm(ps[:, 1, 192:256], TT, TOKAf[:, i, half * 64:half * 64 + 64], r=[XXc.r(), TOKAf.r()], w=[ps.r(1)])
                            p.mm(ps[:, 1, 256:320], TT, Zs[:, :], r=[XXc.r(), Zs.r()], w=[ps.r(1)])
                            p.copy("dve", W12[:, :], ps[:, 1, 192:320], r=[ps.r(1)], w=[W12.r()])
                            p.mm(ps[hp, 1, 320:448], W12[:, 0:64], GMc[:, 0:128], r=[W12.r(), GMc.r()], w=[ps.r(1)])
                            p.tt("dve", GT[hp, :], ps[hp, 1, 320:448], AR[hp, i, 128:256], ALU.add, r=[ps.r(1), AR.r()], w=[GT.r()])
                            p.mm(ps[hp, 1, 448:512], W12[0:64, 0:64], TOK[0:64, i, 128 + half * 64:192 + half * 64],
                                 r=[W12.r(), TOK.r()], w=[ps.r(1)])
                            p.copy("act", QT[hp, 0:64], ps[hp, 1, 448:512], r=[ps.r(1)], w=[QT.r()])
                            p.mm(ps[hp, 7, 0:64], W12[64:128, 0:64], TOK[64:128, i, 128 + half * 64:192 + half * 64],
                                 r=[W12.r(), TOK.r()], w=[ps.r(7)])
                            p.copy("act", QT[hp, 64:128], ps[hp, 7, 0:64], r=[ps.r(7)], w=[QT.r()])
                            p.mm(ps[:, 3, 0:64], GMc[:, 0:128], W12[:, 64:128], start=True, stop=False, r=[GMc.r(), W12.r()], w=[ps.r(3)])
                            p.mm(ps[:, 3, 0:64], GMc[:, 256:384], Vh, start=False, stop=(half == 1), r=[GMc.r(), Vt.r()], w=[ps.r(3)])
                            for j in range(2):
                                rows = slice(j * 64, j * 64 + 64)
                                if half == 0:
                                    p.mm(ps[rows, 3, 0:64], GT[hp, j * 64:j * 64 + 64], Hs[hp, oc, :], start=False, stop=True,
                                         r=[GT.r(), Hs.r()], w=[ps.r(3)], skip_group_check=True)
                                else:
                                    p.mm(ps[rows, 6, 0:64], GT[hp, j * 64:j * 64 + 64], Hs[hp, oc, :], start=True, stop=True,
                                         r=[GT.r(), Hs.r()], w=[ps.r(6)])
                                cN = i * 128 + j * 64 + 63
                                pC = epos[hp, cN:cN + 1]
                                bh = 4 if half == 0 else 7
                                bj = 4 if j == 0 else 7
                                ch = slice(0, 64) if bh == 4 else slice(64, 128)
                                cj = slice(0, 64) if bj == 4 else slice(64, 128)
                                same = bh == bj
                                p.mm(ps[hp, bh, ch], identb[hp, hc], Hs[hp, oc, :], start=True, stop=False, r=[identb.r(), Hs.r()], w=[ps.r(bh)])
                                p.mm(ps[hp, bh, ch], QT[hp, j * 64:j * 64 + 64], Hs[hp, oc, :], start=False, stop=(not same), r=[QT.r(), Hs.r()], w=[ps.r(bh)])
                                p.mm(ps[hp, bj, cj], TOK[rows, i, 128 + half * 64:192 + half * 64], W12[rows, 64:128], start=(not same), stop=False,
                                     r=[TOK.r(), W12.r()], w=[ps.r(bj)])
                                p.mm(ps[hp, bj, cj], TOK[rows, i, 256 + half * 64:320 + half * 64], Vt[rows, i, h * 64:(h + 1) * 64], start=False, stop=True,
                                     r=[TOK.r(), Vt.r()], w=[ps.r(bj)])
                                if same:
                                    p.act(Hs[hp, oc, :], ps[hp, bh, ch], AF.Copy, scale=pC, r=[ps.r(bh), epos.r()], w=[Hs.r()])
                                else:
                                    p.act(Htmp[hp, :], ps[hp, bh, ch], AF.Copy, scale=pC, r=[ps.r(bh), epos.r()], w=[Htmp.r()])
                                    p.stt(Hs[hp, oc, :], ps[hp, bj, cj], pC, Htmp[hp, :], ALU.mult, ALU.add,
                                          r=[ps.r(bj), epos.r(), Htmp.r()], w=[Hs.r()])
                            p.copy("dve", Yoc[:, i, half * 64:half * 64 + 64], ps[:, 3, 0:64], r=[ps.r(3)], w=[Yoc.r("y")])
                            if half == 1:
                                p.tt("dve", Yoc[:, i, 64:128], Yoc[:, i, 64:128], ps[:, 6, 0:64], ALU.add, r=[Yoc.r("y"), ps.r(6)], w=[Yoc.r("y")])
                            self.chk("r8")
                    if not passB:
                        for i in range(nt):
                            bank = i % 2
                            p.mm(ps[:, bank, 0:130], cm[:, 896:1024], Yoc[:, i, :], r=[cm.r(), Yoc.r("y"), Yoc.r("b")], w=[ps.r(bank)])
                            p.copy("act", YBt[:, nt - 1 - i, :], ps[:, bank, 0:130], r=[ps.r(bank)], w=[YBt.r()])
                        yv = YB.t[oc, t0:t0 + nb, :].rearrange("(i q) c -> q i c", q=128)
                        p.dma(yv, YBt[:, 0:nt, :], r=[YBt.r()], w=[YB.r((oc, bi))])
                        self.chk("r9")
                    else:
                        yv = YB.t[oc, t0:t0 + nb, :].rearrange("(i q) c -> q i c", q=128)
                        p.dma(YBt[:, 0:nt, :], yv, r=[YB.r((oc, bi))], w=[YBt.r()])
                        p.tt("dve", Yoc[:, 0:nt, :], Yoc[:, 0:nt, :], YBt[:, 0:nt, :], ALU.add, r=[Yoc.r("y"), Yoc.r("b"), YBt.r()], w=[Yoc.r("y"), Yoc.r("b")])
                        y4 = Yoc[:, 0:nt, 0:128].rearrange("p i (g c) -> p i g c", c=64)
                        bc = lambda t_: t_[:, 0:nt, :].unsqueeze(3).to_broadcast([128, nt, 2, 64])
                        p.op("dve", lambda e, o=st1[:, 0:nt, :], a=y4: e.tensor_reduce(out=o, in_=a, axis=AX.X, op=ALU.add), r=[Yoc.r("y")], w=[st1.r()])
                        p.ts("dve", st1[:, 0:nt, :], st1[:, 0:nt, :], -1.0 / 64, ALU.mult, r=[st1.r()], w=[st1.r()])
                        p.tt("dve", y4, y4, bc(st1), ALU.add, r=[Yoc.r("y"), st1.r()], w=[Yoc.r("y")])
                        sq4 = YBt[:, 0:nt, 0:128].rearrange("p i (g c) -> p i g c", c=64)
                        p.tt("pool", sq4, y4, y4, ALU.mult, r=[Yoc.r("y")], w=[YBt.r()])
                        p.op("dve", lambda e, o=st2[:, 0:nt, :], a=sq4: e.tensor_reduce(out=o, in_=a, axis=AX.X, op=ALU.add), r=[YBt.r()], w=[st2.r()])
                        p.act(st2[:, 0:nt, :], st2[:, 0:nt, :], AF.Sqrt, bias=self.eps_t[:, 3:4], scale=1.0 / 64, r=[st2.r(), self.eps_t.r()], w=[st2.r()])
                        p.op("dve", lambda e, o=st2[:, 0:nt, :]: e.reciprocal(out=o, in_=o), r=[st2.r()], w=[st2.r()])
                        p.tt("dve", y4, y4, bc(st2), ALU.mult, r=[Yoc.r("y"), st2.r()], w=[Yoc.r("y")])
                        yn = Yoc[:, 0:nt, 0:128]
                        gw = gn[:, 0, cs].unsqueeze(1).to_broadcast([128, nt, 128])
                        gb_ = gn[:, 1, cs].unsqueeze(1).to_broadcast([128, nt, 128])
                        p.tt("pool", yn, yn, gw, ALU.mult, r=[Yoc.r("y"), gn.r()], w=[Yoc.r("y")])
                        p.tt("pool", yn, yn, gb_, ALU.add, r=[Yoc.r("y"), gn.r()], w=[Yoc.r("y")])
                        bs = Yoc[:, 0:nt, 128:130].unsqueeze(3).to_broadcast([128, nt, 2, 64])
                        v4 = Vt[:, 0:nt, cs].rearrange("p i (g c) -> p i g c", c=64)
                        s4 = YBt[:, 0:nt, 0:128].rearrange("p i (g c) -> p i g c", c=64)
                        p.tt("dve", s4, v4, bs, ALU.mult, r=[Vt.r(), Yoc.r("b")], w=[YBt.r()])
                        p.tt("dve", yn, yn, YBt[:, 0:nt, 0:128], ALU.add, r=[Yoc.r("y"), YBt.r()], w=[Yoc.r("y")])
                        p.tt("dve", res[:, 0:nt, :], yn, Gt[:, 0:nt, cs], ALU.mult, r=[Yoc.r("y"), Gt.r()], w=[res.r()])
                        for i in range(nt):
                            p.mm(ps[:, 0, i * 128:(i + 1) * 128], res[:, i, :], identb[:, :], r=[res.r(), identb.r()], w=[ps.r(0)])
                        p.copy("act", ogb[:, 0:nb], ps[:, 0, 0:nb], r=[ps.r(0)], w=[ogb.r()])
                        p.dma(OGT.t[oc, :, t0:t0 + nb], ogb[:, 0:nb], r=[ogb.r()], w=[OGT.r(oc)])
        p.sb_reset(mark0)
        ub = [p.sb("ub%d" % i, [128, 8, 512], BF16) for i in range(2)]
        og_v = OGT.t.rearrange("n c t -> c n t")

        def loader(bi, t0, nb):
            u = ub[bi % 2]
            p.dma(u[:, :, 0:nb], og_v[:, :, t0:t0 + nb], r=[OGT.r(n) for n in range(8)], w=[u.r()])
            return u, [u.r()]

        self.stage_out(li, src, dst, last, loader, 8, 128, "w_out", D)


def _fm(v):
    return np.ascontiguousarray(np.asarray(v, np.float32).reshape(8, 128).T)


def host_inputs(inputs, layers):
    x = np.asarray(inputs["x"], np.float32)
    ctx = np.asarray(inputs["ctx"], np.float32)
    c = np.asarray(inputs["c"], np.float32)
    c_ctx = np.asarray(inputs["c_ctx"], np.float32)
    shared = {}
    for li in layers:
        pre = "l%d_" % li
        g = lambda n: np.asarray(inputs[pre + n], np.float32)
        shared[pre + "ada_w"] = np.ascontiguousarray(g("ada_w"))
        av = np.zeros((128, 32), np.float32)
        av[:, 0:24] = g("ada_b").reshape(24, 128).T
        av[:, 24:32] = g("norm_g").reshape(8, 128).T
        shared[pre + "adavec"] = av
        if li % 3 == 0:
            shared[pre + "w_in"] = np.ascontiguousarray(g("w_in"))
            shared[pre + "lora_down"] = np.ascontiguousarray(g("lora_down"))
            shared[pre + "lora_up"] = np.ascontiguousarray(g("lora_up"))
            shared[pre + "w_out"] = np.ascontiguousarray(g("w_out"))
            shared[pre + "muv"] = np.ascontiguousarray(g("mu").reshape(6, 8, 128).transpose(2, 1, 0))
            b0 = g("lora_b0")
            rvv = np.stack([b0[0, 0], b0[0, 1], b0[1, 0], b0[1, 1], g("k_ka")[0], g("k_ka")[1]], 0)
            shared[pre + "rv"] = np.ascontiguousarray(rvv.reshape(6, 8, 128).transpose(2, 1, 0))
            rk = g("r_k")
            rks = np.zeros((128, 8, 2), np.float32)
            for oc in range(8):
                for j in range(2):
                    rks[j * 64:(j + 1) * 64, oc, j] = rk[2 * oc + j]
            shared[pre + "rksel"] = rks
            shared[pre + "gnrep"] = np.ascontiguousarray(np.broadcast_to(g("gn")[None], (128, 2, D)))
        if li % 3 == 2:
            shared[pre + "w_in"] = np.ascontiguousarray(g("w_in"))
            shared[pre + "w_out"] = np.ascontiguousarray(g("w_out"))
            qg = g("qk_g")
            shared[pre + "qkg"] = np.ascontiguousarray(np.stack([np.tile(qg[0], 2), np.tile(qg[1], 2)], 1))
            rpb = g("rpb")
            cols = np.arange(64)
            cst = np.clip(cols - 8, 0, 48)
            col_ok = (cols[None, :] >= cst[:, None]) & (cols[None, :] < cst[:, None] + 16)
            dc = np.clip(cols[None, :] - cols[:, None] + 15, 0, 30)
            tb = np.zeros((128, 8, 15, 64), np.float32)
            for oc in range(8):
                for hf in range(2):
                    gath = rpb[2 * oc + hf][:, dc]
                    gath = np.where(col_ok[None], gath, np.float32(-30000.0))
                    tb[hf * 64:(hf + 1) * 64, oc] = gath.transpose(1, 0, 2)
            shared[pre + "rpbT"] = np.ascontiguousarray(tb.reshape(128, 8 * 15 * 64))
        if li % 3 == 1:
            shared[pre + "w_in"] = np.ascontiguousarray(g("w_in"))
            shared[pre + "gate_w"] = np.ascontiguousarray(g("gate_w"))
            shared[pre + "w_out"] = np.ascontiguousarray(g("w_out"))
            lv = np.zeros((LRU_BD, LRU_NB, 11), np.float32)
            lv[:, :, 0:4] = g("conv_w").reshape(4, LRU_NB, LRU_BD).transpose(2, 1, 0)
            lv[:, :, 4] = g("conv_b").reshape(LRU_NB, LRU_BD).T
            lv[:, :, 5:9] = g("gate_b").reshape(4, LRU_NB, LRU_BD).transpose(2, 1, 0)
            lv[:, :, 9:11] = g("lam").reshape(2, LRU_NB, LRU_BD).transpose(2, 1, 0)
            shared[pre + "lruvec"] = lv
    idx = np.arange(128)
    same = (idx[:, None] // 64) == (idx[None, :] // 64)
    mus = (same & (idx[:, None] < idx[None, :])).astype(np.float32)
    mui = (same & (idx[:, None] <= idx[None, :])).astype(np.float32)
    cmk = np.zeros((128, 1536), np.float32)
    cmk[:, 0:128] = mus
    cmk[:, 128:256] = mui
    cmk[:, 256:384] = mus
    cmk[:, 384:512] = mui
    cmk[:, 512:640] = mus.T
    cmk[:, 640:768] = np.eye(128, dtype=np.float32)
    cmk[:, 768:896] = same.astype(np.float32)
    cmk[:, 896:1024] = np.eye(128, dtype=np.float32)[::-1]
    cmk[:, 1024:1536] = (np.arange(512) % 64 != 0).astype(np.float32)[None, :]
    shared["cmask"] = cmk
    pp = np.arange(128)
    dd = pp % 64
    ww = dd % 32
    ff = ww % 16
    first = ww < 16
    inv = (10000.0 ** (-np.arange(16, dtype=np.float32) / 16)).astype(np.float32)
    tpos = np.arange(SEQ)
    posr = (tpos // 64).astype(np.float32)
    posc = (tpos % 64).astype(np.float32)
    pos = np.where((dd // 32 == 0)[:, None], posr[None, :], posc[None, :])
    ang = (pos * inv[ff][:, None]).astype(np.float32)
    rcs = np.zeros((128, 2, SEQ), np.float32)
    rcs[:, 0] = np.cos(ang)
    rcs[:, 1] = np.where(first[:, None], -np.sin(ang), np.sin(ang))
    shared["ropeCS"] = rcs
    partner = np.where(first, pp + 16, pp - 16)
    pm = np.zeros((128, 128), np.float32)
    pm[partner, pp] = 1.0
    shared["permm"] = pm
    maps = []
    for b in range(NCORES):
        m = dict(shared)
        xs = np.concatenate([ctx[b], x[b]], axis=0)
        m["xT"] = np.ascontiguousarray(xs.T)
        cc = np.zeros((128, 8, 2), np.float32)
        cc[:, :, 0] = c[b].reshape(8, 128).T
        cc[:, :, 1] = c_ctx.reshape(8, 128).T
        m["cc"] = cc
        maps.append(m)
    return maps


_MODEL_CACHE = {}


def run_model(inputs, layers=(0, 1, 2, 3), cores=NCORES, stop=None):
    key = (tuple(layers), stop)
    if key not in _MODEL_CACHE:
        _MODEL_CACHE[key] = Model(layers, stop=stop)
    m = _MODEL_CACHE[key]
    maps = host_inputs(inputs, layers)[:cores]
    res = run_bass_kernel_spmd(m.p.nc, maps, core_ids=list(range(cores)))
    outs = [np.asarray(r["outT"]).T for r in res.results]
    return np.stack(outs, axis=0)


def kernel(**inputs):
    out = run_model(inputs)
    return np.ascontiguousarray(out.astype(np.float32))
```
